# Optimizing a Trainium2 kernel written in Bass

```python
import math
import jax
import jax.numpy as jnp
from jax import lax
import numpy as np

D_MODEL = 1024
BATCH = 2
SEQ = 8192
DEPTH = 1
DEC_BATCH = 32
DEC_SEQ = 4
PAST_LEN = 16384
PAGE_SIZE = 128

HEAD_DIM = 64
ATTN_WIDTH = D_MODEL // 2
N_HEADS = ATTN_WIDTH // HEAD_DIM
GQA = 4
N_KV = N_HEADS // GQA
N_BRANCH = 3
KV_COLS = 2 * N_KV * HEAD_DIM
SSM_WIDTH = D_MODEL - ATTN_WIDTH
SSM_GROUP = 16
N_SSM_GROUPS = SSM_WIDTH // SSM_GROUP
SSM_STATE = 64
IN_COLS = ATTN_WIDTH + N_BRANCH * KV_COLS + N_BRANCH * N_HEADS + SSM_WIDTH
CMP_LEN = 32
CMP_STRIDE = 16
CMP_HID = 2 * HEAD_DIM
SLC_BLOCK = 64
TOP_K = 16
WINDOW = 512
Q_CHUNK = 128
ROT_DIM = HEAD_DIM // 4
ROPE_THETA = 500000.0
D_FF = 256 * math.ceil(8 * D_MODEL / 3 / 256)
EPS = 1e-6
NEG = -1e30
BIG = 1e4

kernel_name = "hymba_nsa_s5_adaln_decode_step"


def rmsnorm(x, g):
    xf = x.astype(jnp.float32)
    y = xf * lax.rsqrt(jnp.mean(xf * xf, axis=-1, keepdims=True) + EPS)
    return (y * g.astype(jnp.float32)).astype(x.dtype)


def rope(x, pos):
    half = ROT_DIM // 2
    inv = ROPE_THETA ** (-(jnp.arange(half, dtype=jnp.float32) * 2.0 / ROT_DIM))
    ang = pos.astype(jnp.float32)[:, None] * inv[None, :]
    shape = (1, pos.shape[0]) + (1,) * (x.ndim - 3) + (half,)
    cos = jnp.cos(ang).reshape(shape)
    sin = jnp.sin(ang).reshape(shape)
    xf = x.astype(jnp.float32)
    x1 = xf[..., :half]
    x2 = xf[..., half:ROT_DIM]
    out = jnp.concatenate([x1 * cos - x2 * sin, x1 * sin + x2 * cos, xf[..., ROT_DIM:]], axis=-1)
    return out.astype(x.dtype)


def masked_softmax(s, mask):
    p = jax.nn.softmax(jnp.where(mask, s, NEG), axis=-1)
    return jnp.where(mask, p, 0.0)


def block_overlap(nc, ns):
    cs = np.arange(nc) * CMP_STRIDE
    ss = np.arange(ns) * SLC_BLOCK
    ov = np.minimum(cs[:, None] + CMP_LEN, ss[None, :] + SLC_BLOCK) - np.maximum(cs[:, None], ss[None, :])
    return jnp.asarray((np.clip(ov, 0, None) / CMP_LEN).astype(np.float32))


def compress(rows, pe, w1, w2):
    bn, length, g, dh = rows.shape
    r = CMP_LEN // CMP_STRIDE
    n_chunk = length // CMP_STRIDE
    nc = n_chunk - r + 1
    ch = rows[:, :n_chunk * CMP_STRIDE].reshape(bn, n_chunk, CMP_STRIDE, g, dh)
    blk = jnp.concatenate([ch[:, m:m + nc] for m in range(r)], axis=2) + pe[None, None, :, None, :]
    flat = blk.transpose(0, 1, 3, 2, 4).reshape(bn, nc, g, CMP_LEN * dh)
    return jax.nn.gelu(flat @ w1) @ w2


def to_blocks(rows):
    bn, length, g, dh = rows.shape
    ns = -(-length // SLC_BLOCK)
    r = jnp.pad(rows, ((0, 0), (0, ns * SLC_BLOCK - length), (0, 0), (0, 0)))
    return r.reshape(bn, ns, SLC_BLOCK, g, dh).transpose(0, 3, 1, 2, 4)


def nsa_attend(q, q_pos, k_cmp, v_cmp, cmp_end, k_blk, v_blk, kv_win, win_pos, gates):
    bn, nq = q.shape[:2]
    nc = k_cmp.shape[1]
    ns = k_blk.shape[2]
    scale = HEAD_DIM ** -0.5
    qg = q.reshape(bn, nq, N_KV, GQA, HEAD_DIM)
    s_c = jnp.einsum('bqgrd,bcgd->bqgrc', qg, k_cmp).astype(jnp.float32) * scale
    m_c = (cmp_end[None, :] <= q_pos[:, None])[None, :, None, None, :]
    p_c = masked_softmax(s_c, m_c)
    o_c = jnp.einsum('bqgrc,bcgd->bqgrd', p_c.astype(v_cmp.dtype), v_cmp)
    imp = jnp.einsum('bqgrc,cj->bqgj', p_c, block_overlap(nc, ns))
    cur = q_pos // SLC_BLOCK
    j = jnp.arange(ns)
    future = (j[None, :] > cur[:, None])[None, :, None, :]
    forced = ((j[None, :] == 0) | (j[None, :] == cur[:, None]) | (j[None, :] == cur[:, None] - 1))[None, :, None, :]
    imp = jnp.where(future, NEG, jnp.where(forced, BIG, imp))
    _, idx = lax.top_k(imp, min(TOP_K, ns))
    bi = jnp.arange(bn)[:, None, None, None]
    gi = jnp.arange(N_KV)[None, None, :, None]
    k_sel = k_blk[bi, gi, idx]
    v_sel = v_blk[bi, gi, idx]
    n_sel = idx.shape[-1] * SLC_BLOCK
    key_pos = (idx[..., None] * SLC_BLOCK + jnp.arange(SLC_BLOCK)).reshape(bn, nq, N_KV, 1, n_sel)
    s_s = jnp.einsum('bqgrd,bqgkjd->bqgrkj', qg, k_sel).reshape(bn, nq, N_KV, GQA, n_sel)
    p_s = masked_softmax(s_s.astype(jnp.float32) * scale, key_pos <= q_pos[None, :, None, None, None])
    o_s = jnp.einsum('bqgrn,bqgnd->bqgrd', p_s.astype(v_sel.dtype), v_sel.reshape(bn, nq, N_KV, n_sel, HEAD_DIM))
    s_w = jnp.einsum('bqgrd,bwgd->bqgrw', qg, kv_win[:, :, 0]).astype(jnp.float32) * scale
    m_w = (win_pos[None, :] <= q_pos[:, None]) & (win_pos[None, :] >= q_pos[:, None] - WINDOW) & (win_pos[None, :] >= 0)
    p_w = masked_softmax(s_w, m_w[None, :, None, None, :])
    o_w = jnp.einsum('bqgrw,bwgd->bqgrd', p_w.astype(kv_win.dtype), kv_win[:, :, 1])
    g = gates.reshape(bn, nq, N_KV, GQA, N_BRANCH)
    o = g[..., 0:1] * o_c + g[..., 1:2] * o_s + g[..., 2:3] * o_w
    return o.reshape(bn, nq, ATTN_WIDTH)


def complex_affine_combine(e1, e2):
    a1r, a1i, b1r, b1i = e1
    a2r, a2i, b2r, b2i = e2
    return (a2r * a1r - a2i * a1i, a2r * a1i + a2i * a1r,
            a2r * b1r - a2i * b1i + b2r, a2r * b1i + a2i * b1r + b2i)


def s5_scan(u, h0_re, h0_im, prm):
    f32 = jnp.float32
    bn, length, _ = u.shape
    uf = u.astype(f32).reshape(bn, length, N_SSM_GROUPS, SSM_GROUP)
    a_re = prm['ssm_a_re'].astype(f32)
    a_im = prm['ssm_a_im'].astype(f32)
    dt = jnp.exp(prm['ssm_log_dt'].astype(f32))[:, None]
    mag = jnp.exp(a_re * dt)
    lb_re = mag * jnp.cos(a_im * dt)
    lb_im = mag * jnp.sin(a_im * dt)
    den = a_re * a_re + a_im * a_im
    n_re = lb_re - 1.0
    f_re = (n_re * a_re + lb_im * a_im) / den
    f_im = (lb_im * a_re - n_re * a_im) / den
    b_re = prm['ssm_b_re'].astype(f32)
    b_im = prm['ssm_b_im'].astype(f32)
    bb_re = f_re[..., None] * b_re - f_im[..., None] * b_im
    bb_im = f_re[..., None] * b_im + f_im[..., None] * b_re
    bu_re = jnp.einsum('blgi,gpi->blgp', uf, bb_re)
    bu_im = jnp.einsum('blgi,gpi->blgp', uf, bb_im)
    h0r = h0_re.astype(f32)
    h0i = h0_im.astype(f32)
    bu_re = bu_re.at[:, 0].add(lb_re * h0r - lb_im * h0i)
    bu_im = bu_im.at[:, 0].add(lb_re * h0i + lb_im * h0r)
    ar = jnp.broadcast_to(lb_re, bu_re.shape)
    ai = jnp.broadcast_to(lb_im, bu_im.shape)
    _, _, h_re, h_im = lax.associative_scan(complex_affine_combine, (ar, ai, bu_re, bu_im), axis=1)
    y = (jnp.einsum('blgp,gip->blgi', h_re, prm['ssm_c_re'].astype(f32))
         - jnp.einsum('blgp,gip->blgi', h_im, prm['ssm_c_im'].astype(f32))
         + prm['ssm_d'].astype(f32) * uf)
    return y.reshape(bn, length, SSM_WIDTH).astype(u.dtype), h_re[:, -1], h_im[:, -1]


def project(x, c, pos, prm):
    bn, length, _ = x.shape
    ada = (jax.nn.silu(c) @ prm['w_ada'] + prm['b_ada']).reshape(bn, 6, D_MODEL)
    shift1, scale1, gate1, shift2, scale2, gate2 = [ada[:, i][:, None, :] for i in range(6)]
    h = rmsnorm(x, prm['norm_mix_g']) * (1.0 + scale1) + shift1
    z = h @ prm['w_in']
    o1 = ATTN_WIDTH
    o2 = o1 + N_BRANCH * KV_COLS
    o3 = o2 + N_BRANCH * N_HEADS
    q = rope(rmsnorm(z[..., :o1].reshape(bn, length, N_HEADS, HEAD_DIM), prm['q_norm_g']), pos)
    kv = z[..., o1:o2].reshape(bn, length, N_BRANCH, 2, N_KV, HEAD_DIM)
    k = rope(rmsnorm(kv[:, :, :, 0], prm['k_norm_g'][:, None, :]), pos)
    kv = jnp.stack([k, kv[:, :, :, 1]], axis=3)
    gates = jax.nn.sigmoid(z[..., o2:o3]).reshape(bn, length, N_HEADS, N_BRANCH)
    u = z[..., o3:]
    return q, kv[:, :, 0], kv[:, :, 1], kv[:, :, 2], gates, u, (gate1, shift2, scale2, gate2)


def finish(x, attn, y_ssm, ada, prm):
    gate1, shift2, scale2, gate2 = ada
    s = jax.nn.gelu(y_ssm)
    s = s * jax.nn.sigmoid(s @ prm['w_glu'] + prm['b_glu'])
    mix = jnp.concatenate([rmsnorm(attn, prm['attn_out_g']), rmsnorm(s, prm['ssm_out_g'])], axis=-1) @ prm['w_out']
    x = x + gate1 * mix
    h = rmsnorm(x, prm['norm_ffn_g']) * (1.0 + scale2) + shift2
    f = (jax.nn.silu(h @ prm['w_ffn_gate']) * (h @ prm['w_ffn_up'])) @ prm['w_ffn_down']
    return x + gate2 * f


def compressed_kv(rows, prm):
    k_cmp = compress(rows[:, :, 0], prm['cmp_pe_k'], prm['cmp_w1_k'], prm['cmp_w2_k'])
    v_cmp = compress(rows[:, :, 1], prm['cmp_pe_v'], prm['cmp_w1_v'], prm['cmp_w2_v'])
    cmp_end = jnp.arange(k_cmp.shape[1]) * CMP_STRIDE + CMP_LEN - 1
    return k_cmp, v_cmp, cmp_end


def prompt_layer(x, c, prm):
    bn, length, _ = x.shape
    pos = jnp.arange(length)
    q, kv_c, kv_s, kv_w, gates, u, ada = project(x, c, pos, prm)
    k_cmp, v_cmp, cmp_end = compressed_kv(kv_c, prm)
    k_blk = to_blocks(kv_s[:, :, 0])
    v_blk = to_blocks(kv_s[:, :, 1])
    kv_w_pad = jnp.pad(kv_w, ((0, 0), (WINDOW, 0), (0, 0), (0, 0), (0, 0)))

    def chunk(ci):
        start = ci * Q_CHUNK
        q_c = lax.dynamic_slice_in_dim(q, start, Q_CHUNK, axis=1)
        g_c = lax.dynamic_slice_in_dim(gates, start, Q_CHUNK, axis=1)
        w_c = lax.dynamic_slice_in_dim(kv_w_pad, start, WINDOW + Q_CHUNK, axis=1)
        q_pos = start + jnp.arange(Q_CHUNK)
        w_pos = start - WINDOW + jnp.arange(WINDOW + Q_CHUNK)
        return nsa_attend(q_c, q_pos, k_cmp, v_cmp, cmp_end, k_blk, v_blk, w_c, w_pos, g_c)

    attn = lax.map(chunk, jnp.arange(length // Q_CHUNK))
    attn = jnp.moveaxis(attn, 0, 1).reshape(bn, length, ATTN_WIDTH)
    h0 = jnp.zeros((bn, N_SSM_GROUPS, SSM_STATE), jnp.float32)
    y_ssm, h_re, h_im = s5_scan(u, h0, h0, prm)
    y = finish(x, attn, y_ssm, ada, prm)
    keep = min(WINDOW, length)
    return y, (kv_c, kv_s, kv_w[:, length - keep:], h_re, h_im)


def sample_layer(x, c, l, cache_kv_cmp, cache_kv_slc, cache_kv_win, state_ssm_re, state_ssm_im, page_table, prm):
    bn, length, _ = x.shape
    past = page_table.shape[1] * PAGE_SIZE
    pos = past + jnp.arange(length)
    q, kv_c, kv_s, kv_w, gates, u, ada = project(x, c, pos, prm)
    row_shape = (bn, past, 2, N_KV, HEAD_DIM)
    full_c = jnp.concatenate([cache_kv_cmp[l, page_table].reshape(row_shape).astype(kv_c.dtype), kv_c], axis=1)
    full_s = jnp.concatenate([cache_kv_slc[l, page_table].reshape(row_shape).astype(kv_s.dtype), kv_s], axis=1)
    k_cmp, v_cmp, cmp_end = compressed_kv(full_c, prm)
    k_blk = to_blocks(full_s[:, :, 0])
    v_blk = to_blocks(full_s[:, :, 1])
    win = jnp.concatenate([cache_kv_win[l].astype(kv_w.dtype), kv_w], axis=1)
    w_pos = past - cache_kv_win.shape[2] + jnp.arange(win.shape[1])
    attn = nsa_attend(q, pos, k_cmp, v_cmp, cmp_end, k_blk, v_blk, win, w_pos, gates)
    y_ssm, h_re, h_im = s5_scan(u, state_ssm_re[l], state_ssm_im[l], prm)
    y = finish(x, attn, y_ssm, ada, prm)
    return y, (kv_c, kv_s, kv_w, h_re, h_im)


def setup_inputs(seed: int = 0) -> dict:
    key = jax.random.key(seed)
    ks = iter(jax.random.split(key, 48))
    f32 = jnp.float32

    def nrm(shape, s):
        return jax.random.normal(next(ks), shape, f32) * s

    n_pages = PAST_LEN // PAGE_SIZE
    n_used = DEC_BATCH * n_pages
    n_phys = n_used + max(1, n_used // 4)
    win_buf = min(WINDOW, PAST_LEN)
    dp = DEPTH
    g, p = N_SSM_GROUPS, SSM_STATE
    x_prompt = nrm((BATCH, SEQ, D_MODEL), 1.0)
    x_sample = nrm((DEC_BATCH, DEC_SEQ, D_MODEL), 1.0)
    cache_kv_cmp = nrm((dp, n_phys, PAGE_SIZE, 2, N_KV, HEAD_DIM), 1.0)
    cache_kv_slc = nrm((dp, n_phys, PAGE_SIZE, 2, N_KV, HEAD_DIM), 1.0)
    cache_kv_win = nrm((dp, DEC_BATCH, win_buf, 2, N_KV, HEAD_DIM), 1.0)
    state_ssm_re = nrm((dp, DEC_BATCH, g, p), 1.0)
    state_ssm_im = nrm((dp, DEC_BATCH, g, p), 1.0)
    page_table = jax.random.permutation(next(ks), n_phys)[:n_used].reshape(DEC_BATCH, n_pages).astype(jnp.int32)
    c_prompt = nrm((BATCH, D_MODEL), 1.0)
    c_sample = nrm((DEC_BATCH, D_MODEL), 1.0)
    return {
        'x_prompt': x_prompt, 'x_sample': x_sample,
        'cache_kv_cmp': cache_kv_cmp, 'cache_kv_slc': cache_kv_slc, 'cache_kv_win': cache_kv_win,
        'state_ssm_re': state_ssm_re, 'state_ssm_im': state_ssm_im, 'page_table': page_table,
        'c_prompt': c_prompt, 'c_sample': c_sample,
        'norm_mix_g': 1.0 + nrm((dp, D_MODEL), 0.01),
        'w_ada': nrm((dp, D_MODEL, 6 * D_MODEL), 0.5 * D_MODEL ** -0.5),
        'b_ada': nrm((dp, 6 * D_MODEL), 0.01),
        'w_in': nrm((dp, D_MODEL, IN_COLS), D_MODEL ** -0.5),
        'q_norm_g': 1.0 + nrm((dp, HEAD_DIM), 0.01),
        'k_norm_g': 1.0 + nrm((dp, N_BRANCH, HEAD_DIM), 0.01),
        'cmp_pe_k': nrm((dp, CMP_LEN, HEAD_DIM), 0.1),
        'cmp_w1_k': nrm((dp, CMP_LEN * HEAD_DIM, CMP_HID), (CMP_LEN * HEAD_DIM) ** -0.5),
        'cmp_w2_k': nrm((dp, CMP_HID, HEAD_DIM), CMP_HID ** -0.5),
        'cmp_pe_v': nrm((dp, CMP_LEN, HEAD_DIM), 0.1),
        'cmp_w1_v': nrm((dp, CMP_LEN * HEAD_DIM, CMP_HID), (CMP_LEN * HEAD_DIM) ** -0.5),
        'cmp_w2_v': nrm((dp, CMP_HID, HEAD_DIM), CMP_HID ** -0.5),
        'ssm_a_re': -0.5 + nrm((dp, g, p), 0.01),
        'ssm_a_im': jnp.pi * jnp.arange(p, dtype=f32)[None, None, :] + nrm((dp, g, p), 0.01),
        'ssm_log_dt': jax.random.uniform(next(ks), (dp, g), f32, math.log(1e-3), math.log(1e-1)),
        'ssm_b_re': nrm((dp, g, p, SSM_GROUP), (2 * SSM_GROUP) ** -0.5),
        'ssm_b_im': nrm((dp, g, p, SSM_GROUP), (2 * SSM_GROUP) ** -0.5),
        'ssm_c_re': nrm((dp, g, SSM_GROUP, p), (2 * p) ** -0.5),
        'ssm_c_im': nrm((dp, g, SSM_GROUP, p), (2 * p) ** -0.5),
        'ssm_d': nrm((dp, g, SSM_GROUP), 1.0),
        'w_glu': nrm((dp, SSM_WIDTH, SSM_WIDTH), SSM_WIDTH ** -0.5),
        'b_glu': nrm((dp, SSM_WIDTH), 0.01),
        'attn_out_g': 1.0 + nrm((dp, ATTN_WIDTH), 0.01),
        'ssm_out_g': 1.0 + nrm((dp, SSM_WIDTH), 0.01),
        'w_out': nrm((dp, ATTN_WIDTH + SSM_WIDTH, D_MODEL), (ATTN_WIDTH + SSM_WIDTH) ** -0.5),
        'norm_ffn_g': 1.0 + nrm((dp, D_MODEL), 0.01),
        'w_ffn_gate': nrm((dp, D_MODEL, D_FF), D_MODEL ** -0.5),
        'w_ffn_up': nrm((dp, D_MODEL, D_FF), D_MODEL ** -0.5),
        'w_ffn_down': nrm((dp, D_FF, D_MODEL), D_FF ** -0.5),
    }


def reference(x_prompt, x_sample, cache_kv_cmp, cache_kv_slc, cache_kv_win, state_ssm_re, state_ssm_im,
              page_table, c_prompt, c_sample, norm_mix_g, w_ada, b_ada, w_in, q_norm_g, k_norm_g,
              cmp_pe_k, cmp_w1_k, cmp_w2_k, cmp_pe_v, cmp_w1_v, cmp_w2_v, ssm_a_re, ssm_a_im, ssm_log_dt,
              ssm_b_re, ssm_b_im, ssm_c_re, ssm_c_im, ssm_d, w_glu, b_glu, attn_out_g, ssm_out_g, w_out,
              norm_ffn_g, w_ffn_gate, w_ffn_up, w_ffn_down):
    xp, xs = x_prompt, x_sample
    outs = [[] for _ in range(10)]
    for l in range(DEPTH):
        prm = dict(
            norm_mix_g=norm_mix_g[l], w_ada=w_ada[l], b_ada=b_ada[l], w_in=w_in[l],
            q_norm_g=q_norm_g[l], k_norm_g=k_norm_g[l],
            cmp_pe_k=cmp_pe_k[l], cmp_w1_k=cmp_w1_k[l], cmp_w2_k=cmp_w2_k[l],
            cmp_pe_v=cmp_pe_v[l], cmp_w1_v=cmp_w1_v[l], cmp_w2_v=cmp_w2_v[l],
            ssm_a_re=ssm_a_re[l], ssm_a_im=ssm_a_im[l], ssm_log_dt=ssm_log_dt[l],
            ssm_b_re=ssm_b_re[l], ssm_b_im=ssm_b_im[l], ssm_c_re=ssm_c_re[l], ssm_c_im=ssm_c_im[l],
            ssm_d=ssm_d[l], w_glu=w_glu[l], b_glu=b_glu[l], attn_out_g=attn_out_g[l],
            ssm_out_g=ssm_out_g[l], w_out=w_out[l], norm_ffn_g=norm_ffn_g[l],
            w_ffn_gate=w_ffn_gate[l], w_ffn_up=w_ffn_up[l], w_ffn_down=w_ffn_down[l])
        xp, st_p = prompt_layer(xp, c_prompt, prm)
        xs, st_s = sample_layer(xs, c_sample, l, cache_kv_cmp, cache_kv_slc, cache_kv_win,
                                state_ssm_re, state_ssm_im, page_table, prm)
        for i, a in enumerate(st_p + st_s):
            outs[i].append(a)
    n = [jnp.stack(o) for o in outs]
    return (xp, xs, n[0], n[1], n[2], n[3], n[4], n[5], n[6], n[7], n[8], n[9])
```

```python
import contextlib
import math
import numpy as np
import ml_dtypes
import concourse.bass as bass
import concourse.mybir as mybir
from concourse.bass_utils import run_bass_kernel_spmd

F32 = mybir.dt.float32
BF16 = mybir.dt.bfloat16
I32 = mybir.dt.int32
AF = mybir.ActivationFunctionType
ALU = mybir.AluOpType
AX = mybir.AxisListType

NW = 8192
OWN = 2048
NT = NW // 512
OWN_T0 = (NW - OWN) // 512
EPS = 1e-6
D = 1024
NCOL = 1816
NPHYS = 5120
MAGIC = 12582912.0
TWO_PI = 2.0 * math.pi


class Tok:
    __slots__ = ("name", "writer", "readers")

    def __init__(self, name=""):
        self.name = name
        self.writer = None
        self.readers = []


class Ins:
    __slots__ = ("eng", "fn", "deps", "signal", "semval", "sem", "is_dma")

    def __init__(self, eng, fn, is_dma):
        self.eng = eng
        self.fn = fn
        self.deps = []
        self.signal = False
        self.semval = None
        self.sem = None
        self.is_dma = is_dma


class Prog:
    ENGS = ("pe", "act", "dve", "pool", "sp")

    def __init__(self, nc, n_dma_sems=32):
        self.nc = nc
        self.streams = {e: [] for e in self.ENGS}
        self.n_dma_sems = n_dma_sems
        self.all = []
        self.pending = {e: [] for e in self.ENGS}
        self.dmas_since = []

    def tok(self, name=""):
        return Tok(name)

    def toks(self, n, name=""):
        return [Tok(f"{name}{i}") for i in range(n)]

    def _add(self, eng, fn, rd, wr, is_dma):
        ins = Ins(eng, fn, is_dma)
        deps = set()
        for t in rd:
            if t.writer is not None:
                deps.add(t.writer)
        for t in wr:
            if t.writer is not None:
                deps.add(t.writer)
            for r in t.readers:
                deps.add(r)
        if self.pending[eng]:
            deps.update(self.pending[eng])
            self.pending[eng] = []
        ins.deps = list(deps)
        if is_dma:
            self.dmas_since.append(ins)
        for t in wr:
            t.writer = ins
            t.readers = []
        for t in rd:
            t.readers.append(ins)
        self.all.append(ins)
        self.streams[eng].append(ins)
        return ins

    def barrier(self):
        lasts = []
        for e in self.ENGS:
            for ins in reversed(self.streams[e]):
                if not ins.is_dma:
                    lasts.append(ins)
                    break
        lasts += self.dmas_since
        self.dmas_since = []
        for e in self.ENGS:
            self.pending[e] = list(self.pending[e]) + lasts

    def op(self, eng, fn, rd=(), wr=()):
        return self._add(eng, fn, rd, wr, False)

    def dma(self, eng, fn, rd=(), wr=()):
        return self._add(eng, fn, rd, wr, True)

    def finalize(self, final_wait_eng="sp"):
        nc = self.nc
        engobj = {"pe": nc.tensor, "act": nc.scalar, "dve": nc.vector, "pool": nc.gpsimd, "sp": nc.sync}
        for ins in self.all:
            for d in ins.deps:
                if d.eng == "pe" and ins.eng == "pe" and not d.is_dma and not ins.is_dma:
                    continue
                d.signal = True
            if ins.is_dma:
                ins.signal = True
        with contextlib.ExitStack() as es:
            csem = {e: es.enter_context(nc.semaphore(f"s_{e}")) for e in self.ENGS}
            qengs = sorted({ins.eng for ins in self.all if ins.is_dma})
            nper = {q: (self.n_dma_sems if q == "sp" else 8) for q in qengs}
            dsems = {q: [es.enter_context(nc.semaphore(f"d_{q}_{i}")) for i in range(nper[q])] for q in qengs}
            ccount = {e: 0 for e in self.ENGS}
            dcount = {q: [0] * nper[q] for q in qengs}
            dlast = {q: [None] * nper[q] for q in qengs}
            dnext = {q: 0 for q in qengs}
            prev_on_sem = {}
            for ins in self.all:
                if ins.is_dma:
                    q = ins.eng
                    k = dnext[q]
                    dnext[q] = (k + 1) % nper[q]
                    dcount[q][k] += 16
                    ins.sem = dsems[q][k]
                    ins.semval = dcount[q][k]
                    if dlast[q][k] is not None:
                        prev_on_sem[ins] = dlast[q][k]
                    dlast[q][k] = ins
                elif ins.signal:
                    ccount[ins.eng] += 1
                    ins.sem = csem[ins.eng]
                    ins.semval = ccount[ins.eng]
            for e in self.ENGS:
                eo = engobj[e]
                waited = {}
                for ins in self.streams[e]:
                    deps = list(ins.deps)
                    if ins in prev_on_sem:
                        deps.append(prev_on_sem[ins])
                    need = {}
                    for d in deps:
                        if d.eng == "pe" and e == "pe" and not d.is_dma and not ins.is_dma:
                            continue
                        key = id(d.sem)
                        if waited.get(key, 0) >= d.semval:
                            continue
                        if key not in need or need[key][1] < d.semval:
                            need[key] = (d.sem, d.semval)
                    for key, (sem, val) in need.items():
                        eo.wait_ge(sem, val)
                        waited[key] = val
                    bi = ins.fn(eo)
                    if ins.signal:
                        bi.then_inc(ins.sem, 16 if ins.is_dma else 1)
            eo = engobj[final_wait_eng]
            for q in qengs:
                for k in range(nper[q]):
                    if dcount[q][k] > 0:
                        eo.wait_ge(dsems[q][k], dcount[q][k])
            self.stats = dict(n_ins=len(self.all), counts=dict(ccount))


class K:
    def __init__(self, nc):
        self.nc = nc
        self.P = Prog(nc)
        self.es = contextlib.ExitStack()

    def dram_in(self, name, shape, dt=F32):
        return self.nc.dram_tensor(name, list(shape), dt, kind="ExternalInput").ap()

    def dram_out(self, name, shape, dt=F32):
        return self.nc.dram_tensor(name, list(shape), dt, kind="ExternalOutput").ap()

    def sb(self, name, shape, dt, es=None):
        return (es or self.es).enter_context(self.nc.sbuf_tensor("s_" + name, list(shape), dt))

    def ps(self, name, shape, dt=F32, es=None):
        return (es or self.es).enter_context(self.nc.psum_tensor("p_" + name, list(shape), dt))

    def scratch(self, name, shape, dt):
        return self.nc.dram_tensor(name, list(shape), dt).ap()

    def mm(self, out, lhsT, rhs, start, stop, rd, wr):
        self.P.op("pe", lambda q: q.matmul(out, lhsT=lhsT, rhs=rhs, start=start, stop=stop), rd, wr)

    def tr(self, out, in_, ident, rd, wr):
        self.P.op("pe", lambda q: q.transpose(out, in_, ident), rd, wr)

    def act(self, out, in_, func, rd, wr, bias=None, scale=None):
        kw = {}
        if bias is not None:
            kw["bias"] = bias
        if scale is not None:
            kw["scale"] = scale
        self.P.op("act", lambda q: q.activation(out=out, in_=in_, func=func, **kw), rd, wr)

    def tt(self, eng, out, in0, in1, op, rd, wr):
        self.P.op(eng, lambda q: q.tensor_tensor(out=out, in0=in0, in1=in1, op=op), rd, wr)

    def ts(self, eng, out, in0, s1, s2, op0, op1, rd, wr):
        if s2 is None:
            self.P.op(eng, lambda q: q.tensor_scalar(out=out, in0=in0, scalar1=s1, scalar2=None, op0=op0), rd, wr)
        else:
            self.P.op(eng, lambda q: q.tensor_scalar(out=out, in0=in0, scalar1=s1, scalar2=s2, op0=op0, op1=op1), rd, wr)

    def stt(self, out, in0, scalar, in1, op0, op1, rd, wr):
        self.P.op("dve", lambda q: q.scalar_tensor_tensor(out=out, in0=in0, scalar=scalar, in1=in1, op0=op0, op1=op1), rd, wr)

    def cp(self, eng, out, in_, rd, wr):
        if eng == "act":
            self.P.op("act", lambda q: q.activation(out=out, in_=in_, func=AF.Identity), rd, wr)
        else:
            self.P.op(eng, lambda q: q.tensor_copy(out=out, in_=in_), rd, wr)

    def recip(self, out, in_, rd, wr):
        self.P.op("dve", lambda q: q.reciprocal(out=out, in_=in_), rd, wr)

    def memset(self, eng, ap, val, wr):
        self.P.op(eng, lambda q: q.memset(ap, val), (), wr)

    def load(self, out, in_, wr, eng="sp", rd=()):
        self.P.dma(eng, lambda q: q.dma_start(out=out, in_=in_), rd, wr)

    def store(self, out, in_, rd, eng="sp"):
        self.P.dma(eng, lambda q: q.dma_start(out=out, in_=in_), rd, ())


def build_program(dbg=False, tlist=None, stop=9, tiles_override=None, noscr=False, chunks=None, ftiles=None, do_sample=True, skip23=False, nsb=4, no_prompt=False):
    tlist = list(range(NT)) if tlist is None else tlist
    nc = bass.Bass("TRN2", target_bir_lowering=False)
    k = K(nc)
    P = k.P
    T = P.tok
    ES = contextlib.ExitStack

    xT = k.dram_in("xT", [D, NW])
    ropeC = k.dram_in("ropeC", [128, NW])
    ropeS = k.dram_in("ropeS", [128, NW])
    tokvalid = k.dram_in("tokvalid", [128, NW])
    w_in_d = k.dram_in("w_in", [D, NCOL])
    w_ada_d = k.dram_in("w_ada", [D, 6 * D])
    b_adaT_d = k.dram_in("b_adaT", [128, 48])
    cT_d = k.dram_in("cT", [128, 8, 5])
    gcols_d = k.dram_in("gcols", [128, 8, 2])
    gqk_d = k.dram_in("gqk", [128, 4])
    ident_d = k.dram_in("ident", [128, 128])
    Rt_d = k.dram_in("Rt", [128, 128])
    ssmc_d = k.dram_in("ssm_cols", [128, 16, 3])
    kk1_d = k.dram_in("kk1", [128, 128])
    LB_d = [k.dram_in(n, [128, 16, 128]) for n in ("LBre", "LBim", "LCre", "LCim")]
    Dcol_d = k.dram_in("Dcol", [128, 4])
    cmp_w1_d = [k.dram_in(n, [2048, 128]) for n in ("cmp_w1_k", "cmp_w1_v")]
    cmp_peT_d = k.dram_in("cmp_peT", [128, 32, 2])
    cmp_w2kp_d = k.dram_in("cmp_w2kp", [128, 2, 128])
    cmp_w2v_d = k.dram_in("cmp_w2v", [128, 64])
    Em_d = k.dram_in("Em", [128, 64, 128], BF16)
    ov_d = k.dram_in("ov", [128, 4, 128], BF16)
    selg_d = k.dram_in("selg", [24, 24, 64])
    sel65_d = k.dram_in("sel65", [65, 128])
    causal4_d = k.dram_in("causal4", [128, 512], BF16)
    cmpb_d = k.dram_in("cmpb", [16, 128, 4, 512], BF16)
    winb_d = k.dram_in("winb", [16, 128, 5, 512], BF16)
    M1_d = k.dram_in("M1", [16, 128, 128])
    M2_d = k.dram_in("M2", [16, 128, 128])
    attn_out = k.dram_out("attn_out", [64, 8, OWN]) if dbg else None
    w_out_d = k.dram_in("w_out", [D, D])
    w_glu_d = k.dram_in("w_glu", [512, 512])
    fcols_d = k.dram_in("fcols", [128, 16])
    w_ffn_gate_d = k.dram_in("w_ffn_gate", [D, 2816])
    w_ffn_up_d = k.dram_in("w_ffn_up", [D, 2816])
    w_ffn_down_d = k.dram_in("w_ffn_down", [2816, D])
    yT_out = k.dram_out("yT_out", [D, OWN])
    xsT_d = k.dram_in("xsT", [128, 8, 16])
    cache_cmp_v = k.dram_in("cache_cmp", [NPHYS * 128, 256]) if do_sample else None
    cache_slc_v = k.dram_in("cache_slc", [NPHYS * 128, 256]) if do_sample else None
    cache_win_d = k.dram_in("cache_win", [4, 512, 256])
    ovs_d = k.dram_in("ovs", [128, 8, 257], BF16)
    Ms_d = k.dram_in("Ms", [4, 2, 257])
    sbias_d = k.dram_in("sbias", [128, 7, 16], BF16)
    ptab_d = k.dram_in("ptab", [128, 4, 128], I32)
    piota_d = k.dram_in("piota", [128, 1])
    ropes_d = k.dram_in("ropes", [128, 2, 16])
    hs0_d = k.dram_in("hs0", [128, 16, 4, 2])
    kvs_out = k.dram_out("kvs_out", [768, 16])
    ysT_out = k.dram_out("ysT_out", [128, 8, 16])
    ssm_s_out = k.dram_out("ssm_s_out", [128, 16, 4, 2])

    kvT_out = k.dram_out("kvT_out", [768, OWN])
    ssm_out = k.dram_out("ssm_out", [128, 16, 2])

    uT_bf_d = k.scratch("uT_bf_d", [128, 4, NW], BF16); t_uTd = T()
    uT_f_d = k.scratch("uT_f_d", [128, 4, OWN], F32); t_uTfd = T()
    KcmpT_d = k.scratch("KcmpT_d", [128, NW], BF16); t_Kcd = T()
    VcmpT_d = k.scratch("VcmpT_d", [128, NW], BF16); t_Vcd = T()

    ident = k.sb("ident", [128, 128], F32); t_ident = T()
    identb = k.sb("identb", [128, 128], BF16); t_identb = T()
    Rt = k.sb("Rt", [128, 128], F32); t_Rt = T()
    onesb = k.sb("onesb", [128, 128], BF16); t_onesb = T()
    blkones = k.sb("blkones", [128, 128], BF16); t_blk = T()
    k.load(ident[:], ident_d, [t_ident])
    k.load(Rt[:], Rt_d, [t_Rt])
    k.cp("dve", identb[:], ident[:], [t_ident], [t_identb])
    k.memset("pool", onesb[:], 1.0, [t_onesb])
    k.memset("pool", blkones[:], 0.0, [t_blk])
    k.memset("pool", blkones[0:64, 0:64], 1.0, [t_blk])
    k.memset("pool", blkones[64:128, 64:128], 1.0, [t_blk])

    adaT = k.sb("adaT", [128, 48, 5], F32); t_ada = T()
    gcols = k.sb("gcols", [128, 8, 2], F32); t_gcols = T()
    gqk = k.sb("gqk", [128, 4], F32); t_gqk = T()
    A1 = k.sb("A1", [128, 8, 5], F32); t_A1 = T()
    A2 = k.sb("A2", [128, 8, 5], F32); t_A2 = T()
    k.load(gcols[:], gcols_d, [t_gcols])
    k.load(gqk[:], gqk_d, [t_gqk])

    mods = k.sb("mods", [128, 6, 8, 16], F32); A1s = k.sb("A1s", [128, 8, 16], F32); A2s = k.sb("A2s", [128, 8, 16], F32); t_mods = T()
    gates_s = k.sb("gates_s", [24, 16], F32); t_gates_s = T()
    us_f = k.sb("us_f", [128, 4, 16], F32); us_b = k.sb("us_b", [128, 4, 16], BF16); t_us = T()
    QTs = k.sb("QTs", [128, 4, 4, 4], BF16); t_QTs = T()
    kvnew = k.sb("kvnew", [128, 4, 16], BF16); t_kvnew = T()
    yssm_s = k.sb("yssm_s", [128, 4, 16], F32); t_yssm_s = T()
    attn_s = k.sb("attn_s", [64, 8, 16], F32); t_attn_s = T()
    k.memset("pool", attn_s[:], 0.0, [t_attn_s])
    eA = ES()
    KslcT = k.sb("KslcT", [128, NW], BF16, eA); t_KslcT = T()
    KwinT = k.sb("KwinT", [128, 2560], BF16, eA); t_KwinT = T()
    QT = k.sb("QT", [128, 16, 4, 128], BF16, eA); t_QT = T()
    gates = k.sb("gates", [24, OWN], F32, eA); t_gates = T()
    yssm_d = k.scratch("yssm_d", [128, 4, OWN], F32); t_yssm = T()
    attn_d = k.scratch("attn_d", [64, 8, OWN], F32); t_attnd = T()
    V1s = k.sb("V1s", [128, 64, 2, 65], BF16, eA); t_V1s = T()
    V1w = k.sb("V1w", [128, 20, 2, 65], BF16, eA); t_V1w = T()
    KcT = k.sb("KcT", [128, 512], BF16, eA); t_KcT = T()
    V1c = k.sb("V1c", [128, 4, 2, 65], BF16, eA); t_V1c = T()
    k.memset("pool", V1s[:], 1.0, [t_V1s])
    k.memset("pool", V1w[:], 1.0, [t_V1w])
    k.memset("pool", V1c[:], 1.0, [t_V1c])
    hst = k.sb("hst", [128, 16, 4], F32, eA); t_hst = T()
    k.memset("dve", hst[:], 0.0, [t_hst])

    with ES() as e0:
        cT = k.sb("cT", [128, 8, 5], F32, e0); t_cT = T()
        scb = k.sb("scb", [128, 8, 5], BF16, e0); t_scb = T()
        b_adaT = k.sb("b_adaT", [128, 48], F32, e0); t_bada = T()
        k.load(cT[:], cT_d, [t_cT])
        k.load(b_adaT[:], b_adaT_d, [t_bada])
        k.act(scb[:], cT[:], AF.Silu, [t_cT], [t_scb])
        ps_ada = k.ps("ps_ada", [128, 48, 5], F32, e0); t_psada = T()
        wada = [k.sb(f"wada{i}", [128, 8, 1024], BF16, e0) for i in range(2)]
        t_wada = [T(), T()]
        w_ada_v = w_ada_d.rearrange("(kt p) n -> p kt n", p=128)
        for i in range(6):
            wb = wada[i % 2]; tw = t_wada[i % 2]
            k.load(wb[:], w_ada_v[:, :, i * 1024:(i + 1) * 1024], [tw], eng="pool")
            for ko in range(8):
                j = i * 8 + ko
                for kt in range(8):
                    k.mm(ps_ada[:, j, :], wb[:, kt, ko * 128:(ko + 1) * 128], scb[:, kt, :], kt == 0, kt == 7,
                         [tw, t_scb], [t_psada])
        k.tt("dve", adaT[:], ps_ada[:], b_adaT[:].unsqueeze(2).to_broadcast([128, 48, 5]), ALU.add,
             [t_psada, t_bada], [t_ada])
        for (A, tA, si, gi) in ((A1, t_A1, 1, 0), (A2, t_A2, 4, 1)):
            k.ts("dve", A[:], adaT[:, si * 8:(si + 1) * 8, :], 1.0, None, ALU.add, None, [t_ada], [tA])
            k.tt("dve", A[:], A[:], gcols[:, :, gi:gi + 1].to_broadcast([128, 8, 5]), ALU.mult, [tA, t_gcols], [tA])

    for i6 in range(6):
        k.cp("dve", mods[:, i6, :, :].rearrange("p k (b q) -> p k b q", q=4),
             adaT[:, i6 * 8:(i6 + 1) * 8, 1:5].unsqueeze(3).to_broadcast([128, 8, 4, 4]), [t_ada], [t_mods])
    k.cp("dve", A1s[:].rearrange("p k (b q) -> p k b q", q=4), A1[:, :, 1:5].unsqueeze(3).to_broadcast([128, 8, 4, 4]), [t_A1], [t_mods])
    k.cp("dve", A2s[:].rearrange("p k (b q) -> p k b q", q=4), A2[:, :, 1:5].unsqueeze(3).to_broadcast([128, 8, 4, 4]), [t_A2], [t_mods])
    P.barrier()
    with ES() as e1:
        w_in = k.sb("w_in", [128, 8, NCOL], BF16, e1); t_win = T()
        for kt_ in range(8):
            k.load(w_in[:, kt_, :], w_in_d[kt_ * 128:(kt_ + 1) * 128, :], [t_win], eng="pool")
        xt = k.sb("xt", [128, 8, 512], F32, e1); t_xt = T()
        hb = k.sb("hb", [128, 8, 512], BF16, e1); t_hb = T()
        tmpf = [k.sb(f"tmpf{i}", [128, 512], F32, e1) for i in range(2)]; t_tmpf = [T(), T()]
        rstd = k.sb("rstd", [128, 512], F32, e1); t_rstd = T()
        cst = k.sb("cst", [128, 512], F32, e1); t_cst = T()
        sst = k.sb("sst", [128, 512], F32, e1); t_sst = T()
        tvl = k.sb("tvl", [128, 512], F32, e1); t_tvl = T()
        sqz = k.sb("sqz", [128, 512], BF16, e1); t_sqz = T()
        rs2 = k.sb("rs2", [128, 512], F32, e1); t_rs2 = T()
        zn = k.sb("zn", [128, 512], F32, e1); t_zn = T()
        r1 = k.sb("r1", [128, 512], F32, e1); t_r1 = T()
        r2 = k.sb("r2", [128, 512], F32, e1); t_r2 = T()
        zraw = k.sb("zraw", [128, 512], F32, e1); t_zraw = T()
        zo = [k.sb(f"zo{i}", [128, 512], F32, e1) for i in range(2)]; t_zo = [T(), T()]
        zb = [k.sb(f"zb{i}", [128, 512], BF16, e1) for i in range(2)]; t_zb = [T(), T()]
        ps_ss = k.ps("ps_ss", [128, 512], F32, e1); t_pss = T()
        ps_z = [k.ps(f"ps_z{i}", [128, 512], F32, e1) for i in range(2)]; t_psz = [T(), T()]
        ps_n = k.ps("ps_n", [128, 512], F32, e1); t_psn = T()
        ps_r = k.ps("ps_r", [128, 512], F32, e1); t_psr = T()
        ps_tb = k.ps("ps_tb", [128, 4, 128], BF16, e1); t_pstb = T()
        xT_v = xT.rearrange("(kt p) n -> p kt n", p=128)
        zc = 0
        for t in (tlist if stop >= 1 else []):
            own = t >= OWN_T0
            c0 = t * 512
            oc0 = (t - OWN_T0) * 512
            k.load(xt[:], xT_v[:, :, c0:c0 + 512], [t_xt])
            k.load(cst[:], ropeC[:, c0:c0 + 512], [t_cst])
            k.load(sst[:], ropeS[:, c0:c0 + 512], [t_sst])
            if not own:
                k.load(tvl[:], tokvalid[:, c0:c0 + 512], [t_tvl])
            k.act(hb[:], xt[:], AF.Square, [t_xt], [t_hb])
            for kt in range(8):
                k.mm(ps_ss[:], onesb[:], hb[:, kt, :], kt == 0, kt == 7, [t_onesb, t_hb], [t_pss])
            k.act(rstd[:], ps_ss[:], AF.Sqrt, [t_pss], [t_rstd], bias=EPS, scale=1.0 / D)
            k.recip(rstd[:], rstd[:], [t_rstd], [t_rstd])
            for kt in range(8):
                tf = tmpf[kt % 2]; ttf = t_tmpf[kt % 2]
                k.stt(tf[:], xt[:, kt, :], A1[:, kt, 0:1], rstd[:], ALU.mult, ALU.mult, [t_xt, t_A1, t_rstd], [ttf])
                k.act(hb[:, kt, :], tf[:], AF.Identity, [ttf, t_ada], [t_hb], bias=adaT[:, kt, 0:1])
            tiles = list(range(4, 14)) + ([0, 1, 2, 3, 14] if own else [])
            if tiles_override is not None:
                tiles = tiles_override
            for m in tiles:
                pz = ps_z[zc % 2]; tpz = t_psz[zc % 2]
                zz = zo[zc % 2]; tzz = t_zo[zc % 2]
                zzb = zb[zc % 2]; tzzb = t_zb[zc % 2]
                zc += 1
                ncols = 128 if m < 14 else 24
                col0 = m * 128
                for kt in range(8):
                    k.mm(pz[0:ncols, :], w_in[:, kt, col0:col0 + ncols], hb[:, kt, :], kt == 0, kt == 7,
                         [t_win, t_hb], [tpz])
                if m == 14:
                    k.act(gates[:, oc0:oc0 + 512], pz[0:24, :], AF.Sigmoid, [tpz], [t_gates])
                    continue
                if 10 <= m < 14:
                    u_i = m - 10
                    if own:
                        k.cp("act", zz[:], pz[:], [tpz], [tzz])
                        if not noscr:
                            k.store(uT_f_d[:, u_i, oc0:oc0 + 512], zz[:], [tzz])
                        k.cp("pool", zzb[:], zz[:], [tzz], [tzzb])
                    else:
                        k.tt("dve", zzb[:], pz[:], tvl[:], ALU.mult, [tpz, t_tvl], [tzzb])
                    if not noscr:
                        P.dma("sp", lambda q, o=uT_bf_d[:, u_i, c0:c0 + 512], i_=zzb[:]: q.dma_start(out=o, in_=i_), [tzzb], [t_uTd])
                    continue
                is_k = (m < 4) or (m in (4, 6, 8))
                if is_k:
                    gi = 0 if m < 4 else 1 + (m - 4) // 2
                    k.cp("act", zraw[:], pz[:], [tpz], [t_zraw])
                    k.act(sqz[:], zraw[:], AF.Square, [t_zraw], [t_sqz])
                    k.mm(ps_n[:], blkones[:], sqz[:], True, True, [t_blk, t_sqz], [t_psn])
                    k.act(rs2[:], ps_n[:], AF.Sqrt, [t_psn], [t_rs2], bias=EPS, scale=1.0 / 64)
                    k.recip(rs2[:], rs2[:], [t_rs2], [t_rs2])
                    k.stt(zn[:], zraw[:], gqk[:, gi:gi + 1], rs2[:], ALU.mult, ALU.mult, [t_zraw, t_gqk, t_rs2], [t_zn])
                    k.mm(ps_r[:], Rt[:], zn[:], True, True, [t_Rt, t_zn], [t_psr])
                    k.tt("pool", r1[:], zn[:], cst[:], ALU.mult, [t_zn, t_cst], [t_r1])
                    k.tt("dve", r2[:], ps_r[:], sst[:], ALU.mult, [t_psr, t_sst], [t_r2])
                    k.tt("pool", zz[:], r1[:], r2[:], ALU.add, [t_r1, t_r2], [tzz])
                else:
                    k.cp("act", zz[:], pz[:], [tpz], [tzz])
                if m < 4:
                    k.cp("act", QT[:, oc0 // 128:oc0 // 128 + 4, m, :], zz[:].rearrange("p (a b) -> p a b", b=128), [tzz], [t_QT])
                    continue
                if own:
                    k.store(kvT_out[(m - 4) * 128:(m - 3) * 128, oc0:oc0 + 512], zz[:], [tzz])
                if m == 6:
                    k.cp("act", KslcT[:, c0:c0 + 512], zz[:], [tzz], [t_KslcT])
                elif m == 8 and c0 + 512 > NW - 2560:
                    w0 = c0 - (NW - 2560)
                    k.cp("act", KwinT[:, w0:w0 + 512], zz[:], [tzz], [t_KwinT])
                elif m in (4, 5):
                    k.cp("act", zzb[:], zz[:], [tzz], [tzzb])
                    dd, td = (KcmpT_d, t_Kcd) if m == 4 else (VcmpT_d, t_Vcd)
                    P.dma("sp", lambda q, o=dd[:, c0:c0 + 512], i_=zzb[:]: q.dma_start(out=o, in_=i_), [tzzb], [td])
                if m == 7 or (m == 9 and c0 + 512 > NW - 2560):
                    k.cp("act", zzb[:], zz[:], [tzz], [tzzb])
                    for s4 in range(4):
                        k.tr(ps_tb[:, s4, :], zzb[:, s4 * 128:(s4 + 1) * 128], identb[:], [tzzb, t_identb], [t_pstb])
                    if m == 7:
                        dstv, tdv, kt0 = V1s, t_V1s, c0 // 128
                    else:
                        dstv, tdv, kt0 = V1w, t_V1w, (c0 - (NW - 2560)) // 128
                    k.cp("dve", dstv[:, kt0:kt0 + 4, :, 0:64], ps_tb[:].rearrange("p a (g d) -> p a g d", g=2), [t_pstb], [tdv])

        if do_sample:
            xs = k.sb("xs", [128, 8, 16], F32, e1); t_xs = T()
            sqs = k.sb("sqs", [128, 8, 16], BF16, e1); t_sqs = T()
            hsb = k.sb("hsb", [128, 8, 16], BF16, e1); t_hsb = T()
            hsf = k.sb("hsf", [128, 8, 16], F32, e1); t_hsf = T()
            rss = k.sb("rss", [128, 16], F32, e1); t_rss = T()
            rcs = k.sb("rcs", [128, 2, 16], F32, e1); t_rcs = T()
            k.load(xs[:], xsT_d, [t_xs])
            k.load(rcs[:], ropes_d, [t_rcs])
            k.act(sqs[:], xs[:], AF.Square, [t_xs], [t_sqs])
            for kt in range(8):
                k.mm(ps_ss[:, 0:16], onesb[:], sqs[:, kt, :], kt == 0, kt == 7, [t_onesb, t_sqs], [t_pss])
            k.act(rss[:], ps_ss[:, 0:16], AF.Sqrt, [t_pss], [t_rss], bias=EPS, scale=1.0 / D)
            k.recip(rss[:], rss[:], [t_rss], [t_rss])
            k.tt("dve", hsf[:], xs[:], A1s[:], ALU.mult, [t_xs, t_mods], [t_hsf])
            k.tt("dve", hsf[:], hsf[:], rss[:].unsqueeze(1).to_broadcast([128, 8, 16]), ALU.mult, [t_hsf, t_rss], [t_hsf])
            k.tt("dve", hsb[:], hsf[:], mods[:, 0, :, :], ALU.add, [t_hsf, t_mods], [t_hsb])
            for m in range(15):
                pz = ps_z[zc % 2]; tpz = t_psz[zc % 2]
                zz = zo[zc % 2]; tzz = t_zo[zc % 2]
                zc += 1
                ncols = 128 if m < 14 else 24
                col0 = m * 128
                for kt in range(8):
                    k.mm(pz[0:ncols, 0:16], w_in[:, kt, col0:col0 + ncols], hsb[:, kt, :], kt == 0, kt == 7, [t_win, t_hsb], [tpz])
                if m == 14:
                    k.act(gates_s[:], pz[0:24, 0:16], AF.Sigmoid, [tpz], [t_gates_s])
                    continue
                if 10 <= m < 14:
                    k.cp("act", us_f[:, m - 10, :], pz[:, 0:16], [tpz], [t_us])
                    k.cp("act", us_b[:, m - 10, :], us_f[:, m - 10, :], [t_us], [t_us])
                    continue
                is_k = (m < 4) or (m in (4, 6, 8))
                Z = zz[:, 0:16]
                if is_k:
                    gi = 0 if m < 4 else 1 + (m - 4) // 2
                    k.cp("act", zraw[:, 0:16], pz[:, 0:16], [tpz], [t_zraw])
                    k.act(sqz[:, 0:16], zraw[:, 0:16], AF.Square, [t_zraw], [t_sqz])
                    k.mm(ps_n[:, 0:16], blkones[:], sqz[:, 0:16], True, True, [t_blk, t_sqz], [t_psn])
                    k.act(rs2[:, 0:16], ps_n[:, 0:16], AF.Sqrt, [t_psn], [t_rs2], bias=EPS, scale=1.0 / 64)
                    k.recip(rs2[:, 0:16], rs2[:, 0:16], [t_rs2], [t_rs2])
                    k.stt(zn[:, 0:16], zraw[:, 0:16], gqk[:, gi:gi + 1], rs2[:, 0:16], ALU.mult, ALU.mult, [t_zraw, t_gqk, t_rs2], [t_zn])
                    k.mm(ps_r[:, 0:16], Rt[:], zn[:, 0:16], True, True, [t_Rt, t_zn], [t_psr])
                    k.tt("pool", r1[:, 0:16], zn[:, 0:16], rcs[:, 0, :], ALU.mult, [t_zn, t_rcs], [t_r1])
                    k.tt("dve", r2[:, 0:16], ps_r[:, 0:16], rcs[:, 1, :], ALU.mult, [t_psr, t_rcs], [t_r2])
                    k.tt("pool", Z, r1[:, 0:16], r2[:, 0:16], ALU.add, [t_r1, t_r2], [tzz])
                else:
                    k.cp("act", Z, pz[:, 0:16], [tpz], [tzz])
                if m < 4:
                    k.cp("act", QTs[:, :, m, :], Z.rearrange("p (b q) -> p b q", q=4), [tzz], [t_QTs])
                    continue
                k.store(kvs_out[(m - 4) * 128:(m - 3) * 128, :], Z, [tzz])
                if m in (6, 7, 8, 9):
                    k.cp("act", kvnew[:, m - 6, :], Z, [tzz], [t_kvnew])
    P.barrier()
    with ES() as e2:
        ssmc = k.sb("ssmc", [128, 16, 3], F32, e2); t_ssmc = T()
        kk1 = k.sb("kk1", [128, 128], F32, e2); t_kk1 = T()
        k.load(ssmc[:], ssmc_d, [t_ssmc])
        k.load(kk1[:], kk1_d, [t_kk1])
        LB = []
        t_LB = T()
        for i, dd in enumerate(LB_d):
            tl = k.sb(f"LB{i}", [128, 16, 128], BF16, e2)
            k.load(tl[:], dd, [t_LB], eng="pool")
            LB.append(tl)
        LBre, LBim, LCre, LCim = LB
        Dcol = k.sb("Dcol", [128, 4], F32, e2); t_Dcol = T()
        k.load(Dcol[:], Dcol_d, [t_Dcol])
        sm = k.sb("sm", [128, 16, 12], F32, e2); t_sm = T()
        DT, ARD, R_, TH, LBR, LBI, DEN, NRE, FRE, FIM, TMP1, TMP2 = [sm[:, :, i] for i in range(12)]
        a_re = ssmc[:, :, 0]; a_im = ssmc[:, :, 1]; logdt = ssmc[:, :, 2]
        Er = k.sb("Er", [128, 16, 128], F32, e2); Ei = k.sb("Ei", [128, 16, 128], F32, e2)
        Fr = k.sb("Fr", [128, 16, 128], F32, e2); Fi = k.sb("Fi", [128, 16, 128], F32, e2)
        Rm = k.sb("Rm", [128, 16, 128], F32, e2)
        t_tab = T()
        ang = k.sb("ang", [128, 16, 128], F32, e2); t_ang = T()
        ang2 = k.sb("ang2", [128, 16, 128], F32, e2); t_ang2 = T()
        k.act(DT, logdt, AF.Exp, [t_ssmc], [t_sm])
        k.tt("dve", ARD, a_re, DT, ALU.mult, [t_ssmc, t_sm], [t_sm])
        k.act(R_, ARD, AF.Exp, [t_sm], [t_sm])
        k.tt("dve", TH, a_im, DT, ALU.mult, [t_ssmc, t_sm], [t_sm])
        for j in range(16):
            k.ts("dve", ang[:, j, :], kk1[:], sm[:, j, 3:4], None, ALU.mult, None, [t_kk1, t_sm], [t_ang])

        def sin_table(dst, shift):
            src = ang
            ts_ = t_ang
            if shift != 0.0:
                k.ts("dve", ang2[:], ang[:], shift, None, ALU.add, None, [t_ang], [t_ang2])
                src = ang2
                ts_ = t_ang2
            k.ts("dve", dst[:], src[:], 1.0 / TWO_PI, MAGIC, ALU.mult, ALU.add, [ts_], [t_tab])
            k.ts("dve", dst[:], dst[:], MAGIC, None, ALU.subtract, None, [t_tab], [t_tab])
            k.stt(dst[:], dst[:], -TWO_PI, src[:], ALU.mult, ALU.add, [t_tab, ts_], [t_tab])
            k.ts("dve", dst[:], dst[:], 3.1415925, -3.1415925, ALU.min, ALU.max, [t_tab], [t_tab])
            k.act(dst[:], dst[:], AF.Sin, [t_tab], [t_tab])

        sin_table(Ei, 0.0)
        sin_table(Er, math.pi / 2)
        k.tt("dve", LBR, R_, Er[:, :, 0], ALU.mult, [t_sm, t_tab], [t_sm])
        k.tt("dve", LBI, R_, Ei[:, :, 0], ALU.mult, [t_sm, t_tab], [t_sm])
        k.tt("dve", DEN, a_re, a_re, ALU.mult, [t_ssmc], [t_sm])
        k.tt("dve", TMP1, a_im, a_im, ALU.mult, [t_ssmc], [t_sm])
        k.tt("dve", DEN, DEN, TMP1, ALU.add, [t_sm], [t_sm])
        k.recip(DEN, DEN, [t_sm], [t_sm])
        k.ts("dve", NRE, LBR, -1.0, None, ALU.add, None, [t_sm], [t_sm])
        k.tt("dve", TMP1, NRE, a_re, ALU.mult, [t_sm, t_ssmc], [t_sm])
        k.tt("dve", TMP2, LBI, a_im, ALU.mult, [t_sm, t_ssmc], [t_sm])
        k.tt("dve", FRE, TMP1, TMP2, ALU.add, [t_sm], [t_sm])
        k.tt("dve", FRE, FRE, DEN, ALU.mult, [t_sm], [t_sm])
        k.tt("dve", TMP1, LBI, a_re, ALU.mult, [t_sm, t_ssmc], [t_sm])
        k.tt("dve", TMP2, NRE, a_im, ALU.mult, [t_sm, t_ssmc], [t_sm])
        k.tt("dve", FIM, TMP1, TMP2, ALU.subtract, [t_sm], [t_sm])
        k.tt("dve", FIM, FIM, DEN, ALU.mult, [t_sm], [t_sm])
        fre_b = sm[:, :, 8:9].to_broadcast([128, 16, 128])
        fim_b = sm[:, :, 9:10].to_broadcast([128, 16, 128])
        k.tt("dve", Fr[:], Er[:], fre_b, ALU.mult, [t_tab, t_sm], [t_tab])
        k.tt("dve", ang[:], Ei[:], fim_b, ALU.mult, [t_tab, t_sm], [t_ang])
        k.tt("dve", Fr[:], Fr[:], ang[:], ALU.add, [t_tab, t_ang], [t_tab])
        k.tt("dve", Fi[:], Er[:], fim_b, ALU.mult, [t_tab, t_sm], [t_tab])
        k.tt("dve", ang[:], Ei[:], fre_b, ALU.mult, [t_tab, t_sm], [t_ang])
        k.tt("dve", Fi[:], Fi[:], ang[:], ALU.subtract, [t_tab, t_ang], [t_tab])
        k.cp("dve", Rm[:], sm[:, :, 2:3].to_broadcast([128, 16, 128]), [t_sm], [t_tab])
        k.memset("dve", Rm[:, :, 0:1], 0.0, [t_tab])

        ub = k.sb("ub", [128, 4, 512], BF16, e2); t_ub = T()
        uf = k.sb("uf", [128, 4, 512], F32, e2); t_uf = T()
        ga = k.sb("ga", [128, 4, 128], F32, e2); t_ga = T()
        gb = k.sb("gb", [128, 4, 128], F32, e2); t_gb = T()
        gri = k.sb("gri", [128, 4, 128], F32, e2); t_gri = T()
        gii = k.sb("gii", [128, 4, 128], F32, e2); t_gii = T()
        gre = k.sb("gre", [128, 4, 128], F32, e2); t_gre = T()
        gim = k.sb("gim", [128, 4, 128], F32, e2); t_gim = T()
        hre = k.sb("hre", [128, 4, 128], F32, e2); t_hre = T()
        him = k.sb("him", [128, 4, 128], F32, e2); t_him = T()
        hbre = k.sb("hbre", [128, 4, 128], BF16, e2); t_hbre = T()
        hbim = k.sb("hbim", [128, 4, 128], BF16, e2); t_hbim = T()
        ps_bre = k.ps("ps_bre", [128, 4, 128], F32, e2); t_pbre = T()
        ps_bim = k.ps("ps_bim", [128, 4, 128], F32, e2); t_pbim = T()
        ps_y = k.ps("ps_y", [128, 128], F32, e2); t_psy = T()
        yt = k.sb("yt", [128, 4, 512], F32, e2); t_yt = T()

        def fl(ap):
            return ap.rearrange("p a b -> p (a b)")

        for t in (tlist if stop >= 2 else []):
            own = t >= OWN_T0
            c0 = t * 512
            oc0 = (t - OWN_T0) * 512
            k.load(ub[:], uT_bf_d[:, :, c0:c0 + 512], [t_ub], rd=[t_uTd])
            if own:
                k.load(uf[:], uT_f_d[:, :, oc0:oc0 + 512], [t_uf], rd=[t_uTfd])
            for s in range(4):
                sc0 = s * 128
                for hf in range(4):
                    js = list(range(hf * 4, hf * 4 + 4))
                    hs = slice(hf * 4, hf * 4 + 4)
                    for jj, j in enumerate(js):
                        k.mm(ps_bre[:, jj, :], LBre[:, j, :], ub[:, hf, sc0:sc0 + 128], True, True, [t_LB, t_ub], [t_pbre])
                        k.mm(ps_bim[:, jj, :], LBim[:, j, :], ub[:, hf, sc0:sc0 + 128], True, True, [t_LB, t_ub], [t_pbim])
                    Frh = Fr[:, hs, :]; Fih = Fi[:, hs, :]
                    Erh = Er[:, hs, :]; Eih = Ei[:, hs, :]
                    k.tt("dve", ga[:], ps_bre[:], Frh, ALU.mult, [t_pbre, t_tab], [t_ga])
                    k.tt("dve", gb[:], ps_bim[:], Fih, ALU.mult, [t_pbim, t_tab], [t_gb])
                    k.tt("pool", gri[:], ga[:], gb[:], ALU.subtract, [t_ga, t_gb], [t_gri])
                    k.tt("dve", ga[:], ps_bre[:], Fih, ALU.mult, [t_pbre, t_tab], [t_ga])
                    k.tt("dve", gb[:], ps_bim[:], Frh, ALU.mult, [t_pbim, t_tab], [t_gb])
                    k.tt("pool", gii[:], ga[:], gb[:], ALU.add, [t_ga, t_gb], [t_gii])
                    k.tt("dve", gri[:, :, 0], gri[:, :, 0], hst[:, hs, 2], ALU.add, [t_gri, t_hst], [t_gri])
                    k.tt("dve", gii[:, :, 0], gii[:, :, 0], hst[:, hs, 3], ALU.add, [t_gii, t_hst], [t_gii])
                    Rmh = fl(Rm[:, hs, :])
                    P.op("dve", lambda q, o=fl(gre[:]), d0=Rmh, d1=fl(gri[:]):
                         q.tensor_tensor_scan(out=o, data0=d0, data1=d1, initial=0.0, op0=ALU.mult, op1=ALU.add),
                         [t_tab, t_gri], [t_gre])
                    P.op("dve", lambda q, o=fl(gim[:]), d0=Rmh, d1=fl(gii[:]):
                         q.tensor_tensor_scan(out=o, data0=d0, data1=d1, initial=0.0, op0=ALU.mult, op1=ALU.add),
                         [t_tab, t_gii], [t_gim])
                    cs = slice(0, 128) if own else slice(127, 128)
                    k.tt("pool", ga[:, :, cs], gre[:, :, cs], Erh[:, :, cs], ALU.mult, [t_gre, t_tab], [t_ga])
                    k.tt("pool", gb[:, :, cs], gim[:, :, cs], Eih[:, :, cs], ALU.mult, [t_gim, t_tab], [t_gb])
                    k.tt("dve", hre[:, :, cs], ga[:, :, cs], gb[:, :, cs], ALU.subtract, [t_ga, t_gb], [t_hre])
                    k.tt("pool", ga[:, :, cs], gre[:, :, cs], Eih[:, :, cs], ALU.mult, [t_gre, t_tab], [t_ga])
                    k.tt("pool", gb[:, :, cs], gim[:, :, cs], Erh[:, :, cs], ALU.mult, [t_gim, t_tab], [t_gb])
                    k.tt("dve", him[:, :, cs], ga[:, :, cs], gb[:, :, cs], ALU.add, [t_ga, t_gb], [t_him])
                    k.cp("dve", hst[:, hs, 0], hre[:, :, 127], [t_hre], [t_hst])
                    k.cp("dve", hst[:, hs, 1], him[:, :, 127], [t_him], [t_hst])
                    k.tt("dve", hst[:, hs, 2], hre[:, :, 127], sm[:, hs, 2], ALU.mult, [t_hre, t_sm], [t_hst])
                    k.tt("dve", hst[:, hs, 3], him[:, :, 127], sm[:, hs, 2], ALU.mult, [t_him, t_sm], [t_hst])
                    if own:
                        k.cp("act", hbre[:], hre[:], [t_hre], [t_hbre])
                        k.act(hbim[:], him[:], AF.Copy, [t_him], [t_hbim], scale=-1.0)
                        for jj, j in enumerate(js):
                            k.mm(ps_y[:], LCre[:, j, :], hbre[:, jj, :], jj == 0, False, [t_LB, t_hbre], [t_psy])
                            k.mm(ps_y[:], LCim[:, j, :], hbim[:, jj, :], False, jj == 3, [t_LB, t_hbim], [t_psy])
                        k.stt(yt[:, hf, sc0:sc0 + 128], uf[:, hf, sc0:sc0 + 128], Dcol[:, hf:hf + 1], ps_y[:],
                              ALU.mult, ALU.add, [t_uf, t_Dcol, t_psy], [t_yt])
            if own:
                P.dma("sp", lambda q, o=yssm_d[:, :, oc0:oc0 + 512], i_=yt[:]: q.dma_start(out=o, in_=i_), [t_yt], [t_yssm])
        k.store(ssm_out[:, :, :], hst[:, :, 0:2], [t_hst])

        if do_sample:
            hs0 = k.sb("hs0", [128, 16, 4, 2], F32, e2); t_hs0 = T()
            k.load(hs0[:], hs0_d, [t_hs0])
            rh0 = k.sb("rh0", [128, 16, 4, 2], F32, e2); t_rh0 = T()
            k.tt("dve", rh0[:], hs0[:], sm[:, :, 2:3].unsqueeze(3).to_broadcast([128, 16, 4, 2]), ALU.mult, [t_hs0, t_sm], [t_rh0])
            Rms = k.sb("Rms", [128, 16, 4, 4], F32, e2); t_Rms = T()
            k.cp("dve", Rms[:], sm[:, :, 2:3].unsqueeze(3).to_broadcast([128, 16, 4, 4]), [t_sm], [t_Rms])
            k.memset("dve", Rms[:, :, :, 0:1], 0.0, [t_Rms])
            hso = k.sb("hso", [128, 16, 4, 2], F32, e2); t_hso = T()
            sw = [k.sb(f"sw{i}", [128, 4, 4, 4], F32, e2) for i in range(8)]; t_sw = [T() for _ in range(8)]
            shb = [k.sb(f"shb{i}", [128, 4, 16], BF16, e2) for i in range(2)]; t_shb = [T(), T()]
            sga, sgb, sgri, sgii, sgre, sgim, shre, shim = sw
            tga, tgb, tgri, tgii, tgre, tgim, thre, thim = t_sw

            def fl2(ap):
                return ap.rearrange("p a b c -> p (a b c)")

            def b4(ap):
                return ap.unsqueeze(2).to_broadcast([128, 4, 4, 4])

            for hf in range(4):
                hs = slice(hf * 4, hf * 4 + 4)
                for jj in range(4):
                    j = hf * 4 + jj
                    k.mm(ps_bre[:, jj, 0:16], LBre[:, j, :], us_b[:, hf, :], True, True, [t_LB, t_us], [t_pbre])
                    k.mm(ps_bim[:, jj, 0:16], LBim[:, j, :], us_b[:, hf, :], True, True, [t_LB, t_us], [t_pbim])
                bre4 = ps_bre[:, :, 0:16].rearrange("p j (b q) -> p j b q", q=4)
                bim4 = ps_bim[:, :, 0:16].rearrange("p j (b q) -> p j b q", q=4)
                Fr4 = b4(Fr[:, hs, 0:4]); Fi4 = b4(Fi[:, hs, 0:4]); Er4 = b4(Er[:, hs, 0:4]); Ei4 = b4(Ei[:, hs, 0:4])
                k.tt("dve", sga[:], bre4, Fr4, ALU.mult, [t_pbre, t_tab], [tga])
                k.tt("dve", sgb[:], bim4, Fi4, ALU.mult, [t_pbim, t_tab], [tgb])
                k.tt("dve", sgri[:], sga[:], sgb[:], ALU.subtract, [tga, tgb], [tgri])
                k.tt("dve", sga[:], bre4, Fi4, ALU.mult, [t_pbre, t_tab], [tga])
                k.tt("dve", sgb[:], bim4, Fr4, ALU.mult, [t_pbim, t_tab], [tgb])
                k.tt("dve", sgii[:], sga[:], sgb[:], ALU.add, [tga, tgb], [tgii])
                k.tt("dve", sgri[:, :, :, 0], sgri[:, :, :, 0], rh0[:, hs, :, 0], ALU.add, [tgri, t_rh0], [tgri])
                k.tt("dve", sgii[:, :, :, 0], sgii[:, :, :, 0], rh0[:, hs, :, 1], ALU.add, [tgii, t_rh0], [tgii])
                Rmsh = fl2(Rms[:, hs, :, :])
                P.op("dve", lambda q, o=fl2(sgre[:]), d0=Rmsh, d1=fl2(sgri[:]):
                     q.tensor_tensor_scan(out=o, data0=d0, data1=d1, initial=0.0, op0=ALU.mult, op1=ALU.add), [t_Rms, tgri], [tgre])
                P.op("dve", lambda q, o=fl2(sgim[:]), d0=Rmsh, d1=fl2(sgii[:]):
                     q.tensor_tensor_scan(out=o, data0=d0, data1=d1, initial=0.0, op0=ALU.mult, op1=ALU.add), [t_Rms, tgii], [tgim])
                k.tt("dve", sga[:], sgre[:], Er4, ALU.mult, [tgre, t_tab], [tga])
                k.tt("dve", sgb[:], sgim[:], Ei4, ALU.mult, [tgim, t_tab], [tgb])
                k.tt("dve", shre[:], sga[:], sgb[:], ALU.subtract, [tga, tgb], [thre])
                k.tt("dve", sga[:], sgre[:], Ei4, ALU.mult, [tgre, t_tab], [tga])
                k.tt("dve", sgb[:], sgim[:], Er4, ALU.mult, [tgim, t_tab], [tgb])
                k.tt("dve", shim[:], sga[:], sgb[:], ALU.add, [tga, tgb], [thim])
                k.cp("dve", hso[:, hs, :, 0], shre[:, :, :, 3], [thre], [t_hso])
                k.cp("dve", hso[:, hs, :, 1], shim[:, :, :, 3], [thim], [t_hso])
                k.cp("act", shb[0][:], shre[:].rearrange("p j b q -> p j (b q)"), [thre], [t_shb[0]])
                k.act(shb[1][:], shim[:].rearrange("p j b q -> p j (b q)"), AF.Copy, [thim], [t_shb[1]], scale=-1.0)
                for jj in range(4):
                    j = hf * 4 + jj
                    k.mm(ps_y[:, 0:16], LCre[:, j, :], shb[0][:, jj, :], jj == 0, False, [t_LB, t_shb[0]], [t_psy])
                    k.mm(ps_y[:, 0:16], LCim[:, j, :], shb[1][:, jj, :], False, jj == 3, [t_LB, t_shb[1]], [t_psy])
                k.stt(yssm_s[:, hf, :], us_f[:, hf, :], Dcol[:, hf:hf + 1], ps_y[:, 0:16], ALU.mult, ALU.add, [t_us, t_Dcol, t_psy], [t_yssm_s])
            k.store(ssm_s_out, hso[:], [t_hso])
    if skip23:
        P.finalize()
        eA.close()
        k.es.close()
        return nc, P
    if not no_prompt:
        P.barrier()
        with ES() as e3:
            Xc = [k.sb(f"Xc{i}", [128, NW], BF16, e3) for i in range(2)]; t_Xc = [T(), T()]
            k.load(Xc[0][:], KcmpT_d, [t_Xc[0]], rd=[t_Kcd])
            k.load(Xc[1][:], VcmpT_d, [t_Xc[1]], rd=[t_Vcd])
            w1 = [k.sb(f"w1_{i}", [128, 32, 128], BF16, e3) for i in range(2)]; t_w1 = T()
            for i in range(2):
                src = cmp_w1_d[i].rearrange("(r d) h -> d r h", d=64)
                k.load(w1[i][0:64], src, [t_w1], eng="pool")
                k.load(w1[i][64:128], src, [t_w1], eng="pool")
            peT = k.sb("peT", [128, 32, 2], BF16, e3); t_peT = T()
            k.load(peT[:], cmp_peT_d, [t_peT], eng="pool")
            w2kp = k.sb("w2kp", [128, 2, 128], BF16, e3); t_w2 = T()
            w2v = k.sb("w2v", [128, 64], BF16, e3)
            k.load(w2kp[:], cmp_w2kp_d, [t_w2], eng="pool")
            k.load(w2v[:], cmp_w2v_d, [t_w2], eng="pool")
            pebias = k.sb("pebias", [128, 2], F32, e3); t_peb = T()
            ps_pb = k.ps("ps_pb", [128, 2], F32, e3); t_pspb = T()
            ps_pre = k.ps("ps_pre", [128, 512], F32, e3); t_pspre = T()
            ps_kc = k.ps("ps_kc", [128, 512], F32, e3); t_pskc = T()
            ps_vc = k.ps("ps_vc", [128, 64], F32, e3); t_psvc = T()
            for kv in range(2):
                for r in range(32):
                    k.mm(ps_pb[:, kv:kv + 1], w1[kv][0:64, r, :], peT[0:64, r, kv:kv + 1], r == 0, r == 31, [t_w1, t_peT], [t_pspb])
            k.cp("dve", pebias[:], ps_pb[:], [t_pspb], [t_peb])
            xg = k.sb("xg", [128, 512], F32, e3); t_xg = T()
            tg = k.sb("tg", [128, 512], F32, e3); t_tg = T()
            sg = k.sb("sg", [128, 512], F32, e3); t_sg = T()
            gh = [k.sb(f"gh{i}", [128, 512], BF16, e3) for i in range(2)]; t_gh = [T(), T()]
            for g in range(2):
                k.memset("dve", gh[g][:], 0.0, [t_gh[g]])
            for kv in range(2):
                for g in range(2):
                    gs = slice(g * 64, (g + 1) * 64)
                    for r in range(32):
                        k.mm(ps_pre[:, 0:511], w1[kv][gs, r, :], Xc[kv][gs, r:r + 16 * 510 + 1:16], r == 0, r == 31,
                             [t_w1, t_Xc[kv]], [t_pspre])
                    k.act(xg[:, 0:511], ps_pre[:, 0:511], AF.Identity, [t_pspre, t_peb], [t_xg], bias=pebias[:, kv:kv + 1])
                    k.tt("dve", tg[:, 0:511], xg[:, 0:511], xg[:, 0:511], ALU.mult, [t_xg], [t_tg])
                    k.ts("dve", tg[:, 0:511], tg[:, 0:511], 0.044715, 1.0, ALU.mult, ALU.add, [t_tg], [t_tg])
                    k.tt("dve", tg[:, 0:511], tg[:, 0:511], xg[:, 0:511], ALU.mult, [t_tg, t_xg], [t_tg])
                    k.act(sg[:, 0:511], tg[:, 0:511], AF.Sigmoid, [t_tg], [t_sg], scale=1.5957691216)
                    k.tt("dve", gh[g][:, 0:511], xg[:, 0:511], sg[:, 0:511], ALU.mult, [t_xg, t_sg], [t_gh[g]])
                    if kv == 1:
                        for ct in range(4):
                            k.mm(ps_vc[:], gh[g][:, ct * 128:(ct + 1) * 128], w2v[:], True, True, [t_gh[g], t_w2], [t_psvc])
                            k.cp("dve", V1c[:, ct, g, 0:64], ps_vc[:], [t_psvc], [t_V1c])
                if kv == 0:
                    for g in range(2):
                        k.mm(ps_kc[:], w2kp[:, g, :], gh[g][:], g == 0, g == 1, [t_w2, t_gh[g]], [t_pskc])
                    k.cp("dve", KcT[:], ps_kc[:], [t_pskc], [t_KcT])

        P.barrier()
        with ES() as e4:
            Em = k.sb("Em", [128, 64, 128], BF16, e4); t_Em = T()
            ov = k.sb("ov", [128, 4, 128], BF16, e4); t_ov = T()
            selg = k.sb("selg", [24, 24, 64], F32, e4); t_selg = T()
            sel65 = k.sb("sel65", [65, 128], F32, e4); t_sel65 = T()
            causal4 = k.sb("causal4", [128, 512], BF16, e4); t_causal = T()
            k.load(Em[:], Em_d, [t_Em]); k.load(ov[:], ov_d, [t_ov]); k.load(selg[:], selg_d, [t_selg])
            k.load(sel65[:], sel65_d, [t_sel65]); k.load(causal4[:], causal4_d, [t_causal])
            cmpb = k.sb("cmpb", [128, 4, 512], BF16, e4); t_cmpb = T()
            winb = k.sb("winb", [128, 5, 512], BF16, e4); t_winb = T()
            M1 = k.sb("M1", [128, 128], F32, e4); M2 = k.sb("M2", [128, 128], F32, e4); t_M = T()
            Pc = [k.sb(f"Pc{i}", [128, 512], BF16, e4) for i in range(4)]; t_Pc = [T() for _ in range(4)]
            Pn = [k.sb(f"Pn{i}", [128, 512], BF16, e4) for i in range(4)]; t_Pn = [T() for _ in range(4)]
            Pt = [k.sb(f"Pt{i}", [128, 512], BF16, e4) for i in range(3)]; t_Pt = [T() for _ in range(3)]
            osb = [k.sb(f"osb{i}", [65, 512], F32, e4) for i in range(3)]; t_osb = [T() for _ in range(3)]
            rden = [k.sb(f"rden{i}", [128, 512], F32, e4) for i in range(3)]; t_rden = [T() for _ in range(3)]
            imp2 = k.sb("imp2", [128, 128], F32, e4); t_imp2 = T()
            imp3 = k.sb("imp3", [128, 128], F32, e4); t_imp3 = T()
            mx = k.sb("mx", [128, 16], F32, e4); t_mx = T()
            thr = k.sb("thr", [128, 1], F32, e4); t_thr = T()
            nsel = k.sb("nsel", [128, 128], BF16, e4); t_nsel = T()
            nselT4 = k.sb("nselT4", [128, 4, 128], BF16, e4); t_nselT = T()
            acc = k.sb("acc", [64, 512], F32, e4); t_acc = T()
            tA = k.sb("tA", [64, 512], F32, e4); t_tA = T()
            tB = k.sb("tB", [64, 512], F32, e4); t_tB = T()
            ps_s = [k.ps(f"ps_s{i}", [128, 512], F32, e4) for i in range(3)]; t_pss_ = [T() for _ in range(3)]
            ps_o = k.ps("ps_o", [65, 512], F32, e4); t_pso = T()
            ps_den = k.ps("ps_den", [128, 512], F32, e4); t_psden = T()
            ps_imp = k.ps("ps_imp", [128, 128], F32, e4); t_psimp = T()
            ps_tt = k.ps("ps_tt", [128, 128], BF16, e4); t_pstt = T()
            ps_g = k.ps("ps_g", [64, 512], F32, e4); t_psg = T()
            sc_ = [0]

            def score_tile(lhsT_k, qt, extra, rd_extra):
                i3 = sc_[0] % 3
                sc_[0] += 1
                pss, tps = ps_s[i3], t_pss_[i3]
                n = len(extra)
                k.mm(pss[:], lhsT_k, qt, True, n == 0, rd_extra + [t_QT], [tps])
                for ei, (l_, r_, rds) in enumerate(extra):
                    k.mm(pss[:], l_, r_, False, ei == n - 1, rds, [tps])
                return pss, tps

            def finish_branch(bi):
                k.cp("act", osb[bi][:], ps_o[:], [t_pso], [t_osb[bi]])
                k.mm(ps_den[:], sel65[:], osb[bi][:], True, True, [t_sel65, t_osb[bi]], [t_psden])
                k.ts("dve", rden[bi][:], ps_den[:], 1e-30, None, ALU.max, None, [t_psden], [t_rden[bi]])
                k.recip(rden[bi][:], rden[bi][:], [t_rden[bi]], [t_rden[bi]])

            for i in (chunks if chunks is not None else range(16)):
                ktd = 48 + i
                k.load(cmpb[:], cmpb_d[i], [t_cmpb])
                k.load(winb[:], winb_d[i], [t_winb])
                k.load(M1[:], M1_d[i], [t_M]); k.load(M2[:], M2_d[i], [t_M])
                for g in range(2):
                    gs = slice(g * 64, (g + 1) * 64)
                    qt = QT[gs, i, :, :].rearrange("p a b -> p (a b)")
                    for ct in range(4):
                        pss, tps = score_tile(KcT[gs, ct * 128:(ct + 1) * 128], qt,
                                              [(identb[:], cmpb[:, ct, :], [t_identb, t_cmpb])], [t_KcT])
                        k.act(Pc[ct][:], pss[:], AF.Exp, [tps], [t_Pc[ct]], scale=0.125)
                        k.mm(ps_o[:], V1c[:, ct, g, :], Pc[ct][:], ct == 0, ct == 3, [t_V1c, t_Pc[ct]], [t_pso])
                    finish_branch(0)
                    for ct in range(4):
                        k.tt("pool", Pn[ct][:], Pc[ct][:], rden[0][:], ALU.mult, [t_Pc[ct], t_rden[0]], [t_Pn[ct]])
                    for ct in range(4):
                        for r in range(4):
                            k.mm(ps_imp[:], Pn[ct][:, r * 128:(r + 1) * 128], ov[:, ct, :], ct == 0 and r == 0, ct == 3 and r == 3,
                                 [t_Pn[ct], t_ov], [t_psimp])
                    k.tt("dve", imp2[:], ps_imp[:], M1[:], ALU.mult, [t_psimp, t_M], [t_imp2])
                    k.tt("dve", imp2[:], imp2[:], M2[:], ALU.add, [t_imp2, t_M], [t_imp2])
                    P.op("dve", lambda q: q.max(out=mx[:, 0:8], in_=imp2[:]), [t_imp2], [t_mx])
                    P.op("dve", lambda q: q.match_replace(out=imp3[:], in_to_replace=mx[:, 0:8], in_values=imp2[:], imm_value=-2e30),
                         [t_imp2, t_mx], [t_imp3])
                    P.op("dve", lambda q: q.max(out=mx[:, 8:16], in_=imp3[:]), [t_imp3], [t_mx])
                    k.ts("dve", thr[:], mx[:, 15:16], -1e29, None, ALU.max, None, [t_mx], [t_thr])
                    k.ts("dve", imp3[:], imp2[:], thr[:, 0:1], 1.0, ALU.is_ge, ALU.subtract, [t_imp2, t_thr], [t_imp3])
                    k.ts("dve", nsel[:], imp3[:], 30000.0, None, ALU.mult, None, [t_imp3], [t_nsel])
                    k.tr(ps_tt[:], nsel[:], identb[:], [t_nsel, t_identb], [t_pstt])
                    k.cp("dve", nselT4[:], ps_tt[:].unsqueeze(1).to_broadcast([128, 4, 128]), [t_pstt], [t_nselT])
                    nsT = nselT4[:].rearrange("p a b -> p (a b)")
                    for kt in range(ktd + 1):
                        extra = [(Em[:, kt, :], nsT, [t_Em, t_nselT])]
                        if kt == ktd:
                            extra.append((identb[:], causal4[:], [t_identb, t_causal]))
                        pss, tps = score_tile(KslcT[gs, kt * 128:(kt + 1) * 128], qt, extra, [t_KslcT])
                        pt, tpt = Pt[kt % 3], t_Pt[kt % 3]
                        k.act(pt[:], pss[:], AF.Exp, [tps], [tpt], scale=0.125)
                        k.mm(ps_o[:], V1s[:, kt, g, :], pt[:], kt == 0, kt == ktd, [t_V1s, tpt], [t_pso])
                    finish_branch(1)
                    for w in range(5):
                        kw = i + w
                        pss, tps = score_tile(KwinT[gs, kw * 128:(kw + 1) * 128], qt,
                                              [(identb[:], winb[:, w, :], [t_identb, t_winb])], [t_KwinT])
                        pt, tpt = Pt[w % 3], t_Pt[w % 3]
                        k.act(pt[:], pss[:], AF.Exp, [tps], [tpt], scale=0.125)
                        k.mm(ps_o[:], V1w[:, kw, g, :], pt[:], w == 0, w == 4, [t_V1w, tpt], [t_pso])
                    finish_branch(2)
                    for n in range(3):
                        for r in range(4):
                            k.mm(ps_g[:, r * 128:(r + 1) * 128], selg[:, (g * 4 + r) * 3 + n, :], gates[:, i * 128:(i + 1) * 128],
                                 True, True, [t_selg, t_gates], [t_psg])
                        k.tt("pool", tA[:], osb[n][0:64, :], rden[n][0:64, :], ALU.mult, [t_osb[n], t_rden[n]], [t_tA])
                        if n == 0:
                            k.tt("dve", acc[:], tA[:], ps_g[:], ALU.mult, [t_tA, t_psg], [t_acc])
                        else:
                            k.tt("dve", tB[:], tA[:], ps_g[:], ALU.mult, [t_tA, t_psg], [t_tB])
                            k.tt("pool", acc[:], acc[:], tB[:], ALU.add, [t_acc, t_tB], [t_acc])
                    P.dma("sp", lambda q, o=attn_d[:, g * 4:(g + 1) * 4, i * 128:(i + 1) * 128], i_=acc[:].rearrange("p (a b) -> p a b", b=128):
                          q.dma_start(out=o, in_=i_), [t_acc], [t_attnd])
            if dbg:
                att_sb = k.sb("att_sb", [64, 8, 128], F32, e4); t_attsb = T()
                for i in (chunks if chunks is not None else range(16)):
                    k.load(att_sb[:], attn_d[:, :, i * 128:(i + 1) * 128], [t_attsb], rd=[t_attnd])
                    k.store(attn_out[:, :, i * 128:(i + 1) * 128], att_sb[:], [t_attsb])

    eA.close()
    P.barrier()
    if do_sample:
      with ES() as e6:
        NS = 16896
        big = k.sb("big", [128, 2, NS], BF16, e6); t_big = [T(), T()]
        XK = big[:, 0, :]; XV = big[:, 1, :]; KsT = big[:, 0, :]
        V1ss = big[:, 1, 0:129 * 130].rearrange("p (t g e) -> p t g e", g=2, e=65)
        Em = k.sb("Em2", [128, 64, 128], BF16, e6); t_Em = T()
        k.load(Em[:], Em_d, [t_Em])
        w1 = [k.sb(f"w1s_{i}", [128, 32, 128], BF16, e6) for i in range(2)]; t_w1 = T()
        for i in range(2):
            src = cmp_w1_d[i].rearrange("(r d) h -> d r h", d=64)
            k.load(w1[i][0:64], src, [t_w1], eng="pool")
            k.load(w1[i][64:128], src, [t_w1], eng="pool")
        peT = k.sb("peT2", [128, 32, 2], BF16, e6); t_peT = T()
        k.load(peT[:], cmp_peT_d, [t_peT], eng="pool")
        w2kp = k.sb("w2kp2", [128, 2, 128], BF16, e6); t_w2 = T()
        w2v = k.sb("w2v2", [128, 64], BF16, e6)
        k.load(w2kp[:], cmp_w2kp_d, [t_w2], eng="pool")
        k.load(w2v[:], cmp_w2v_d, [t_w2], eng="pool")
        selg = k.sb("selg2", [24, 24, 64], F32, e6); t_selg = T()
        sel65 = k.sb("sel652", [65, 128], F32, e6); t_sel65 = T()
        k.load(selg[:], selg_d, [t_selg]); k.load(sel65[:], sel65_d, [t_sel65])
        ovs = k.sb("ovs", [128, 8, 257], BF16, e6); t_ovs = T()
        Ms = k.sb("Ms", [4, 2, 257], F32, e6); t_Ms = T()
        sbias = k.sb("sbias", [128, 7, 16], BF16, e6); t_sbias = T()
        k.load(ovs[:], ovs_d, [t_ovs]); k.load(Ms[:], Ms_d, [t_Ms]); k.load(sbias[:], sbias_d, [t_sbias])
        ptab = k.sb("ptab", [128, 4, 128], I32, e6); t_ptab = T()
        piota = k.sb("piota", [128, 1], F32, e6); t_piota = T()
        idxs = k.sb("idxs", [128, 4, 128], I32, e6); t_idxs = T()
        k.load(ptab[:], ptab_d, [t_ptab]); k.load(piota[:], piota_d, [t_piota])
        k.ts("dve", idxs[:], ptab[:], 128.0, piota[:, 0:1], ALU.mult, ALU.add, [t_ptab, t_piota], [t_idxs])
        pg = [k.sb(f"pg{i}", [128, 256], BF16, e6) for i in range(4)]; t_pg = [T() for _ in range(4)]
        wpg = k.sb("wpg", [128, 4, 256], BF16, e6); t_wpg = T()
        KwT = k.sb("KwT", [128, 640], BF16, e6); t_KwT = T()
        V1ws = k.sb("V1ws", [128, 5, 2, 65], BF16, e6); t_V1ws = T()
        KcTs = k.sb("KcTs", [128, 1024], BF16, e6); t_KcTs = T()
        V1cs = k.sb("V1cs", [128, 8, 2, 65], BF16, e6); t_V1cs = T()
        k.memset("pool", V1cs[:], 1.0, [t_V1cs])
        pebias = k.sb("pebias2", [128, 2], F32, e6); t_peb = T()
        xg = k.sb("xg2", [128, 512], F32, e6); t_xg = T()
        tg = k.sb("tg2", [128, 512], F32, e6); t_tg = T()
        sg = k.sb("sg2", [128, 512], F32, e6); t_sg = T()
        gh = [k.sb(f"ghs{i}", [128, 1024], BF16, e6) for i in range(2)]; t_gh = [T(), T()]
        P8 = [k.sb(f"P8_{i}", [128, 8, 16], BF16, e6) for i in range(2)]; t_P8 = [T(), T()]
        Pn8 = k.sb("Pn8", [128, 8, 16], BF16, e6); t_Pn8 = T()
        osb = [k.sb(f"osbs{i}", [65, 16], F32, e6) for i in range(3)]; t_osb = [T() for _ in range(3)]
        rden = [k.sb(f"rdens{i}", [128, 16], F32, e6) for i in range(3)]; t_rden = [T() for _ in range(3)]
        imp2 = k.sb("imp2s", [4, 257], F32, e6); t_imp2 = T()
        imp3 = k.sb("imp3s", [4, 257], F32, e6); t_imp3 = T()
        mx = k.sb("mxs", [4, 16], F32, e6); t_mx = T()
        thr = k.sb("thrs", [4, 1], F32, e6); t_thr = T()
        nsel = k.sb("nsels", [4, 384], BF16, e6); t_nsel = T()
        k.memset("dve", nsel[:], 0.0, [t_nsel])
        nselT4 = k.sb("nselT4s", [128, 3, 4, 4], BF16, e6); t_nselT = T()
        acc = k.sb("accs", [64, 16], F32, e6); t_acc = T()
        tA = k.sb("tAs", [64, 16], F32, e6); t_tA = T()
        tB = k.sb("tBs", [64, 16], F32, e6); t_tB = T()
        ps_t1 = [k.ps(f"ps_t1{i}", [128, 4, 128], BF16, e6) for i in range(2)]; t_pt1 = [T(), T()]
        ps_t2 = [k.ps(f"ps_t2{i}", [128, 4, 128], BF16, e6) for i in range(2)]; t_pt2 = [T(), T()]
        ps_m = k.ps("ps_ms", [128, 512], F32, e6); t_psm = T()
        ps_s8 = [k.ps(f"ps_s8{i}", [128, 8, 16], F32, e6) for i in range(2)]; t_ps8 = [T(), T()]
        ps_o = k.ps("ps_os", [128, 512], F32, e6); t_pso = T()
        for kv in range(2):
            for r in range(32):
                k.mm(ps_m[:, kv:kv + 1], w1[kv][0:64, r, :], peT[0:64, r, kv:kv + 1], r == 0, r == 31, [t_w1, t_peT], [t_psm])
        k.cp("dve", pebias[:], ps_m[:, 0:2], [t_psm], [t_peb])
        k.memset("pool", big[:, 1, :], 1.0, [t_big[1]])
        cc_ = [0]

        def gather_pages(cache_v, bi, kdst, on_v):
            for lp in range(128):
                pgi = pg[lp % 4]; tpg = t_pg[lp % 4]
                P.dma("pool", lambda q, o=pgi[:], src=cache_v, ix=idxs[:, bi, lp:lp + 1]:
                      q.indirect_dma_start(out=o, out_offset=None, in_=src, in_offset=bass.IndirectOffsetOnAxis(ap=ix, axis=0)),
                      [t_idxs], [tpg])
                grp = (lp // 4) % 2
                k.tr(ps_t1[grp][:, lp % 4, :], pgi[:, 0:128], identb[:], [tpg, t_identb], [t_pt1[grp]])
                on_v(lp, pgi, tpg, grp)
                if lp % 4 == 3:
                    l0 = lp - 3
                    k.cp("dve", kdst[:, l0 * 128:(l0 + 4) * 128], ps_t1[grp][:].rearrange("p a b -> p (a b)"), [t_pt1[grp]], [t_big[0]])

        for bi in range(nsb):
            def v_cmp(lp, pgi, tpg, grp):
                k.tr(ps_t2[grp][:, lp % 4, :], pgi[:, 128:256], identb[:], [tpg, t_identb], [t_pt2[grp]])
                if lp % 4 == 3:
                    l0 = lp - 3
                    k.cp("act", XV[:, l0 * 128:(l0 + 4) * 128], ps_t2[grp][:].rearrange("p a b -> p (a b)"), [t_pt2[grp]], [t_big[1]])
            gather_pages(cache_cmp_v, bi, XK, v_cmp)
            for g in range(2):
                k.memset("dve", gh[g][:, 1023:1024], 0.0, [t_gh[g]])
            for kv in range(2):
                X = XK if kv == 0 else XV
                for g in range(2):
                    gs = slice(g * 64, (g + 1) * 64)
                    for half in range(2):
                        n = 512 if half == 0 else 511
                        b0 = 16 * 512 * half
                        for r in range(32):
                            k.mm(ps_m[:, 0:n], w1[kv][gs, r, :], X[gs, b0 + r:b0 + r + 16 * (n - 1) + 1:16], r == 0, r == 31,
                                 [t_w1, t_big[kv]], [t_psm])
                        k.act(xg[:, 0:n], ps_m[:, 0:n], AF.Identity, [t_psm, t_peb], [t_xg], bias=pebias[:, kv:kv + 1])
                        k.tt("dve", tg[:, 0:n], xg[:, 0:n], xg[:, 0:n], ALU.mult, [t_xg], [t_tg])
                        k.ts("dve", tg[:, 0:n], tg[:, 0:n], 0.044715, 1.0, ALU.mult, ALU.add, [t_tg], [t_tg])
                        k.tt("dve", tg[:, 0:n], tg[:, 0:n], xg[:, 0:n], ALU.mult, [t_tg, t_xg], [t_tg])
                        k.act(sg[:, 0:n], tg[:, 0:n], AF.Sigmoid, [t_tg], [t_sg], scale=1.5957691216)
                        k.tt("dve", gh[g][:, half * 512:half * 512 + n], xg[:, 0:n], sg[:, 0:n], ALU.mult, [t_xg, t_sg], [t_gh[g]])
                    if kv == 1:
                        for ct in range(8):
                            k.mm(ps_m[:, 0:64], gh[g][:, ct * 128:(ct + 1) * 128], w2v[:], True, True, [t_gh[g], t_w2], [t_psm])
                            k.cp("dve", V1cs[:, ct, g, 0:64], ps_m[:, 0:64], [t_psm], [t_V1cs])
                if kv == 0:
                    for half in range(2):
                        for g in range(2):
                            k.mm(ps_m[:], w2kp[:, g, :], gh[g][:, half * 512:(half + 1) * 512], g == 0, g == 1, [t_w2, t_gh[g]], [t_psm])
                        k.cp("dve", KcTs[:, half * 512:(half + 1) * 512], ps_m[:], [t_psm], [t_KcTs])
            def v_slc(lp, pgi, tpg, grp):
                k.cp("pool", V1ss[:, lp, :, 0:64], pgi[:, 128:256].rearrange("p (g d) -> p g d", g=2), [tpg], [t_big[1]])
            k.memset("pool", V1ss[:, 0:128, :, 64:65], 1.0, [t_big[1]])
            gather_pages(cache_slc_v, bi, KsT, v_slc)
            bc = slice(bi * 4, bi * 4 + 4)
            k.memset("dve", KsT[:, 16384:16512], 0.0, [t_big[0]])
            k.cp("dve", KsT[:, 16384:16388], kvnew[:, 0, bc], [t_kvnew], [t_big[0]])
            k.memset("pool", V1ss[:, 128, :, :], 0.0, [t_big[1]])
            k.tr(ps_t2[0][0:4, 0, :], kvnew[:, 1, bc], identb[:], [t_kvnew, t_identb], [t_pt2[0]])
            k.cp("act", V1ss[0:4, 128, :, 0:64], ps_t2[0][0:4, 0, :].rearrange("p (g d) -> p g d", g=2), [t_pt2[0]], [t_big[1]])
            k.memset("pool", V1ss[0:4, 128, :, 64:65], 1.0, [t_big[1]])
            k.load(wpg[:], cache_win_d[bi].rearrange("(t p) f -> p t f", p=128), [t_wpg], eng="pool")
            for t4 in range(4):
                k.tr(ps_t1[0][:, t4, :], wpg[:, t4, 0:128], identb[:], [t_wpg, t_identb], [t_pt1[0]])
            k.memset("dve", KwT[:], 0.0, [t_KwT])
            k.cp("dve", KwT[:, 0:512], ps_t1[0][:].rearrange("p a b -> p (a b)"), [t_pt1[0]], [t_KwT])
            k.cp("dve", KwT[:, 512:516], kvnew[:, 2, bc], [t_kvnew], [t_KwT])
            k.memset("pool", V1ws[:, 0:4, :, 64:65], 1.0, [t_V1ws])
            k.cp("pool", V1ws[:, 0:4, :, 0:64], wpg[:, :, 128:256].rearrange("p t (g d) -> p t g d", g=2), [t_wpg], [t_V1ws])
            k.memset("pool", V1ws[:, 4, :, :], 0.0, [t_V1ws])
            k.tr(ps_t2[1][0:4, 0, :], kvnew[:, 3, bc], identb[:], [t_kvnew, t_identb], [t_pt2[1]])
            k.cp("act", V1ws[0:4, 4, :, 0:64], ps_t2[1][0:4, 0, :].rearrange("p (g d) -> p g d", g=2), [t_pt2[1]], [t_V1ws])
            k.memset("pool", V1ws[0:4, 4, :, 64:65], 1.0, [t_V1ws])

            def branch_done(bidx):
                k.cp("act", osb[bidx][:], ps_o[0:65, 0:16], [t_pso], [t_osb[bidx]])
                k.mm(ps_m[:, 0:16], sel65[:], osb[bidx][:], True, True, [t_sel65, t_osb[bidx]], [t_psm])
                k.ts("dve", rden[bidx][:], ps_m[:, 0:16], 1e-30, None, ALU.max, None, [t_psm], [t_rden[bidx]])
                k.recip(rden[bidx][:], rden[bidx][:], [t_rden[bidx]], [t_rden[bidx]])

            def run_tiles(ntiles, kfn, extra_fn, vfn, rdk, rdv):
                done = 0
                while done < ntiles:
                    n8 = min(8, ntiles - done)
                    pi = cc_[0] % 2; cc_[0] += 1
                    pss, tps = ps_s8[pi], t_ps8[pi]
                    for t8 in range(n8):
                        kt = done + t8
                        ex = extra_fn(kt)
                        k.mm(pss[:, t8, :], kfn(kt), qt, True, len(ex) == 0, rdk + [t_QTs], [tps])
                        for ei, (l_, r_, rds) in enumerate(ex):
                            k.mm(pss[:, t8, :], l_, r_, False, ei == len(ex) - 1, rds, [tps])
                    p8, tp8 = P8[pi], t_P8[pi]
                    k.act(p8[:, 0:n8, :], pss[:, 0:n8, :], AF.Exp, [tps], [tp8], scale=0.125)
                    for t8 in range(n8):
                        kt = done + t8
                        k.mm(ps_o[0:65, 0:16], vfn(kt), p8[:, t8, :], kt == 0, kt == ntiles - 1, rdv + [tp8], [t_pso])
                    done += n8
                return p8, tp8

            for g in range(2):
                gs = slice(g * 64, (g + 1) * 64)
                qt = QTs[gs, bi, :, :].rearrange("p a b -> p (a b)")
                p8, tp8 = run_tiles(8, lambda kt: KcTs[gs, kt * 128:(kt + 1) * 128],
                                    lambda kt: ([(identb[:], sbias[:, 0, :], [t_identb, t_sbias])] if kt == 7 else []),
                                    lambda kt: V1cs[:, kt, g, :], [t_KcTs], [t_V1cs])
                branch_done(0)
                k.tt("dve", Pn8[:], p8[:], rden[0][:].unsqueeze(1).to_broadcast([128, 8, 16]), ALU.mult, [tp8, t_rden[0]], [t_Pn8])
                for ct in range(8):
                    for r in range(4):
                        k.mm(ps_m[0:4, 0:257], Pn8[:, ct, r * 4:(r + 1) * 4], ovs[:, ct, :], ct == 0 and r == 0, ct == 7 and r == 3,
                             [t_Pn8, t_ovs], [t_psm])
                k.tt("dve", imp2[:], ps_m[0:4, 0:257], Ms[:, 0, :], ALU.mult, [t_psm, t_Ms], [t_imp2])
                k.tt("dve", imp2[:], imp2[:], Ms[:, 1, :], ALU.add, [t_imp2, t_Ms], [t_imp2])
                P.op("dve", lambda q: q.max(out=mx[:, 0:8], in_=imp2[:]), [t_imp2], [t_mx])
                P.op("dve", lambda q: q.match_replace(out=imp3[:], in_to_replace=mx[:, 0:8], in_values=imp2[:], imm_value=-2e30),
                     [t_imp2, t_mx], [t_imp3])
                P.op("dve", lambda q: q.max(out=mx[:, 8:16], in_=imp3[:]), [t_imp3], [t_mx])
                k.ts("dve", thr[:], mx[:, 15:16], -1e29, None, ALU.max, None, [t_mx], [t_thr])
                k.ts("dve", imp3[:], imp2[:], thr[:, 0:1], 1.0, ALU.is_ge, ALU.subtract, [t_imp2, t_thr], [t_imp3])
                k.ts("dve", nsel[:, 0:257], imp3[:], 30000.0, None, ALU.mult, None, [t_imp3], [t_nsel])
                for jt in range(3):
                    k.tr(ps_t1[1][:, jt, 0:4], nsel[:, jt * 128:(jt + 1) * 128], identb[0:4, 0:4], [t_nsel, t_identb], [t_pt1[1]])
                k.cp("dve", nselT4[:], ps_t1[1][:, 0:3, 0:4].unsqueeze(2).to_broadcast([128, 3, 4, 4]), [t_pt1[1]], [t_nselT])
                def ex_slc(kt):
                    ex = [(Em[:, kt % 64, :], nselT4[:, kt // 64, :, :].rearrange("p a b -> p (a b)"), [t_Em, t_nselT])]
                    if kt == 128:
                        ex.append((identb[:], sbias[:, 1, :], [t_identb, t_sbias]))
                    return ex
                run_tiles(129, lambda kt: KsT[gs, kt * 128:(kt + 1) * 128], ex_slc, lambda kt: V1ss[:, kt, g, :], [t_big[0]], [t_big[1]])
                branch_done(1)
                run_tiles(5, lambda kt: KwT[gs, kt * 128:(kt + 1) * 128],
                          lambda kt: [(identb[:], sbias[:, 2 + kt, :], [t_identb, t_sbias])],
                          lambda kt: V1ws[:, kt, g, :], [t_KwT], [t_V1ws])
                branch_done(2)
                for n in range(3):
                    for r in range(4):
                        k.mm(ps_m[0:64, r * 4:(r + 1) * 4], selg[:, (g * 4 + r) * 3 + n, :], gates_s[:, bc], True, True,
                             [t_selg, t_gates_s], [t_psm])
                    k.tt("pool", tA[:], osb[n][0:64, :], rden[n][0:64, :], ALU.mult, [t_osb[n], t_rden[n]], [t_tA])
                    if n == 0:
                        k.tt("dve", acc[:], tA[:], ps_m[0:64, 0:16], ALU.mult, [t_tA, t_psm], [t_acc])
                    else:
                        k.tt("dve", tB[:], tA[:], ps_m[0:64, 0:16], ALU.mult, [t_tA, t_psm], [t_tB])
                        k.tt("pool", acc[:], acc[:], tB[:], ALU.add, [t_acc, t_tB], [t_acc])
                k.cp("dve", attn_s[:, g * 4:(g + 1) * 4, bc], acc[:].rearrange("p (a b) -> p a b", b=4), [t_acc], [t_attn_s])
    P.barrier()
    with ES() as e5:
        wo_a = k.sb("wo_a", [64, 8, 1024], BF16, e5); wo_s = k.sb("wo_s", [128, 4, 1024], BF16, e5); t_wo = T()
        k.load(wo_a[:], w_out_d[0:512, :].rearrange("(h d) n -> d h n", d=64), [t_wo], eng="pool")
        k.load(wo_s[:], w_out_d[512:1024, :].rearrange("(kt p) n -> p kt n", p=128), [t_wo], eng="pool")
        wglu = k.sb("wglu", [128, 4, 512], BF16, e5); t_wglu = T()
        k.load(wglu[:], w_glu_d.rearrange("(kt p) n -> p kt n", p=128), [t_wglu], eng="pool")
        fcols = k.sb("fcols", [128, 16], F32, e5); t_fcols = T()
        k.load(fcols[:], fcols_d, [t_fcols])
        xt2 = k.sb("xt2", [128, 8, 512], F32, e5); t_xt2 = T()
        x1 = k.sb("x1", [128, 8, 512], F32, e5); t_x1 = T()
        at = k.sb("at", [64, 8, 512], F32, e5); t_at = T()
        anb = k.sb("anb", [64, 8, 512], BF16, e5); t_anb = T()
        sqb = k.sb("sqb", [128, 8, 512], BF16, e5); t_sqb = T()
        h2 = k.sb("h2", [128, 8, 512], BF16, e5); t_h2 = T()
        ys = k.sb("ys", [128, 4, 512], F32, e5); t_ys = T()
        gsf = k.sb("gsf", [128, 4, 512], F32, e5); t_gsf = T()
        gsb = k.sb("gsb", [128, 4, 512], BF16, e5); t_gsb = T()
        snb = k.sb("snb", [128, 4, 512], BF16, e5); t_snb = T()
        hm = k.sb("hm", [128, 22, 512], BF16, e5); t_hm = T()
        wg = [k.sb(f"wg{i}", [128, 8, 128], BF16, e5) for i in range(2)]; t_wg = [T(), T()]
        wu = [k.sb(f"wu{i}", [128, 8, 128], BF16, e5) for i in range(2)]; t_wu = [T(), T()]
        wd = [k.sb(f"wd{i}", [128, 22, 128], BF16, e5) for i in range(2)]; t_wd = [T(), T()]
        tmpa = [k.sb(f"tmpa{i}", [128, 512], F32, e5) for i in range(2)]; t_tmpa = [T(), T()]
        rsd = k.sb("rsd", [128, 512], F32, e5); t_rsd = T()
        yo = [k.sb(f"yo{i}", [128, 512], F32, e5) for i in range(2)]; t_yo = [T(), T()]
        ps_a = [k.ps(f"ps_a{i}", [128, 512], F32, e5) for i in range(2)]; t_psa = [T(), T()]
        ps_u = [k.ps(f"ps_u{i}", [128, 512], F32, e5) for i in range(2)]; t_psu = [T(), T()]
        ps_q = k.ps("ps_q", [128, 512], F32, e5); t_psq = T()
        wgv = w_ffn_gate_d.rearrange("(kt p) n -> p kt n", p=128)
        wuv = w_ffn_up_d.rearrange("(kt p) n -> p kt n", p=128)
        wdv = w_ffn_down_d.rearrange("(f p) n -> p f n", p=128)
        yT_v = yT_out.rearrange("(kt p) n -> p kt n", p=128)
        xT_v2 = xT.rearrange("(kt p) n -> p kt n", p=128)
        tgs = x1[:, 0:4, :]; sgs = x1[:, 4:8, :]

        def rms_rstd(sq_tiles, rows, nfeat):
            n = len(sq_tiles)
            for ii, sq_ap in enumerate(sq_tiles):
                k.mm(ps_q[:], onesb[0:rows, :], sq_ap, ii == 0, ii == n - 1, [t_onesb, t_sqb], [t_psq])
            k.act(rsd[:], ps_q[:], AF.Sqrt, [t_psq], [t_rsd], bias=EPS, scale=1.0 / nfeat)
            k.recip(rsd[:], rsd[:], [t_rsd], [t_rsd])

        cnt = 0
        for tt_ in (ftiles if ftiles is not None else range(4)):
            oc0 = tt_ * 512
            c0 = (NW - OWN) + oc0
            k.load(xt2[:], xT_v2[:, :, c0:c0 + 512], [t_xt2])
            k.load(at[:], attn_d[:, :, oc0:oc0 + 512], [t_at], rd=[t_attnd])
            k.load(ys[:], yssm_d[:, :, oc0:oc0 + 512], [t_ys], rd=[t_yssm])
            k.tt("dve", tgs, ys[:], ys[:], ALU.mult, [t_ys], [t_x1])
            k.ts("dve", tgs, tgs, 0.044715, 1.0, ALU.mult, ALU.add, [t_x1], [t_x1])
            k.tt("dve", tgs, tgs, ys[:], ALU.mult, [t_x1, t_ys], [t_x1])
            k.act(sgs, tgs, AF.Sigmoid, [t_x1], [t_x1], scale=1.5957691216)
            k.tt("pool", gsf[:], ys[:], sgs, ALU.mult, [t_ys, t_x1], [t_gsf])
            k.cp("act", gsb[:], gsf[:], [t_gsf], [t_gsb])
            for oc in range(4):
                pa, tpa = ps_a[oc % 2], t_psa[oc % 2]
                for kt in range(4):
                    k.mm(pa[:], wglu[:, kt, oc * 128:(oc + 1) * 128], gsb[:, kt, :], kt == 0, kt == 3, [t_wglu, t_gsb], [tpa])
                ta, tta = tmpa[oc % 2], t_tmpa[oc % 2]
                k.act(ta[:], pa[:], AF.Sigmoid, [tpa, t_fcols], [tta], bias=fcols[:, oc:oc + 1])
                k.tt("pool", ys[:, oc, :], gsf[:, oc, :], ta[:], ALU.mult, [t_gsf, tta], [t_ys])
            k.act(sqb[:, 0:4, :], ys[:], AF.Square, [t_ys], [t_sqb])
            rms_rstd([sqb[:, kt, :] for kt in range(4)], 128, 512)
            for kt in range(4):
                k.stt(snb[:, kt, :], ys[:, kt, :], fcols[:, 4 + kt:5 + kt], rsd[:], ALU.mult, ALU.mult, [t_ys, t_fcols, t_rsd], [t_snb])
            k.act(sqb[0:64, :, :], at[:], AF.Square, [t_at], [t_sqb])
            rms_rstd([sqb[0:64, h, :] for h in range(8)], 64, 512)
            for h in range(8):
                k.stt(anb[:, h, :], at[:, h, :], fcols[0:64, 8 + h:9 + h], rsd[0:64, :], ALU.mult, ALU.mult, [t_at, t_fcols, t_rsd], [t_anb])
            for oc in range(8):
                pa, tpa = ps_a[oc % 2], t_psa[oc % 2]
                for h in range(8):
                    k.mm(pa[:], wo_a[:, h, oc * 128:(oc + 1) * 128], anb[:, h, :], h == 0, False, [t_wo, t_anb], [tpa])
                for kt in range(4):
                    k.mm(pa[:], wo_s[:, kt, oc * 128:(oc + 1) * 128], snb[:, kt, :], False, kt == 3, [t_wo, t_snb], [tpa])
                k.stt(x1[:, oc, :], pa[:], adaT[:, 2 * 8 + oc, 0:1], xt2[:, oc, :], ALU.mult, ALU.add, [tpa, t_ada, t_xt2], [t_x1])
            k.act(sqb[:], x1[:], AF.Square, [t_x1], [t_sqb])
            rms_rstd([sqb[:, kt, :] for kt in range(8)], 128, D)
            for kt in range(8):
                ta, tta = tmpa[kt % 2], t_tmpa[kt % 2]
                k.stt(ta[:], x1[:, kt, :], A2[:, kt, 0:1], rsd[:], ALU.mult, ALU.mult, [t_x1, t_A2, t_rsd], [tta])
                k.act(h2[:, kt, :], ta[:], AF.Identity, [tta, t_ada], [t_h2], bias=adaT[:, 3 * 8 + kt, 0:1])
            for f in range(22):
                wgb, twg = wg[f % 2], t_wg[f % 2]
                wub, twu = wu[f % 2], t_wu[f % 2]
                k.load(wgb[:], wgv[:, :, f * 128:(f + 1) * 128], [twg], eng="pool")
                k.load(wub[:], wuv[:, :, f * 128:(f + 1) * 128], [twu], eng="pool")
                pa, tpa = ps_a[f % 2], t_psa[f % 2]
                pu, tpu = ps_u[f % 2], t_psu[f % 2]
                for kt in range(8):
                    k.mm(pa[:], wgb[:, kt, :], h2[:, kt, :], kt == 0, kt == 7, [twg, t_h2], [tpa])
                for kt in range(8):
                    k.mm(pu[:], wub[:, kt, :], h2[:, kt, :], kt == 0, kt == 7, [twu, t_h2], [tpu])
                ta, tta = tmpa[f % 2], t_tmpa[f % 2]
                k.act(ta[:], pa[:], AF.Silu, [tpa], [tta])
                k.tt("dve", hm[:, f, :], pu[:], ta[:], ALU.mult, [tpu, tta], [t_hm])
            for oc in range(8):
                wdb, twd = wd[oc % 2], t_wd[oc % 2]
                k.load(wdb[:], wdv[:, :, oc * 128:(oc + 1) * 128], [twd], eng="pool")
                pa, tpa = ps_a[oc % 2], t_psa[oc % 2]
                for f in range(22):
                    k.mm(pa[:], wdb[:, f, :], hm[:, f, :], f == 0, f == 21, [twd, t_hm], [tpa])
                yb, tyb = yo[oc % 2], t_yo[oc % 2]
                k.stt(yb[:], pa[:], adaT[:, 5 * 8 + oc, 0:1], x1[:, oc, :], ALU.mult, ALU.add, [tpa, t_ada, t_x1], [tyb])
                k.store(yT_v[:, oc, oc0:oc0 + 512], yb[:], [tyb])

        if do_sample:
            N = 16
            xs2 = k.sb("xs2", [128, 8, N], F32, e5); t_xs2 = T()
            k.load(xs2[:], xsT_d, [t_xs2])
            f1 = k.sb("f1", [128, 8, N], F32, e5); t_f1 = T()
            f2 = k.sb("f2", [128, 8, N], F32, e5); t_f2 = T()
            gss = k.sb("gss", [128, 4, N], F32, e5); t_gss = T()
            gsbs = k.sb("gsbs", [128, 4, N], BF16, e5); t_gsbs = T()
            s2s = k.sb("s2s", [128, 4, N], F32, e5); t_s2s = T()
            sqs2 = k.sb("sqs2", [128, 8, N], BF16, e5); t_sqs2 = T()
            snbs = k.sb("snbs", [128, 4, N], BF16, e5); t_snbs = T()
            anbs = k.sb("anbs", [64, 8, N], BF16, e5); t_anbs = T()
            x1s = k.sb("x1s", [128, 8, N], F32, e5); t_x1s = T()
            h2s = k.sb("h2s", [128, 8, N], BF16, e5); t_h2s = T()
            hms = k.sb("hms", [128, 22, N], BF16, e5); t_hms = T()
            rs_s = k.sb("rs_s", [128, N], F32, e5); t_rs_s = T()
            ysT = k.sb("ysT", [128, 8, N], F32, e5); t_ysT = T()
            tmps = k.sb("tmps", [128, N], F32, e5); t_tmps = T()
            Y4 = yssm_s[:]
            k.tt("dve", f1[:, 0:4, :], Y4, Y4, ALU.mult, [t_yssm_s], [t_f1])
            k.ts("dve", f1[:, 0:4, :], f1[:, 0:4, :], 0.044715, 1.0, ALU.mult, ALU.add, [t_f1], [t_f1])
            k.tt("dve", f1[:, 0:4, :], f1[:, 0:4, :], Y4, ALU.mult, [t_f1, t_yssm_s], [t_f1])
            k.act(f2[:, 0:4, :], f1[:, 0:4, :], AF.Sigmoid, [t_f1], [t_f2], scale=1.5957691216)
            k.tt("dve", gss[:], Y4, f2[:, 0:4, :], ALU.mult, [t_yssm_s, t_f2], [t_gss])
            k.cp("act", gsbs[:], gss[:], [t_gss], [t_gsbs])
            for oc in range(4):
                for kt in range(4):
                    k.mm(ps_q[:, 0:N], wglu[:, kt, oc * 128:(oc + 1) * 128], gsbs[:, kt, :], kt == 0, kt == 3, [t_wglu, t_gsbs], [t_psq])
                k.act(tmps[:], ps_q[:, 0:N], AF.Sigmoid, [t_psq, t_fcols], [t_tmps], bias=fcols[:, oc:oc + 1])
                k.tt("dve", s2s[:, oc, :], gss[:, oc, :], tmps[:], ALU.mult, [t_gss, t_tmps], [t_s2s])

            def rstd_s(sq_list, rows, nfeat):
                n = len(sq_list)
                for ii, sq_ap in enumerate(sq_list):
                    k.mm(ps_q[:, 0:N], onesb[0:rows, :], sq_ap, ii == 0, ii == n - 1, [t_onesb, t_sqs2], [t_psq])
                k.act(rs_s[:], ps_q[:, 0:N], AF.Sqrt, [t_psq], [t_rs_s], bias=EPS, scale=1.0 / nfeat)
                k.recip(rs_s[:], rs_s[:], [t_rs_s], [t_rs_s])

            k.act(sqs2[:, 0:4, :], s2s[:], AF.Square, [t_s2s], [t_sqs2])
            rstd_s([sqs2[:, kt, :] for kt in range(4)], 128, 512)
            for kt in range(4):
                k.stt(snbs[:, kt, :], s2s[:, kt, :], fcols[:, 4 + kt:5 + kt], rs_s[:], ALU.mult, ALU.mult, [t_s2s, t_fcols, t_rs_s], [t_snbs])
            k.act(sqs2[0:64, :, :], attn_s[:], AF.Square, [t_attn_s], [t_sqs2])
            rstd_s([sqs2[0:64, h, :] for h in range(8)], 64, 512)
            for h in range(8):
                k.stt(anbs[:, h, :], attn_s[:, h, :], fcols[0:64, 8 + h:9 + h], rs_s[0:64, :], ALU.mult, ALU.mult,
                      [t_attn_s, t_fcols, t_rs_s], [t_anbs])
            for oc in range(8):
                for h in range(8):
                    k.mm(ps_q[:, 0:N], wo_a[:, h, oc * 128:(oc + 1) * 128], anbs[:, h, :], h == 0, False, [t_wo, t_anbs], [t_psq])
                for kt in range(4):
                    k.mm(ps_q[:, 0:N], wo_s[:, kt, oc * 128:(oc + 1) * 128], snbs[:, kt, :], False, kt == 3, [t_wo, t_snbs], [t_psq])
                k.tt("dve", tmps[:], ps_q[:, 0:N], mods[:, 2, oc, :], ALU.mult, [t_psq, t_mods], [t_tmps])
                k.tt("dve", x1s[:, oc, :], tmps[:], xs2[:, oc, :], ALU.add, [t_tmps, t_xs2], [t_x1s])
            k.act(sqs2[:], x1s[:], AF.Square, [t_x1s], [t_sqs2])
            rstd_s([sqs2[:, kt, :] for kt in range(8)], 128, D)
            k.tt("dve", f1[:], x1s[:], A2s[:], ALU.mult, [t_x1s, t_mods], [t_f1])
            k.tt("dve", f1[:], f1[:], rs_s[:].unsqueeze(1).to_broadcast([128, 8, N]), ALU.mult, [t_f1, t_rs_s], [t_f1])
            k.tt("dve", h2s[:], f1[:], mods[:, 3, :, :], ALU.add, [t_f1, t_mods], [t_h2s])
            for f in range(22):
                wgb, twg = wg[f % 2], t_wg[f % 2]
                wub, twu = wu[f % 2], t_wu[f % 2]
                k.load(wgb[:], wgv[:, :, f * 128:(f + 1) * 128], [twg], eng="pool")
                k.load(wub[:], wuv[:, :, f * 128:(f + 1) * 128], [twu], eng="pool")
                pa, tpa = ps_a[f % 2], t_psa[f % 2]
                pu, tpu = ps_u[f % 2], t_psu[f % 2]
                for kt in range(8):
                    k.mm(pa[:, 0:N], wgb[:, kt, :], h2s[:, kt, :], kt == 0, kt == 7, [twg, t_h2s], [tpa])
                for kt in range(8):
                    k.mm(pu[:, 0:N], wub[:, kt, :], h2s[:, kt, :], kt == 0, kt == 7, [twu, t_h2s], [tpu])
                k.act(tmps[:], pa[:, 0:N], AF.Silu, [tpa], [t_tmps])
                k.tt("dve", hms[:, f, :], pu[:, 0:N], tmps[:], ALU.mult, [tpu, t_tmps], [t_hms])
            for oc in range(8):
                wdb, twd = wd[oc % 2], t_wd[oc % 2]
                k.load(wdb[:], wdv[:, :, oc * 128:(oc + 1) * 128], [twd], eng="pool")
                pa, tpa = ps_a[oc % 2], t_psa[oc % 2]
                for f in range(22):
                    k.mm(pa[:, 0:N], wdb[:, f, :], hms[:, f, :], f == 0, f == 21, [twd, t_hms], [tpa])
                k.tt("dve", tmps[:], pa[:, 0:N], mods[:, 5, oc, :], ALU.mult, [tpa, t_mods], [t_tmps])
                k.tt("dve", ysT[:, oc, :], tmps[:], x1s[:, oc, :], ALU.add, [t_tmps, t_x1s], [t_ysT])
            k.store(ysT_out, ysT[:], [t_ysT])

    P.finalize()
    k.es.close()
    return nc, P


def _rope_tables(pos):
    half = 8
    inv = 500000.0 ** (-(np.arange(half, dtype=np.float64) * 2.0 / 16))
    ang = pos[None, :] * inv[:, None]
    C = np.ones((128, pos.shape[0]), np.float32)
    S = np.zeros((128, pos.shape[0]), np.float32)
    for base in (0, 64):
        C[base:base + 8] = np.cos(ang); C[base + 8:base + 16] = np.cos(ang)
        S[base:base + 8] = np.sin(ang); S[base + 8:base + 16] = np.sin(ang)
    return C, S


def _col(v):
    return np.ascontiguousarray(v.reshape(8, 128).T)


def prepare_inputs(inp):
    x_prompt = inp["x_prompt"]
    w_in = inp["w_in"][0]
    o1, o2, o3 = 512, 512 + 768, 512 + 768 + 24
    qperm = []
    for r in range(4):
        for g in range(2):
            h = g * 4 + r
            qperm += list(range(h * 64, (h + 1) * 64))
    perm = qperm + list(range(o1, o2)) + list(range(o3, NCOL)) + list(range(o2, o3))
    w_in_p = np.ascontiguousarray(w_in[:, perm])
    w_ada = np.ascontiguousarray(inp["w_ada"][0])
    b_adaT = np.ascontiguousarray(inp["b_ada"][0].reshape(48, 128).T)
    gcols = np.stack([_col(inp["norm_mix_g"][0]), _col(inp["norm_ffn_g"][0])], axis=2)
    gq = inp["q_norm_g"][0]
    gk = inp["k_norm_g"][0]
    gqk = np.stack([np.tile(gq, 2)] + [np.tile(gk[n], 2) for n in range(3)], axis=1).astype(np.float32)
    ident = np.eye(128, dtype=np.float32)
    Rt = np.zeros((128, 128), np.float32)
    for base in (0, 64):
        for d in range(8):
            Rt[base + d + 8, base + d] = -1.0
            Rt[base + d, base + d + 8] = 1.0
    def pj(a):
        return np.ascontiguousarray(a.reshape(16, 2, 64).transpose(1, 2, 0).reshape(128, 16))
    a_re = inp["ssm_a_re"][0]; a_im = inp["ssm_a_im"][0]
    logdt = np.repeat(inp["ssm_log_dt"][0][:, None], 64, axis=1)
    ssm_cols = np.stack([pj(a_re), pj(a_im), pj(logdt)], axis=2).astype(np.float32)
    kk1 = np.tile(np.arange(1, 129, dtype=np.float32)[None, :], (128, 1))
    b_re = inp["ssm_b_re"][0]; b_im = inp["ssm_b_im"][0]
    c_re = inp["ssm_c_re"][0]; c_im = inp["ssm_c_im"][0]
    LBre = np.zeros((128, 16, 128), np.float32); LBim = np.zeros_like(LBre)
    LCre = np.zeros_like(LBre); LCim = np.zeros_like(LBre)
    for j in range(16):
        for gs in range(2):
            g = 2 * j + gs
            gl = g % 8
            LBre[gl * 16:(gl + 1) * 16, j, gs * 64:(gs + 1) * 64] = b_re[g].T
            LBim[gl * 16:(gl + 1) * 16, j, gs * 64:(gs + 1) * 64] = b_im[g].T
            LCre[gs * 64:(gs + 1) * 64, j, gl * 16:(gl + 1) * 16] = c_re[g].T
            LCim[gs * 64:(gs + 1) * 64, j, gl * 16:(gl + 1) * 16] = c_im[g].T
    Dcol = np.ascontiguousarray(inp["ssm_d"][0].reshape(4, 128).T)

    bf = ml_dtypes.bfloat16
    NEGB = -30000.0
    cmp_peT = np.zeros((128, 32, 2), np.float32)
    for kv_, nm in enumerate(("cmp_pe_k", "cmp_pe_v")):
        pe = inp[nm][0]
        cmp_peT[0:64, :, kv_] = pe.T
        cmp_peT[64:128, :, kv_] = pe.T
    w2k = inp["cmp_w2_k"][0]
    cmp_w2kp = np.zeros((128, 2, 128), np.float32)
    cmp_w2kp[:, 0, 0:64] = w2k
    cmp_w2kp[:, 1, 64:128] = w2k
    cmp_w2v = np.ascontiguousarray(inp["cmp_w2_v"][0])
    Em = np.zeros((128, 64, 128), np.float32)
    for kt in range(64):
        for half in range(2):
            Em[2 * kt + half, kt, half * 64:(half + 1) * 64] = 1.0
    cs = np.arange(512) * 16
    ss = np.arange(128) * 64
    ovl = np.minimum(cs[:, None] + 32, ss[None, :] + 64) - np.maximum(cs[:, None], ss[None, :])
    ovl = (np.clip(ovl, 0, None) / 32.0).astype(np.float32)
    ovl[511] = 0.0
    ov = np.ascontiguousarray(ovl.reshape(4, 128, 128).transpose(1, 0, 2))
    selg = np.zeros((24, 24, 64), np.float32)
    for i_ in range(24):
        selg[i_, i_, :] = 1.0
    sel65 = np.zeros((65, 128), np.float32); sel65[64] = 1.0
    kk_ = np.arange(128)
    causal = np.where(kk_[:, None] <= kk_[None, :], 0.0, NEGB).astype(np.float32)
    causal4 = np.tile(causal, (1, 4))
    w_out_h = np.ascontiguousarray(inp["w_out"][0]); w_glu_h = np.ascontiguousarray(inp["w_glu"][0])
    wfg = np.ascontiguousarray(inp["w_ffn_gate"][0]); wfu = np.ascontiguousarray(inp["w_ffn_up"][0]); wfd = np.ascontiguousarray(inp["w_ffn_down"][0])
    fcols = np.zeros((128, 16), np.float32)
    fcols[:, 0:4] = inp["b_glu"][0].reshape(4, 128).T
    fcols[:, 4:8] = inp["ssm_out_g"][0].reshape(4, 128).T
    fcols[0:64, 8:16] = inp["attn_out_g"][0].reshape(8, 64).T
    cache_cmp2 = inp["cache_kv_cmp"][0].reshape(NPHYS * 128, 256)
    cache_slc2 = inp["cache_kv_slc"][0].reshape(NPHYS * 128, 256)
    cs2 = np.arange(1024) * 16
    ss2 = np.arange(257) * 64
    ov2 = np.minimum(cs2[:, None] + 32, ss2[None, :] + 64) - np.maximum(cs2[:, None], ss2[None, :])
    ov2 = (np.clip(ov2, 0, None) / 32.0).astype(np.float32)
    ov2[1023] = 0.0
    ovs = np.ascontiguousarray(ov2.reshape(8, 128, 257).transpose(1, 0, 2))
    Ms = np.zeros((4, 2, 257), np.float32)
    Ms[:, 0, :] = 1.0
    for jf in (0, 255, 256):
        Ms[:, 0, jf] = 0.0
        Ms[:, 1, jf] = 1e4
    sbias = np.zeros((128, 7, 16), np.float32)
    sbias[127, 0, :] = NEGB
    qcol = np.tile(np.arange(4), 4)
    for t_ in range(4):
        sbias[t_, 1, :] = np.where(t_ <= qcol, 0.0, NEGB)
        sbias[t_, 6, :] = np.where(t_ <= qcol, 0.0, NEGB)
        sbias[t_, 2, :] = np.where(t_ >= qcol, 0.0, NEGB)
    piota = np.arange(128, dtype=np.float32)[:, None].copy()
    Cs_, Ss_ = _rope_tables(16384.0 + np.arange(4, dtype=np.float64))
    ropes = np.stack([np.tile(Cs_, (1, 4)), np.tile(Ss_, (1, 4))], axis=1).astype(np.float32)
    maps = []
    for c in range(8):
        b, kq = c // 4, c % 4
        start = kq * OWN
        pad = (NW - OWN) - start
        xw = np.zeros((NW, D), np.float32)
        xw[pad:] = x_prompt[b, :start + OWN]
        xT = np.ascontiguousarray(xw.T)
        pos = np.arange(NW, dtype=np.float64) - pad
        C, S = _rope_tables(np.maximum(pos, 0.0))
        tv = np.zeros((128, NW), np.float32); tv[:, pad:] = 1.0
        cT = np.zeros((128, 8, 5), np.float32)
        cT[:, :, 0] = _col(inp["c_prompt"][b])
        for i in range(4):
            cT[:, :, 1 + i] = _col(inp["c_sample"][4 * c + i])
        qi = np.arange(128)
        cmpb = np.zeros((16, 128, 4, 512), np.float32)
        winb = np.zeros((16, 128, 5, 512), np.float32)
        M1 = np.zeros((16, 128, 128), np.float32); M2 = np.zeros((16, 128, 128), np.float32)
        j0 = pad // 64
        jj_ = np.arange(128)
        for i_ in range(16):
            qp = (NW - OWN) + 128 * i_ + qi
            for ct in range(4):
                cc = ct * 128 + np.arange(128)
                valid = (16 * cc[:, None] + 31 <= qp[None, :]) & (16 * cc[:, None] >= pad) & (cc[:, None] <= 510)
                cmpb[i_, :, ct, :] = np.tile(np.where(valid, 0.0, NEGB), (1, 4))
            for w in range(5):
                kp = (44 + i_ + w) * 128 + np.arange(128)
                valid = (kp[:, None] <= qp[None, :]) & (kp[:, None] >= qp[None, :] - 512) & (kp[:, None] >= pad)
                winb[i_, :, w, :] = np.tile(np.where(valid, 0.0, NEGB), (1, 4))
            cur = qp // 64
            fut = (jj_[None, :] > cur[:, None]) | (jj_[None, :] < j0)
            forced = ((jj_[None, :] == j0) | (jj_[None, :] == cur[:, None]) | (jj_[None, :] == cur[:, None] - 1)) & ~fut
            M1[i_] = np.where(fut | forced, 0.0, 1.0)
            M2[i_] = np.where(fut, -1e30, np.where(forced, 1e4, 0.0))
        xsT = np.ascontiguousarray(inp["x_sample"][4 * c:4 * c + 4].reshape(16, 8, 128).transpose(2, 1, 0))
        hs0 = np.zeros((128, 16, 4, 2), np.float32)
        for ri, nm in enumerate(("state_ssm_re", "state_ssm_im")):
            st_ = inp[nm][0, 4 * c:4 * c + 4]
            hs0[:, :, :, ri] = st_.reshape(4, 16, 2, 64).transpose(2, 3, 1, 0).reshape(128, 16, 4)
        ptab = np.ascontiguousarray(np.broadcast_to(inp["page_table"][4 * c:4 * c + 4].astype(np.int32)[None], (128, 4, 128)))
        cwin = np.ascontiguousarray(inp["cache_kv_win"][0, 4 * c:4 * c + 4].reshape(4, 512, 256))
        maps.append(dict(cache_cmp=cache_cmp2, cache_slc=cache_slc2, cache_win=cwin, ovs=ovs.astype(bf), Ms=Ms, sbias=sbias.astype(bf),
                         ptab=ptab, piota=piota, xsT=xsT, ropes=ropes, hs0=hs0, w_out=w_out_h, w_glu=w_glu_h, fcols=fcols, w_ffn_gate=wfg, w_ffn_up=wfu, w_ffn_down=wfd,
                         cmp_w1_k=np.ascontiguousarray(inp["cmp_w1_k"][0]), cmp_w1_v=np.ascontiguousarray(inp["cmp_w1_v"][0]),
                         cmp_peT=cmp_peT, cmp_w2kp=cmp_w2kp, cmp_w2v=cmp_w2v, Em=Em.astype(bf), ov=ov.astype(bf), selg=selg,
                         sel65=sel65, causal4=causal4.astype(bf), cmpb=cmpb.astype(bf), winb=winb.astype(bf), M1=M1, M2=M2,
                         xT=xT, ropeC=C, ropeS=S, tokvalid=tv, w_in=w_in_p, w_ada=w_ada, b_adaT=b_adaT, cT=cT,
                         gcols=gcols.astype(np.float32), gqk=gqk, ident=ident, Rt=Rt, ssm_cols=ssm_cols, kk1=kk1,
                         LBre=LBre, LBim=LBim, LCre=LCre, LCim=LCim, Dcol=Dcol))
    return maps


_CACHE = {}


def run_device(inp, dbg=False):
    if dbg not in _CACHE:
        _CACHE[dbg] = build_program(dbg)
    nc, P = _CACHE[dbg]
    maps = prepare_inputs(inp)
    res = run_bass_kernel_spmd(nc, maps, core_ids=list(range(8)))
    return res.results


def kernel(**inp):
    inp = {k_: np.asarray(v) for k_, v in inp.items()}
    res = run_device(inp)
    B, L = 2, 8192
    y_prompt = np.zeros((B, L, D), np.float32)
    y_sample = np.zeros((32, 4, D), np.float32)
    kv = np.zeros((3, B, L, 2, 2, 64), np.float32)
    kvs = np.zeros((3, 32, 4, 2, 2, 64), np.float32)
    ssm_p = np.zeros((2, 1, B, 32, 64), np.float32)
    ssm_s = np.zeros((2, 1, 32, 32, 64), np.float32)
    for c in range(8):
        b, kq = c // 4, c % 4
        r = res[c]
        y_prompt[b, kq * OWN:(kq + 1) * OWN] = np.asarray(r["yT_out"]).T
        y_sample[4 * c:4 * c + 4] = np.asarray(r["ysT_out"]).transpose(2, 1, 0).reshape(4, 4, D)
        rows = np.asarray(r["kvT_out"]).T.reshape(OWN, 3, 2, 2, 64)
        for n in range(3):
            kv[n, b, kq * OWN:(kq + 1) * OWN] = rows[:, n]
        rows_s = np.asarray(r["kvs_out"]).T.reshape(4, 4, 3, 2, 2, 64)
        for n in range(3):
            kvs[n, 4 * c:4 * c + 4] = rows_s[:, :, n]
        if kq == 3:
            so = np.asarray(r["ssm_out"])
            st = so.reshape(2, 64, 16, 2).transpose(2, 0, 1, 3).reshape(32, 64, 2)
            ssm_p[0, 0, b] = st[:, :, 0]
            ssm_p[1, 0, b] = st[:, :, 1]
        ss = np.asarray(r["ssm_s_out"])
        st = ss.reshape(2, 64, 16, 4, 2).transpose(3, 2, 0, 1, 4).reshape(4, 32, 64, 2)
        ssm_s[0, 0, 4 * c:4 * c + 4] = st[..., 0]
        ssm_s[1, 0, 4 * c:4 * c + 4] = st[..., 1]
    return (y_prompt, y_sample, kv[0][None], kv[1][None], np.ascontiguousarray(kv[2][None][:, :, L - 512:]),
            ssm_p[0], ssm_p[1], kvs[0][None], kvs[1][None], kvs[2][None], ssm_s[0], ssm_s[1])
```

```python
import contextlib
import math
import numpy as np
import ml_dtypes
import concourse.bass as bass
import concourse.mybir as mybir
from concourse.bass_utils import run_bass_kernel_spmd

F32 = mybir.dt.float32
BF16 = mybir.dt.bfloat16
I32 = mybir.dt.int32
AF = mybir.ActivationFunctionType
ALU = mybir.AluOpType
AX = mybir.AxisListType

NW = 8192
OWN = 2048
NT = NW // 512
OWN_T0 = (NW - OWN) // 512
EPS = 1e-6
D = 1024
NCOL = 1816
NPHYS = 5120
MAGIC = 12582912.0
TWO_PI = 2.0 * math.pi


class Tok:
    __slots__ = ("name", "writer", "readers")

    def __init__(self, name=""):
        self.name = name
        self.writer = None
        self.readers = []


class Ins:
    __slots__ = ("eng", "fn", "deps", "signal", "semval", "sem", "is_dma")

    def __init__(self, eng, fn, is_dma):
        self.eng = eng
        self.fn = fn
        self.deps = []
        self.signal = False
        self.semval = None
        self.sem = None
        self.is_dma = is_dma


class Prog:
    ENGS = ("pe", "act", "dve", "pool", "sp")

    def __init__(self, nc, n_dma_sems=32):
        self.nc = nc
        self.streams = {e: [] for e in self.ENGS}
        self.n_dma_sems = n_dma_sems
        self.all = []
        self.pending = {e: [] for e in self.ENGS}
        self.dmas_since = []

    def tok(self, name=""):
        return Tok(name)

    def toks(self, n, name=""):
        return [Tok(f"{name}{i}") for i in range(n)]

    def _add(self, eng, fn, rd, wr, is_dma):
        ins = Ins(eng, fn, is_dma)
        deps = set()
        for t in rd:
            if t.writer is not None:
                deps.add(t.writer)
        for t in wr:
            if t.writer is not None:
                deps.add(t.writer)
            for r in t.readers:
                deps.add(r)
        if self.pending[eng]:
            deps.update(self.pending[eng])
            self.pending[eng] = []
        ins.deps = list(deps)
        if is_dma:
            self.dmas_since.append(ins)
        for t in wr:
            t.writer = ins
            t.readers = []
        for t in rd:
            t.readers.append(ins)
        self.all.append(ins)
        self.streams[eng].append(ins)
        return ins

    def barrier(self):
        lasts = []
        for e in self.ENGS:
            for ins in reversed(self.streams[e]):
                if not ins.is_dma:
                    lasts.append(ins)
                    break
        lasts += self.dmas_since
        self.dmas_since = []
        for e in self.ENGS:
            self.pending[e] = list(self.pending[e]) + lasts

    def op(self, eng, fn, rd=(), wr=()):
        return self._add(eng, fn, rd, wr, False)

    def dma(self, eng, fn, rd=(), wr=()):
        return self._add(eng, fn, rd, wr, True)

    def finalize(self, final_wait_eng="sp"):
        nc = self.nc
        engobj = {"pe": nc.tensor, "act": nc.scalar, "dve": nc.vector, "pool": nc.gpsimd, "sp": nc.sync}
        for ins in self.all:
            for d in ins.deps:
                if d.eng == "pe" and ins.eng == "pe" and not d.is_dma and not ins.is_dma:
                    continue
                d.signal = True
            if ins.is_dma:
                ins.signal = True
        with contextlib.ExitStack() as es:
            csem = {e: es.enter_context(nc.semaphore(f"s_{e}")) for e in self.ENGS}
            qengs = sorted({ins.eng for ins in self.all if ins.is_dma})
            nper = {q: (self.n_dma_sems if q == "sp" else 8) for q in qengs}
            dsems = {q: [es.enter_context(nc.semaphore(f"d_{q}_{i}")) for i in range(nper[q])] for q in qengs}
            ccount = {e: 0 for e in self.ENGS}
            dcount = {q: [0] * nper[q] for q in qengs}
            dlast = {q: [None] * nper[q] for q in qengs}
            dnext = {q: 0 for q in qengs}
            prev_on_sem = {}
            for ins in self.all:
                if ins.is_dma:
                    q = ins.eng
                    k = dnext[q]
                    dnext[q] = (k + 1) % nper[q]
                    dcount[q][k] += 16
                    ins.sem = dsems[q][k]
                    ins.semval = dcount[q][k]
                    if dlast[q][k] is not None:
                        prev_on_sem[ins] = dlast[q][k]
                    dlast[q][k] = ins
                elif ins.signal:
                    ccount[ins.eng] += 1
                    ins.sem = csem[ins.eng]
                    ins.semval = ccount[ins.eng]
            for e in self.ENGS:
                eo = engobj[e]
                waited = {}
                for ins in self.streams[e]:
                    deps = list(ins.deps)
                    if ins in prev_on_sem:
                        deps.append(prev_on_sem[ins])
                    need = {}
                    for d in deps:
                        if d.eng == "pe" and e == "pe" and not d.is_dma and not ins.is_dma:
                            continue
                        key = id(d.sem)
                        if waited.get(key, 0) >= d.semval:
                            continue
                        if key not in need or need[key][1] < d.semval:
                            need[key] = (d.sem, d.semval)
                    for key, (sem, val) in need.items():
                        eo.wait_ge(sem, val)
                        waited[key] = val
                    bi = ins.fn(eo)
                    if ins.signal:
                        bi.then_inc(ins.sem, 16 if ins.is_dma else 1)
            eo = engobj[final_wait_eng]
            for q in qengs:
                for k in range(nper[q]):
                    if dcount[q][k] > 0:
                        eo.wait_ge(dsems[q][k], dcount[q][k])
            self.stats = dict(n_ins=len(self.all), counts=dict(ccount))


class K:
    def __init__(self, nc):
        self.nc = nc
        self.P = Prog(nc)
        self.es = contextlib.ExitStack()

    def dram_in(self, name, shape, dt=F32):
        return self.nc.dram_tensor(name, list(shape), dt, kind="ExternalInput").ap()

    def dram_out(self, name, shape, dt=F32):
        return self.nc.dram_tensor(name, list(shape), dt, kind="ExternalOutput").ap()

    def sb(self, name, shape, dt, es=None):
        return (es or self.es).enter_context(self.nc.sbuf_tensor("s_" + name, list(shape), dt))

    def ps(self, name, shape, dt=F32, es=None):
        return (es or self.es).enter_context(self.nc.psum_tensor("p_" + name, list(shape), dt))

    def scratch(self, name, shape, dt):
        return self.nc.dram_tensor(name, list(shape), dt).ap()

    def mm(self, out, lhsT, rhs, start, stop, rd, wr):
        self.P.op("pe", lambda q: q.matmul(out, lhsT=lhsT, rhs=rhs, start=start, stop=stop), rd, wr)

    def tr(self, out, in_, ident, rd, wr):
        self.P.op("pe", lambda q: q.transpose(out, in_, ident), rd, wr)

    def act(self, out, in_, func, rd, wr, bias=None, scale=None):
        kw = {}
        if bias is not None:
            kw["bias"] = bias
        if scale is not None:
            kw["scale"] = scale
        self.P.op("act", lambda q: q.activation(out=out, in_=in_, func=func, **kw), rd, wr)

    def tt(self, eng, out, in0, in1, op, rd, wr):
        self.P.op(eng, lambda q: q.tensor_tensor(out=out, in0=in0, in1=in1, op=op), rd, wr)

    def ts(self, eng, out, in0, s1, s2, op0, op1, rd, wr):
        if s2 is None:
            self.P.op(eng, lambda q: q.tensor_scalar(out=out, in0=in0, scalar1=s1, scalar2=None, op0=op0), rd, wr)
        else:
            self.P.op(eng, lambda q: q.tensor_scalar(out=out, in0=in0, scalar1=s1, scalar2=s2, op0=op0, op1=op1), rd, wr)

    def stt(self, out, in0, scalar, in1, op0, op1, rd, wr):
        self.P.op("dve", lambda q: q.scalar_tensor_tensor(out=out, in0=in0, scalar=scalar, in1=in1, op0=op0, op1=op1), rd, wr)

    def cp(self, eng, out, in_, rd, wr):
        if eng == "act":
            self.P.op("act", lambda q: q.activation(out=out, in_=in_, func=AF.Identity), rd, wr)
        else:
            self.P.op(eng, lambda q: q.tensor_copy(out=out, in_=in_), rd, wr)

    def recip(self, out, in_, rd, wr):
        self.P.op("dve", lambda q: q.reciprocal(out=out, in_=in_), rd, wr)

    def memset(self, eng, ap, val, wr):
        self.P.op(eng, lambda q: q.memset(ap, val), (), wr)

    def load(self, out, in_, wr, eng="sp", rd=()):
        self.P.dma(eng, lambda q: q.dma_start(out=out, in_=in_), rd, wr)

    def store(self, out, in_, rd, eng="sp"):
        self.P.dma(eng, lambda q: q.dma_start(out=out, in_=in_), rd, ())


def build_program(dbg=False, tlist=None, stop=9, tiles_override=None, noscr=False, chunks=None, ftiles=None, do_sample=True, skip23=False, nsb=4, no_prompt=False):
    tlist = list(range(NT)) if tlist is None else tlist
    nc = bass.Bass("TRN2", target_bir_lowering=False)
    k = K(nc)
    P = k.P
    T = P.tok
    ES = contextlib.ExitStack

    xT = k.dram_in("xT", [D, NW])
    ropeC = k.dram_in("ropeC", [128, NW])
    ropeS = k.dram_in("ropeS", [128, NW])
    tokvalid = k.dram_in("tokvalid", [128, NW])
    w_in_d = k.dram_in("w_in", [D, NCOL])
    w_ada_d = k.dram_in("w_ada", [D, 6 * D])
    b_adaT_d = k.dram_in("b_adaT", [128, 48])
    cT_d = k.dram_in("cT", [128, 8, 5])
    gcols_d = k.dram_in("gcols", [128, 8, 2])
    gqk_d = k.dram_in("gqk", [128, 4])
    ident_d = k.dram_in("ident", [128, 128])
    Rt_d = k.dram_in("Rt", [128, 128])
    ssmc_d = k.dram_in("ssm_cols", [128, 16, 3])
    kk1_d = k.dram_in("kk1", [128, 128])
    LB_d = [k.dram_in(n, [128, 16, 128]) for n in ("LBre", "LBim", "LCre", "LCim")]
    Dcol_d = k.dram_in("Dcol", [128, 4])
    cmp_w1_d = [k.dram_in(n, [2048, 128]) for n in ("cmp_w1_k", "cmp_w1_v")]
    cmp_peT_d = k.dram_in("cmp_peT", [128, 32, 2])
    cmp_w2kp_d = k.dram_in("cmp_w2kp", [128, 2, 128])
    cmp_w2v_d = k.dram_in("cmp_w2v", [128, 64])
    Em_d = k.dram_in("Em", [128, 64, 128], BF16)
    ov_d = k.dram_in("ov", [128, 4, 128], BF16)
    selg_d = k.dram_in("selg", [24, 24, 64])
    sel65_d = k.dram_in("sel65", [65, 128])
    causal4_d = k.dram_in("causal4", [128, 512], BF16)
    cmpb_d = k.dram_in("cmpb", [16, 128, 4, 512], BF16)
    winb_d = k.dram_in("winb", [16, 128, 5, 512], BF16)
    M1_d = k.dram_in("M1", [16, 128, 128])
    M2_d = k.dram_in("M2", [16, 128, 128])
    attn_out = k.dram_out("attn_out", [64, 8, OWN]) if dbg else None
    w_out_d = k.dram_in("w_out", [D, D])
    w_glu_d = k.dram_in("w_glu", [512, 512])
    fcols_d = k.dram_in("fcols", [128, 16])
    w_ffn_gate_d = k.dram_in("w_ffn_gate", [D, 2816])
    w_ffn_up_d = k.dram_in("w_ffn_up", [D, 2816])
    w_ffn_down_d = k.dram_in("w_ffn_down", [2816, D])
    yT_out = k.dram_out("yT_out", [D, OWN])
    xsT_d = k.dram_in("xsT", [128, 8, 16])
    cache_cmp_v = k.dram_in("cache_cmp", [NPHYS * 128, 256]) if do_sample else None
    cache_slc_v = k.dram_in("cache_slc", [NPHYS * 128, 256]) if do_sample else None
    cache_win_d = k.dram_in("cache_win", [4, 512, 256])
    ovs_d = k.dram_in("ovs", [128, 8, 257], BF16)
    Ms_d = k.dram_in("Ms", [4, 2, 257])
    sbias_d = k.dram_in("sbias", [128, 7, 16], BF16)
    ptab_d = k.dram_in("ptab", [128, 4, 128], I32)
    piota_d = k.dram_in("piota", [128, 1])
    ropes_d = k.dram_in("ropes", [128, 2, 16])
    hs0_d = k.dram_in("hs0", [128, 16, 4, 2])
    kvs_out = k.dram_out("kvs_out", [768, 16])
    ysT_out = k.dram_out("ysT_out", [128, 8, 16])
    ssm_s_out = k.dram_out("ssm_s_out", [128, 16, 4, 2])

    kvT_out = k.dram_out("kvT_out", [768, OWN])
    ssm_out = k.dram_out("ssm_out", [128, 16, 2])

    uT_bf_d = k.scratch("uT_bf_d", [128, 4, NW], BF16); t_uTd = T()
    uT_f_d = k.scratch("uT_f_d", [128, 4, OWN], F32); t_uTfd = T()
    KcmpT_d = k.scratch("KcmpT_d", [128, NW], BF16); t_Kcd = T()
    VcmpT_d = k.scratch("VcmpT_d", [128, NW], BF16); t_Vcd = T()

    ident = k.sb("ident", [128, 128], F32); t_ident = T()
    identb = k.sb("identb", [128, 128], BF16); t_identb = T()
    Rt = k.sb("Rt", [128, 128], F32); t_Rt = T()
    onesb = k.sb("onesb", [128, 128], BF16); t_onesb = T()
    blkones = k.sb("blkones", [128, 128], BF16); t_blk = T()
    k.load(ident[:], ident_d, [t_ident])
    k.load(Rt[:], Rt_d, [t_Rt])
    k.cp("dve", identb[:], ident[:], [t_ident], [t_identb])
    k.memset("pool", onesb[:], 1.0, [t_onesb])
    k.memset("pool", blkones[:], 0.0, [t_blk])
    k.memset("pool", blkones[0:64, 0:64], 1.0, [t_blk])
    k.memset("pool", blkones[64:128, 64:128], 1.0, [t_blk])

    adaT = k.sb("adaT", [128, 48, 5], F32); t_ada = T()
    gcols = k.sb("gcols", [128, 8, 2], F32); t_gcols = T()
    gqk = k.sb("gqk", [128, 4], F32); t_gqk = T()
    A1 = k.sb("A1", [128, 8, 5], F32); t_A1 = T()
    A2 = k.sb("A2", [128, 8, 5], F32); t_A2 = T()
    k.load(gcols[:], gcols_d, [t_gcols])
    k.load(gqk[:], gqk_d, [t_gqk])

    mods = k.sb("mods", [128, 6, 8, 16], F32); A1s = k.sb("A1s", [128, 8, 16], F32); A2s = k.sb("A2s", [128, 8, 16], F32); t_mods = T()
    gates_s = k.sb("gates_s", [24, 16], F32); t_gates_s = T()
    us_f = k.sb("us_f", [128, 4, 16], F32); us_b = k.sb("us_b", [128, 4, 16], BF16); t_us = T()
    QTs = k.sb("QTs", [128, 4, 4, 4], BF16); t_QTs = T()
    kvnew = k.sb("kvnew", [128, 4, 16], BF16); t_kvnew = T()
    yssm_s = k.sb("yssm_s", [128, 4, 16], F32); t_yssm_s = T()
    attn_s = k.sb("attn_s", [64, 8, 16], F32); t_attn_s = T()
    k.memset("pool", attn_s[:], 0.0, [t_attn_s])
    eA = ES()
    KslcT = k.sb("KslcT", [128, NW], BF16, eA); t_KslcT = T()
    KwinT = k.sb("KwinT", [128, 2560], BF16, eA); t_KwinT = T()
    QT = k.sb("QT", [128, 16, 4, 128], BF16, eA); t_QT = T()
    gates = k.sb("gates", [24, OWN], F32, eA); t_gates = T()
    yssm_d = k.scratch("yssm_d", [128, 4, OWN], F32); t_yssm = T()
    attn_d = k.scratch("attn_d", [64, 8, OWN], F32); t_attnd = T()
    V1s = k.sb("V1s", [128, 64, 2, 65], BF16, eA); t_V1s = T()
    V1w = k.sb("V1w", [128, 20, 2, 65], BF16, eA); t_V1w = T()
    KcT = k.sb("KcT", [128, 512], BF16, eA); t_KcT = T()
    V1c = k.sb("V1c", [128, 4, 2, 65], BF16, eA); t_V1c = T()
    k.memset("pool", V1s[:], 1.0, [t_V1s])
    k.memset("pool", V1w[:], 1.0, [t_V1w])
    k.memset("pool", V1c[:], 1.0, [t_V1c])
    hst = k.sb("hst", [128, 16, 4], F32, eA); t_hst = T()
    k.memset("dve", hst[:], 0.0, [t_hst])

    with ES() as e0:
        cT = k.sb("cT", [128, 8, 5], F32, e0); t_cT = T()
        scb = k.sb("scb", [128, 8, 5], BF16, e0); t_scb = T()
        b_adaT = k.sb("b_adaT", [128, 48], F32, e0); t_bada = T()
        k.load(cT[:], cT_d, [t_cT])
        k.load(b_adaT[:], b_adaT_d, [t_bada])
        k.act(scb[:], cT[:], AF.Silu, [t_cT], [t_scb])
        ps_ada = k.ps("ps_ada", [128, 48, 5], F32, e0); t_psada = T()
        wada = [k.sb(f"wada{i}", [128, 8, 1024], BF16, e0) for i in range(2)]
        t_wada = [T(), T()]
        w_ada_v = w_ada_d.rearrange("(kt p) n -> p kt n", p=128)
        for i in range(6):
            wb = wada[i % 2]; tw = t_wada[i % 2]
            k.load(wb[:], w_ada_v[:, :, i * 1024:(i + 1) * 1024], [tw], eng="pool")
            for ko in range(8):
                j = i * 8 + ko
                for kt in range(8):
                    k.mm(ps_ada[:, j, :], wb[:, kt, ko * 128:(ko + 1) * 128], scb[:, kt, :], kt == 0, kt == 7,
                         [tw, t_scb], [t_psada])
        k.tt("dve", adaT[:], ps_ada[:], b_adaT[:].unsqueeze(2).to_broadcast([128, 48, 5]), ALU.add,
             [t_psada, t_bada], [t_ada])
        for (A, tA, si, gi) in ((A1, t_A1, 1, 0), (A2, t_A2, 4, 1)):
            k.ts("dve", A[:], adaT[:, si * 8:(si + 1) * 8, :], 1.0, None, ALU.add, None, [t_ada], [tA])
            k.tt("dve", A[:], A[:], gcols[:, :, gi:gi + 1].to_broadcast([128, 8, 5]), ALU.mult, [tA, t_gcols], [tA])

    for i6 in range(6):
        k.cp("dve", mods[:, i6, :, :].rearrange("p k (b q) -> p k b q", q=4),
             adaT[:, i6 * 8:(i6 + 1) * 8, 1:5].unsqueeze(3).to_broadcast([128, 8, 4, 4]), [t_ada], [t_mods])
    k.cp("dve", A1s[:].rearrange("p k (b q) -> p k b q", q=4), A1[:, :, 1:5].unsqueeze(3).to_broadcast([128, 8, 4, 4]), [t_A1], [t_mods])
    k.cp("dve", A2s[:].rearrange("p k (b q) -> p k b q", q=4), A2[:, :, 1:5].unsqueeze(3).to_broadcast([128, 8, 4, 4]), [t_A2], [t_mods])
    P.barrier()
    with ES() as e1:
        w_in = k.sb("w_in", [128, 8, NCOL], BF16, e1); t_win = T()
        for kt_ in range(8):
            k.load(w_in[:, kt_, :], w_in_d[kt_ * 128:(kt_ + 1) * 128, :], [t_win], eng="pool")
        xt = k.sb("xt", [128, 8, 512], F32, e1); t_xt = T()
        hb = k.sb("hb", [128, 8, 512], BF16, e1); t_hb = T()
        tmpf = [k.sb(f"tmpf{i}", [128, 512], F32, e1) for i in range(2)]; t_tmpf = [T(), T()]
        rstd = k.sb("rstd", [128, 512], F32, e1); t_rstd = T()
        cst = k.sb("cst", [128, 512], F32, e1); t_cst = T()
        sst = k.sb("sst", [128, 512], F32, e1); t_sst = T()
        tvl = k.sb("tvl", [128, 512], F32, e1); t_tvl = T()
        sqz = k.sb("sqz", [128, 512], BF16, e1); t_sqz = T()
        rs2 = k.sb("rs2", [128, 512], F32, e1); t_rs2 = T()
        zn = k.sb("zn", [128, 512], F32, e1); t_zn = T()
        r1 = k.sb("r1", [128, 512], F32, e1); t_r1 = T()
        r2 = k.sb("r2", [128, 512], F32, e1); t_r2 = T()
        zraw = k.sb("zraw", [128, 512], F32, e1); t_zraw = T()
        zo = [k.sb(f"zo{i}", [128, 512], F32, e1) for i in range(2)]; t_zo = [T(), T()]
        zb = [k.sb(f"zb{i}", [128, 512], BF16, e1) for i in range(2)]; t_zb = [T(), T()]
        ps_ss = k.ps("ps_ss", [128, 512], F32, e1); t_pss = T()
        ps_z = [k.ps(f"ps_z{i}", [128, 512], F32, e1) for i in range(2)]; t_psz = [T(), T()]
        ps_n = k.ps("ps_n", [128, 512], F32, e1); t_psn = T()
        ps_r = k.ps("ps_r", [128, 512], F32, e1); t_psr = T()
        ps_tb = k.ps("ps_tb", [128, 4, 128], BF16, e1); t_pstb = T()
        xT_v = xT.rearrange("(kt p) n -> p kt n", p=128)
        zc = 0
        for t in (tlist if stop >= 1 else []):
            own = t >= OWN_T0
            c0 = t * 512
            oc0 = (t - OWN_T0) * 512
            k.load(xt[:], xT_v[:, :, c0:c0 + 512], [t_xt])
            k.load(cst[:], ropeC[:, c0:c0 + 512], [t_cst])
            k.load(sst[:], ropeS[:, c0:c0 + 512], [t_sst])
            if not own:
                k.load(tvl[:], tokvalid[:, c0:c0 + 512], [t_tvl])
            k.act(hb[:], xt[:], AF.Square, [t_xt], [t_hb])
            for kt in range(8):
                k.mm(ps_ss[:], onesb[:], hb[:, kt, :], kt == 0, kt == 7, [t_onesb, t_hb], [t_pss])
            k.act(rstd[:], ps_ss[:], AF.Sqrt, [t_pss], [t_rstd], bias=EPS, scale=1.0 / D)
            k.recip(rstd[:], rstd[:], [t_rstd], [t_rstd])
            for kt in range(8):
                tf = tmpf[kt % 2]; ttf = t_tmpf[kt % 2]
                k.stt(tf[:], xt[:, kt, :], A1[:, kt, 0:1], rstd[:], ALU.mult, ALU.mult, [t_xt, t_A1, t_rstd], [ttf])
                k.act(hb[:, kt, :], tf[:], AF.Identity, [ttf, t_ada], [t_hb], bias=adaT[:, kt, 0:1])
            tiles = list(range(4, 14)) + ([0, 1, 2, 3, 14] if own else [])
            if tiles_override is not None:
                tiles = tiles_override
            for m in tiles:
                pz = ps_z[zc % 2]; tpz = t_psz[zc % 2]
                zz = zo[zc % 2]; tzz = t_zo[zc % 2]
                zzb = zb[zc % 2]; tzzb = t_zb[zc % 2]
                zc += 1
                ncols = 128 if m < 14 else 24
                col0 = m * 128
                for kt in range(8):
                    k.mm(pz[0:ncols, :], w_in[:, kt, col0:col0 + ncols], hb[:, kt, :], kt == 0, kt == 7,
                         [t_win, t_hb], [tpz])
                if m == 14:
                    k.act(gates[:, oc0:oc0 + 512], pz[0:24, :], AF.Sigmoid, [tpz], [t_gates])
                    continue
                if 10 <= m < 14:
                    u_i = m - 10
                    if own:
                        k.cp("act", zz[:], pz[:], [tpz], [tzz])
                        if not noscr:
                            k.store(uT_f_d[:, u_i, oc0:oc0 + 512], zz[:], [tzz])
                        k.cp("pool", zzb[:], zz[:], [tzz], [tzzb])
                    else:
                        k.tt("dve", zzb[:], pz[:], tvl[:], ALU.mult, [tpz, t_tvl], [tzzb])
                    if not noscr:
                        P.dma("sp", lambda q, o=uT_bf_d[:, u_i, c0:c0 + 512], i_=zzb[:]: q.dma_start(out=o, in_=i_), [tzzb], [t_uTd])
                    continue
                is_k = (m < 4) or (m in (4, 6, 8))
                if is_k:
                    gi = 0 if m < 4 else 1 + (m - 4) // 2
                    k.cp("act", zraw[:], pz[:], [tpz], [t_zraw])
                    k.act(sqz[:], zraw[:], AF.Square, [t_zraw], [t_sqz])
                    k.mm(ps_n[:], blkones[:], sqz[:], True, True, [t_blk, t_sqz], [t_psn])
                    k.act(rs2[:], ps_n[:], AF.Sqrt, [t_psn], [t_rs2], bias=EPS, scale=1.0 / 64)
                    k.recip(rs2[:], rs2[:], [t_rs2], [t_rs2])
                    k.stt(zn[:], zraw[:], gqk[:, gi:gi + 1], rs2[:], ALU.mult, ALU.mult, [t_zraw, t_gqk, t_rs2], [t_zn])
                    k.mm(ps_r[:], Rt[:], zn[:], True, True, [t_Rt, t_zn], [t_psr])
                    k.tt("pool", r1[:], zn[:], cst[:], ALU.mult, [t_zn, t_cst], [t_r1])
                    k.tt("dve", r2[:], ps_r[:], sst[:], ALU.mult, [t_psr, t_sst], [t_r2])
                    k.tt("pool", zz[:], r1[:], r2[:], ALU.add, [t_r1, t_r2], [tzz])
                else:
                    k.cp("act", zz[:], pz[:], [tpz], [tzz])
                if m < 4:
                    k.cp("act", QT[:, oc0 // 128:oc0 // 128 + 4, m, :], zz[:].rearrange("p (a b) -> p a b", b=128), [tzz], [t_QT])
                    continue
                if own:
                    k.store(kvT_out[(m - 4) * 128:(m - 3) * 128, oc0:oc0 + 512], zz[:], [tzz])
                if m == 6:
                    k.cp("act", KslcT[:, c0:c0 + 512], zz[:], [tzz], [t_KslcT])
                elif m == 8 and c0 + 512 > NW - 2560:
                    w0 = c0 - (NW - 2560)
                    k.cp("act", KwinT[:, w0:w0 + 512], zz[:], [tzz], [t_KwinT])
                elif m in (4, 5):
                    k.cp("act", zzb[:], zz[:], [tzz], [tzzb])
                    dd, td = (KcmpT_d, t_Kcd) if m == 4 else (VcmpT_d, t_Vcd)
                    P.dma("sp", lambda q, o=dd[:, c0:c0 + 512], i_=zzb[:]: q.dma_start(out=o, in_=i_), [tzzb], [td])
                if m == 7 or (m == 9 and c0 + 512 > NW - 2560):
                    k.cp("act", zzb[:], zz[:], [tzz], [tzzb])
                    for s4 in range(4):
                        k.tr(ps_tb[:, s4, :], zzb[:, s4 * 128:(s4 + 1) * 128], identb[:], [tzzb, t_identb], [t_pstb])
                    if m == 7:
                        dstv, tdv, kt0 = V1s, t_V1s, c0 // 128
                    else:
                        dstv, tdv, kt0 = V1w, t_V1w, (c0 - (NW - 2560)) // 128
                    k.cp("dve", dstv[:, kt0:kt0 + 4, :, 0:64], ps_tb[:].rearrange("p a (g d) -> p a g d", g=2), [t_pstb], [tdv])

        if do_sample:
            xs = k.sb("xs", [128, 8, 16], F32, e1); t_xs = T()
            sqs = k.sb("sqs", [128, 8, 16], BF16, e1); t_sqs = T()
            hsb = k.sb("hsb", [128, 8, 16], BF16, e1); t_hsb = T()
            hsf = k.sb("hsf", [128, 8, 16], F32, e1); t_hsf = T()
            rss = k.sb("rss", [128, 16], F32, e1); t_rss = T()
            rcs = k.sb("rcs", [128, 2, 16], F32, e1); t_rcs = T()
            k.load(xs[:], xsT_d, [t_xs])
            k.load(rcs[:], ropes_d, [t_rcs])
            k.act(sqs[:], xs[:], AF.Square, [t_xs], [t_sqs])
            for kt in range(8):
                k.mm(ps_ss[:, 0:16], onesb[:], sqs[:, kt, :], kt == 0, kt == 7, [t_onesb, t_sqs], [t_pss])
            k.act(rss[:], ps_ss[:, 0:16], AF.Sqrt, [t_pss], [t_rss], bias=EPS, scale=1.0 / D)
            k.recip(rss[:], rss[:], [t_rss], [t_rss])
            k.tt("dve", hsf[:], xs[:], A1s[:], ALU.mult, [t_xs, t_mods], [t_hsf])
            k.tt("dve", hsf[:], hsf[:], rss[:].unsqueeze(1).to_broadcast([128, 8, 16]), ALU.mult, [t_hsf, t_rss], [t_hsf])
            k.tt("dve", hsb[:], hsf[:], mods[:, 0, :, :], ALU.add, [t_hsf, t_mods], [t_hsb])
            for m in range(15):
                pz = ps_z[zc % 2]; tpz = t_psz[zc % 2]
                zz = zo[zc % 2]; tzz = t_zo[zc % 2]
                zc += 1
                ncols = 128 if m < 14 else 24
                col0 = m * 128
                for kt in range(8):
                    k.mm(pz[0:ncols, 0:16], w_in[:, kt, col0:col0 + ncols], hsb[:, kt, :], kt == 0, kt == 7, [t_win, t_hsb], [tpz])
                if m == 14:
                    k.act(gates_s[:], pz[0:24, 0:16], AF.Sigmoid, [tpz], [t_gates_s])
                    continue
                if 10 <= m < 14:
                    k.cp("act", us_f[:, m - 10, :], pz[:, 0:16], [tpz], [t_us])
                    k.cp("act", us_b[:, m - 10, :], us_f[:, m - 10, :], [t_us], [t_us])
                    continue
                is_k = (m < 4) or (m in (4, 6, 8))
                Z = zz[:, 0:16]
                if is_k:
                    gi = 0 if m < 4 else 1 + (m - 4) // 2
                    k.cp("act", zraw[:, 0:16], pz[:, 0:16], [tpz], [t_zraw])
                    k.act(sqz[:, 0:16], zraw[:, 0:16], AF.Square, [t_zraw], [t_sqz])
                    k.mm(ps_n[:, 0:16], blkones[:], sqz[:, 0:16], True, True, [t_blk, t_sqz], [t_psn])
                    k.act(rs2[:, 0:16], ps_n[:, 0:16], AF.Sqrt, [t_psn], [t_rs2], bias=EPS, scale=1.0 / 64)
                    k.recip(rs2[:, 0:16], rs2[:, 0:16], [t_rs2], [t_rs2])
                    k.stt(zn[:, 0:16], zraw[:, 0:16], gqk[:, gi:gi + 1], rs2[:, 0:16], ALU.mult, ALU.mult, [t_zraw, t_gqk, t_rs2], [t_zn])
                    k.mm(ps_r[:, 0:16], Rt[:], zn[:, 0:16], True, True, [t_Rt, t_zn], [t_psr])
                    k.tt("pool", r1[:, 0:16], zn[:, 0:16], rcs[:, 0, :], ALU.mult, [t_zn, t_rcs], [t_r1])
                    k.tt("dve", r2[:, 0:16], ps_r[:, 0:16], rcs[:, 1, :], ALU.mult, [t_psr, t_rcs], [t_r2])
                    k.tt("pool", Z, r1[:, 0:16], r2[:, 0:16], ALU.add, [t_r1, t_r2], [tzz])
                else:
                    k.cp("act", Z, pz[:, 0:16], [tpz], [tzz])
                if m < 4:
                    k.cp("act", QTs[:, :, m, :], Z.rearrange("p (b q) -> p b q", q=4), [tzz], [t_QTs])
                    continue
                k.store(kvs_out[(m - 4) * 128:(m - 3) * 128, :], Z, [tzz])
                if m in (6, 7, 8, 9):
                    k.cp("act", kvnew[:, m - 6, :], Z, [tzz], [t_kvnew])
    P.barrier()
    with ES() as e2:
        ssmc = k.sb("ssmc", [128, 16, 3], F32, e2); t_ssmc = T()
        kk1 = k.sb("kk1", [128, 128], F32, e2); t_kk1 = T()
        k.load(ssmc[:], ssmc_d, [t_ssmc])
        k.load(kk1[:], kk1_d, [t_kk1])
        LB = []
        t_LB = T()
        for i, dd in enumerate(LB_d):
            tl = k.sb(f"LB{i}", [128, 16, 128], BF16, e2)
            k.load(tl[:], dd, [t_LB], eng="pool")
            LB.append(tl)
        LBre, LBim, LCre, LCim = LB
        Dcol = k.sb("Dcol", [128, 4], F32, e2); t_Dcol = T()
        k.load(Dcol[:], Dcol_d, [t_Dcol])
        sm = k.sb("sm", [128, 16, 12], F32, e2); t_sm = T()
        DT, ARD, R_, TH, LBR, LBI, DEN, NRE, FRE, FIM, TMP1, TMP2 = [sm[:, :, i] for i in range(12)]
        a_re = ssmc[:, :, 0]; a_im = ssmc[:, :, 1]; logdt = ssmc[:, :, 2]
        Er = k.sb("Er", [128, 16, 128], F32, e2); Ei = k.sb("Ei", [128, 16, 128], F32, e2)
        Fr = k.sb("Fr", [128, 16, 128], F32, e2); Fi = k.sb("Fi", [128, 16, 128], F32, e2)
        Rm = k.sb("Rm", [128, 16, 128], F32, e2)
        t_tab = T()
        ang = k.sb("ang", [128, 16, 128], F32, e2); t_ang = T()
        ang2 = k.sb("ang2", [128, 16, 128], F32, e2); t_ang2 = T()
        k.act(DT, logdt, AF.Exp, [t_ssmc], [t_sm])
        k.tt("dve", ARD, a_re, DT, ALU.mult, [t_ssmc, t_sm], [t_sm])
        k.act(R_, ARD, AF.Exp, [t_sm], [t_sm])
        k.tt("dve", TH, a_im, DT, ALU.mult, [t_ssmc, t_sm], [t_sm])
        for j in range(16):
            k.ts("dve", ang[:, j, :], kk1[:], sm[:, j, 3:4], None, ALU.mult, None, [t_kk1, t_sm], [t_ang])

        def sin_table(dst, shift):
            src = ang
            ts_ = t_ang
            if shift != 0.0:
                k.ts("dve", ang2[:], ang[:], shift, None, ALU.add, None, [t_ang], [t_ang2])
                src = ang2
                ts_ = t_ang2
            k.ts("dve", dst[:], src[:], 1.0 / TWO_PI, MAGIC, ALU.mult, ALU.add, [ts_], [t_tab])
            k.ts("dve", dst[:], dst[:], MAGIC, None, ALU.subtract, None, [t_tab], [t_tab])
            k.stt(dst[:], dst[:], -TWO_PI, src[:], ALU.mult, ALU.add, [t_tab, ts_], [t_tab])
            k.ts("dve", dst[:], dst[:], 3.1415925, -3.1415925, ALU.min, ALU.max, [t_tab], [t_tab])
            k.act(dst[:], dst[:], AF.Sin, [t_tab], [t_tab])

        sin_table(Ei, 0.0)
        sin_table(Er, math.pi / 2)
        k.tt("dve", LBR, R_, Er[:, :, 0], ALU.mult, [t_sm, t_tab], [t_sm])
        k.tt("dve", LBI, R_, Ei[:, :, 0], ALU.mult, [t_sm, t_tab], [t_sm])
        k.tt("dve", DEN, a_re, a_re, ALU.mult, [t_ssmc], [t_sm])
        k.tt("dve", TMP1, a_im, a_im, ALU.mult, [t_ssmc], [t_sm])
        k.tt("dve", DEN, DEN, TMP1, ALU.add, [t_sm], [t_sm])
        k.recip(DEN, DEN, [t_sm], [t_sm])
        k.ts("dve", NRE, LBR, -1.0, None, ALU.add, None, [t_sm], [t_sm])
        k.tt("dve", TMP1, NRE, a_re, ALU.mult, [t_sm, t_ssmc], [t_sm])
        k.tt("dve", TMP2, LBI, a_im, ALU.mult, [t_sm, t_ssmc], [t_sm])
        k.tt("dve", FRE, TMP1, TMP2, ALU.add, [t_sm], [t_sm])
        k.tt("dve", FRE, FRE, DEN, ALU.mult, [t_sm], [t_sm])
        k.tt("dve", TMP1, LBI, a_re, ALU.mult, [t_sm, t_ssmc], [t_sm])
        k.tt("dve", TMP2, NRE, a_im, ALU.mult, [t_sm, t_ssmc], [t_sm])
        k.tt("dve", FIM, TMP1, TMP2, ALU.subtract, [t_sm], [t_sm])
        k.tt("dve", FIM, FIM, DEN, ALU.mult, [t_sm], [t_sm])
        fre_b = sm[:, :, 8:9].to_broadcast([128, 16, 128])
        fim_b = sm[:, :, 9:10].to_broadcast([128, 16, 128])
        k.tt("dve", Fr[:], Er[:], fre_b, ALU.mult, [t_tab, t_sm], [t_tab])
        k.tt("dve", ang[:], Ei[:], fim_b, ALU.mult, [t_tab, t_sm], [t_ang])
        k.tt("dve", Fr[:], Fr[:], ang[:], ALU.add, [t_tab, t_ang], [t_tab])
        k.tt("dve", Fi[:], Er[:], fim_b, ALU.mult, [t_tab, t_sm], [t_tab])
        k.tt("dve", ang[:], Ei[:], fre_b, ALU.mult, [t_tab, t_sm], [t_ang])
        k.tt("dve", Fi[:], Fi[:], ang[:], ALU.subtract, [t_tab, t_ang], [t_tab])
        k.cp("dve", Rm[:], sm[:, :, 2:3].to_broadcast([128, 16, 128]), [t_sm], [t_tab])
        k.memset("dve", Rm[:, :, 0:1], 0.0, [t_tab])

        ub = k.sb("ub", [128, 4, 512], BF16, e2); t_ub = T()
        uf = k.sb("uf", [128, 4, 512], F32, e2); t_uf = T()
        ga = k.sb("ga", [128, 4, 128], F32, e2); t_ga = T()
        gb = k.sb("gb", [128, 4, 128], F32, e2); t_gb = T()
        gri = k.sb("gri", [128, 4, 128], F32, e2); t_gri = T()
        gii = k.sb("gii", [128, 4, 128], F32, e2); t_gii = T()
        gre = k.sb("gre", [128, 4, 128], F32, e2); t_gre = T()
        gim = k.sb("gim", [128, 4, 128], F32, e2); t_gim = T()
        hre = k.sb("hre", [128, 4, 128], F32, e2); t_hre = T()
        him = k.sb("him", [128, 4, 128], F32, e2); t_him = T()
        hbre = k.sb("hbre", [128, 4, 128], BF16, e2); t_hbre = T()
        hbim = k.sb("hbim", [128, 4, 128], BF16, e2); t_hbim = T()
        ps_bre = k.ps("ps_bre", [128, 4, 128], F32, e2); t_pbre = T()
        ps_bim = k.ps("ps_bim", [128, 4, 128], F32, e2); t_pbim = T()
        ps_y = k.ps("ps_y", [128, 128], F32, e2); t_psy = T()
        yt = k.sb("yt", [128, 4, 512], F32, e2); t_yt = T()

        def fl(ap):
            return ap.rearrange("p a b -> p (a b)")

        for t in (tlist if stop >= 2 else []):
            own = t >= OWN_T0
            c0 = t * 512
            oc0 = (t - OWN_T0) * 512
            k.load(ub[:], uT_bf_d[:, :, c0:c0 + 512], [t_ub], rd=[t_uTd])
            if own:
                k.load(uf[:], uT_f_d[:, :, oc0:oc0 + 512], [t_uf], rd=[t_uTfd])
            for s in range(4):
                sc0 = s * 128
                for hf in range(4):
                    js = list(range(hf * 4, hf * 4 + 4))
                    hs = slice(hf * 4, hf * 4 + 4)
                    for jj, j in enumerate(js):
                        k.mm(ps_bre[:, jj, :], LBre[:, j, :], ub[:, hf, sc0:sc0 + 128], True, True, [t_LB, t_ub], [t_pbre])
                        k.mm(ps_bim[:, jj, :], LBim[:, j, :], ub[:, hf, sc0:sc0 + 128], True, True, [t_LB, t_ub], [t_pbim])
                    Frh = Fr[:, hs, :]; Fih = Fi[:, hs, :]
                    Erh = Er[:, hs, :]; Eih = Ei[:, hs, :]
                    k.tt("dve", ga[:], ps_bre[:], Frh, ALU.mult, [t_pbre, t_tab], [t_ga])
                    k.tt("dve", gb[:], ps_bim[:], Fih, ALU.mult, [t_pbim, t_tab], [t_gb])
                    k.tt("pool", gri[:], ga[:], gb[:], ALU.subtract, [t_ga, t_gb], [t_gri])
                    k.tt("dve", ga[:], ps_bre[:], Fih, ALU.mult, [t_pbre, t_tab], [t_ga])
                    k.tt("dve", gb[:], ps_bim[:], Frh, ALU.mult, [t_pbim, t_tab], [t_gb])
                    k.tt("pool", gii[:], ga[:], gb[:], ALU.add, [t_ga, t_gb], [t_gii])
                    k.tt("dve", gri[:, :, 0], gri[:, :, 0], hst[:, hs, 2], ALU.add, [t_gri, t_hst], [t_gri])
                    k.tt("dve", gii[:, :, 0], gii[:, :, 0], hst[:, hs, 3], ALU.add, [t_gii, t_hst], [t_gii])
                    Rmh = fl(Rm[:, hs, :])
                    P.op("dve", lambda q, o=fl(gre[:]), d0=Rmh, d1=fl(gri[:]):
                         q.tensor_tensor_scan(out=o, data0=d0, data1=d1, initial=0.0, op0=ALU.mult, op1=ALU.add),
                         [t_tab, t_gri], [t_gre])
                    P.op("dve", lambda q, o=fl(gim[:]), d0=Rmh, d1=fl(gii[:]):
                         q.tensor_tensor_scan(out=o, data0=d0, data1=d1, initial=0.0, op0=ALU.mult, op1=ALU.add),
                         [t_tab, t_gii], [t_gim])
                    cs = slice(0, 128) if own else slice(127, 128)
                    k.tt("pool", ga[:, :, cs], gre[:, :, cs], Erh[:, :, cs], ALU.mult, [t_gre, t_tab], [t_ga])
                    k.tt("pool", gb[:, :, cs], gim[:, :, cs], Eih[:, :, cs], ALU.mult, [t_gim, t_tab], [t_gb])
                    k.tt("dve", hre[:, :, cs], ga[:, :, cs], gb[:, :, cs], ALU.subtract, [t_ga, t_gb], [t_hre])
                    k.tt("pool", ga[:, :, cs], gre[:, :, cs], Eih[:, :, cs], ALU.mult, [t_gre, t_tab], [t_ga])
                    k.tt("pool", gb[:, :, cs], gim[:, :, cs], Erh[:, :, cs], ALU.mult, [t_gim, t_tab], [t_gb])
                    k.tt("dve", him[:, :, cs], ga[:, :, cs], gb[:, :, cs], ALU.add, [t_ga, t_gb], [t_him])
                    k.cp("dve", hst[:, hs, 0], hre[:, :, 127], [t_hre], [t_hst])
                    k.cp("dve", hst[:, hs, 1], him[:, :, 127], [t_him], [t_hst])
                    k.tt("dve", hst[:, hs, 2], hre[:, :, 127], sm[:, hs, 2], ALU.mult, [t_hre, t_sm], [t_hst])
                    k.tt("dve", hst[:, hs, 3], him[:, :, 127], sm[:, hs, 2], ALU.mult, [t_him, t_sm], [t_hst])
                    if own:
                        k.cp("act", hbre[:], hre[:], [t_hre], [t_hbre])
                        k.act(hbim[:], him[:], AF.Copy, [t_him], [t_hbim], scale=-1.0)
                        for jj, j in enumerate(js):
                            k.mm(ps_y[:], LCre[:, j, :], hbre[:, jj, :], jj == 0, False, [t_LB, t_hbre], [t_psy])
                            k.mm(ps_y[:], LCim[:, j, :], hbim[:, jj, :], False, jj == 3, [t_LB, t_hbim], [t_psy])
                        k.stt(yt[:, hf, sc0:sc0 + 128], uf[:, hf, sc0:sc0 + 128], Dcol[:, hf:hf + 1], ps_y[:],
                              ALU.mult, ALU.add, [t_uf, t_Dcol, t_psy], [t_yt])
            if own:
                P.dma("sp", lambda q, o=yssm_d[:, :, oc0:oc0 + 512], i_=yt[:]: q.dma_start(out=o, in_=i_), [t_yt], [t_yssm])
        k.store(ssm_out[:, :, :], hst[:, :, 0:2], [t_hst])

        if do_sample:
            hs0 = k.sb("hs0", [128, 16, 4, 2], F32, e2); t_hs0 = T()
            k.load(hs0[:], hs0_d, [t_hs0])
            rh0 = k.sb("rh0", [128, 16, 4, 2], F32, e2); t_rh0 = T()
            k.tt("dve", rh0[:], hs0[:], sm[:, :, 2:3].unsqueeze(3).to_broadcast([128, 16, 4, 2]), ALU.mult, [t_hs0, t_sm], [t_rh0])
            Rms = k.sb("Rms", [128, 16, 4, 4], F32, e2); t_Rms = T()
            k.cp("dve", Rms[:], sm[:, :, 2:3].unsqueeze(3).to_broadcast([128, 16, 4, 4]), [t_sm], [t_Rms])
            k.memset("dve", Rms[:, :, :, 0:1], 0.0, [t_Rms])
            hso = k.sb("hso", [128, 16, 4, 2], F32, e2); t_hso = T()
            sw = [k.sb(f"sw{i}", [128, 4, 4, 4], F32, e2) for i in range(8)]; t_sw = [T() for _ in range(8)]
            shb = [k.sb(f"shb{i}", [128, 4, 16], BF16, e2) for i in range(2)]; t_shb = [T(), T()]
            sga, sgb, sgri, sgii, sgre, sgim, shre, shim = sw
            tga, tgb, tgri, tgii, tgre, tgim, thre, thim = t_sw

            def fl2(ap):
                return ap.rearrange("p a b c -> p (a b c)")

            def b4(ap):
                return ap.unsqueeze(2).to_broadcast([128, 4, 4, 4])

            for hf in range(4):
                hs = slice(hf * 4, hf * 4 + 4)
                for jj in range(4):
                    j = hf * 4 + jj
                    k.mm(ps_bre[:, jj, 0:16], LBre[:, j, :], us_b[:, hf, :], True, True, [t_LB, t_us], [t_pbre])
                    k.mm(ps_bim[:, jj, 0:16], LBim[:, j, :], us_b[:, hf, :], True, True, [t_LB, t_us], [t_pbim])
                bre4 = ps_bre[:, :, 0:16].rearrange("p j (b q) -> p j b q", q=4)
                bim4 = ps_bim[:, :, 0:16].rearrange("p j (b q) -> p j b q", q=4)
                Fr4 = b4(Fr[:, hs, 0:4]); Fi4 = b4(Fi[:, hs, 0:4]); Er4 = b4(Er[:, hs, 0:4]); Ei4 = b4(Ei[:, hs, 0:4])
                k.tt("dve", sga[:], bre4, Fr4, ALU.mult, [t_pbre, t_tab], [tga])
                k.tt("dve", sgb[:], bim4, Fi4, ALU.mult, [t_pbim, t_tab], [tgb])
                k.tt("dve", sgri[:], sga[:], sgb[:], ALU.subtract, [tga, tgb], [tgri])
                k.tt("dve", sga[:], bre4, Fi4, ALU.mult, [t_pbre, t_tab], [tga])
                k.tt("dve", sgb[:], bim4, Fr4, ALU.mult, [t_pbim, t_tab], [tgb])
                k.tt("dve", sgii[:], sga[:], sgb[:], ALU.add, [tga, tgb], [tgii])
                k.tt("dve", sgri[:, :, :, 0], sgri[:, :, :, 0], rh0[:, hs, :, 0], ALU.add, [tgri, t_rh0], [tgri])
                k.tt("dve", sgii[:, :, :, 0], sgii[:, :, :, 0], rh0[:, hs, :, 1], ALU.add, [tgii, t_rh0], [tgii])
                Rmsh = fl2(Rms[:, hs, :, :])
                P.op("dve", lambda q, o=fl2(sgre[:]), d0=Rmsh, d1=fl2(sgri[:]):
                     q.tensor_tensor_scan(out=o, data0=d0, data1=d1, initial=0.0, op0=ALU.mult, op1=ALU.add), [t_Rms, tgri], [tgre])
                P.op("dve", lambda q, o=fl2(sgim[:]), d0=Rmsh, d1=fl2(sgii[:]):
                     q.tensor_tensor_scan(out=o, data0=d0, data1=d1, initial=0.0, op0=ALU.mult, op1=ALU.add), [t_Rms, tgii], [tgim])
                k.tt("dve", sga[:], sgre[:], Er4, ALU.mult, [tgre, t_tab], [tga])
                k.tt("dve", sgb[:], sgim[:], Ei4, ALU.mult, [tgim, t_tab], [tgb])
                k.tt("dve", shre[:], sga[:], sgb[:], ALU.subtract, [tga, tgb], [thre])
                k.tt("dve", sga[:], sgre[:], Ei4, ALU.mult, [tgre, t_tab], [tga])
                k.tt("dve", sgb[:], sgim[:], Er4, ALU.mult, [tgim, t_tab], [tgb])
                k.tt("dve", shim[:], sga[:], sgb[:], ALU.add, [tga, tgb], [thim])
                k.cp("dve", hso[:, hs, :, 0], shre[:, :, :, 3], [thre], [t_hso])
                k.cp("dve", hso[:, hs, :, 1], shim[:, :, :, 3], [thim], [t_hso])
                k.cp("act", shb[0][:], shre[:].rearrange("p j b q -> p j (b q)"), [thre], [t_shb[0]])
                k.act(shb[1][:], shim[:].rearrange("p j b q -> p j (b q)"), AF.Copy, [thim], [t_shb[1]], scale=-1.0)
                for jj in range(4):
                    j = hf * 4 + jj
                    k.mm(ps_y[:, 0:16], LCre[:, j, :], shb[0][:, jj, :], jj == 0, False, [t_LB, t_shb[0]], [t_psy])
                    k.mm(ps_y[:, 0:16], LCim[:, j, :], shb[1][:, jj, :], False, jj == 3, [t_LB, t_shb[1]], [t_psy])
                k.stt(yssm_s[:, hf, :], us_f[:, hf, :], Dcol[:, hf:hf + 1], ps_y[:, 0:16], ALU.mult, ALU.add, [t_us, t_Dcol, t_psy], [t_yssm_s])
            k.store(ssm_s_out, hso[:], [t_hso])
    if skip23:
        P.finalize()
        eA.close()
        k.es.close()
        return nc, P
    if not no_prompt:
        P.barrier()
        with ES() as e3:
            Xc = [k.sb(f"Xc{i}", [128, NW], BF16, e3) for i in range(2)]; t_Xc = [T(), T()]
            k.load(Xc[0][:], KcmpT_d, [t_Xc[0]], rd=[t_Kcd])
            k.load(Xc[1][:], VcmpT_d, [t_Xc[1]], rd=[t_Vcd])
            w1 = [k.sb(f"w1_{i}", [128, 32, 128], BF16, e3) for i in range(2)]; t_w1 = T()
            for i in range(2):
                src = cmp_w1_d[i].rearrange("(r d) h -> d r h", d=64)
                k.load(w1[i][0:64], src, [t_w1], eng="pool")
                k.load(w1[i][64:128], src, [t_w1], eng="pool")
            peT = k.sb("peT", [128, 32, 2], BF16, e3); t_peT = T()
            k.load(peT[:], cmp_peT_d, [t_peT], eng="pool")
            w2kp = k.sb("w2kp", [128, 2, 128], BF16, e3); t_w2 = T()
            w2v = k.sb("w2v", [128, 64], BF16, e3)
            k.load(w2kp[:], cmp_w2kp_d, [t_w2], eng="pool")
            k.load(w2v[:], cmp_w2v_d, [t_w2], eng="pool")
            pebias = k.sb("pebias", [128, 2], F32, e3); t_peb = T()
            ps_pb = k.ps("ps_pb", [128, 2], F32, e3); t_pspb = T()
            ps_pre = k.ps("ps_pre", [128, 512], F32, e3); t_pspre = T()
            ps_kc = k.ps("ps_kc", [128, 512], F32, e3); t_pskc = T()
            ps_vc = k.ps("ps_vc", [128, 64], F32, e3); t_psvc = T()
            for kv in range(2):
                for r in range(32):
                    k.mm(ps_pb[:, kv:kv + 1], w1[kv][0:64, r, :], peT[0:64, r, kv:kv + 1], r == 0, r == 31, [t_w1, t_peT], [t_pspb])
            k.cp("dve", pebias[:], ps_pb[:], [t_pspb], [t_peb])
            xg = k.sb("xg", [128, 512], F32, e3); t_xg = T()
            tg = k.sb("tg", [128, 512], F32, e3); t_tg = T()
            sg = k.sb("sg", [128, 512], F32, e3); t_sg = T()
            gh = [k.sb(f"gh{i}", [128, 512], BF16, e3) for i in range(2)]; t_gh = [T(), T()]
            for g in range(2):
                k.memset("dve", gh[g][:], 0.0, [t_gh[g]])
            for kv in range(2):
                for g in range(2):
                    gs = slice(g * 64, (g + 1) * 64)
                    for r in range(32):
                        k.mm(ps_pre[:, 0:511], w1[kv][gs, r, :], Xc[kv][gs, r:r + 16 * 510 + 1:16], r == 0, r == 31,
                             [t_w1, t_Xc[kv]], [t_pspre])
                    k.act(xg[:, 0:511], ps_pre[:, 0:511], AF.Identity, [t_pspre, t_peb], [t_xg], bias=pebias[:, kv:kv + 1])
                    k.tt("dve", tg[:, 0:511], xg[:, 0:511], xg[:, 0:511], ALU.mult, [t_xg], [t_tg])
                    k.ts("dve", tg[:, 0:511], tg[:, 0:511], 0.044715, 1.0, ALU.mult, ALU.add, [t_tg], [t_tg])
                    k.tt("dve", tg[:, 0:511], tg[:, 0:511], xg[:, 0:511], ALU.mult, [t_tg, t_xg], [t_tg])
                    k.act(sg[:, 0:511], tg[:, 0:511], AF.Sigmoid, [t_tg], [t_sg], scale=1.5957691216)
                    k.tt("dve", gh[g][:, 0:511], xg[:, 0:511], sg[:, 0:511], ALU.mult, [t_xg, t_sg], [t_gh[g]])
                    if kv == 1:
                        for ct in range(4):
                            k.mm(ps_vc[:], gh[g][:, ct * 128:(ct + 1) * 128], w2v[:], True, True, [t_gh[g], t_w2], [t_psvc])
                            k.cp("dve", V1c[:, ct, g, 0:64], ps_vc[:], [t_psvc], [t_V1c])
                if kv == 0:
                    for g in range(2):
                        k.mm(ps_kc[:], w2kp[:, g, :], gh[g][:], g == 0, g == 1, [t_w2, t_gh[g]], [t_pskc])
                    k.cp("dve", KcT[:], ps_kc[:], [t_pskc], [t_KcT])

        P.barrier()
        with ES() as e4:
            Em = k.sb("Em", [128, 64, 128], BF16, e4); t_Em = T()
            ov = k.sb("ov", [128, 4, 128], BF16, e4); t_ov = T()
            selg = k.sb("selg", [24, 24, 64], F32, e4); t_selg = T()
            sel65 = k.sb("sel65", [65, 128], F32, e4); t_sel65 = T()
            causal4 = k.sb("causal4", [128, 512], BF16, e4); t_causal = T()
            k.load(Em[:], Em_d, [t_Em]); k.load(ov[:], ov_d, [t_ov]); k.load(selg[:], selg_d, [t_selg])
            k.load(sel65[:], sel65_d, [t_sel65]); k.load(causal4[:], causal4_d, [t_causal])
            cmpb = k.sb("cmpb", [128, 4, 512], BF16, e4); t_cmpb = T()
            winb = k.sb("winb", [128, 5, 512], BF16, e4); t_winb = T()
            M1 = k.sb("M1", [128, 128], F32, e4); M2 = k.sb("M2", [128, 128], F32, e4); t_M = T()
            Pc = [k.sb(f"Pc{i}", [128, 512], BF16, e4) for i in range(4)]; t_Pc = [T() for _ in range(4)]
            Pn = [k.sb(f"Pn{i}", [128, 512], BF16, e4) for i in range(4)]; t_Pn = [T() for _ in range(4)]
            Pt = [k.sb(f"Pt{i}", [128, 512], BF16, e4) for i in range(3)]; t_Pt = [T() for _ in range(3)]
            osb = [k.sb(f"osb{i}", [65, 512], F32, e4) for i in range(3)]; t_osb = [T() for _ in range(3)]
            rden = [k.sb(f"rden{i}", [128, 512], F32, e4) for i in range(3)]; t_rden = [T() for _ in range(3)]
            imp2 = k.sb("imp2", [128, 128], F32, e4); t_imp2 = T()
            imp3 = k.sb("imp3", [128, 128], F32, e4); t_imp3 = T()
            mx = k.sb("mx", [128, 16], F32, e4); t_mx = T()
            thr = k.sb("thr", [128, 1], F32, e4); t_thr = T()
            nsel = k.sb("nsel", [128, 128], BF16, e4); t_nsel = T()
            nselT4 = k.sb("nselT4", [128, 4, 128], BF16, e4); t_nselT = T()
            acc = k.sb("acc", [64, 512], F32, e4); t_acc = T()
            tA = k.sb("tA", [64, 512], F32, e4); t_tA = T()
            tB = k.sb("tB", [64, 512], F32, e4); t_tB = T()
            ps_s = [k.ps(f"ps_s{i}", [128, 512], F32, e4) for i in range(3)]; t_pss_ = [T() for _ in range(3)]
            ps_o = k.ps("ps_o", [65, 512], F32, e4); t_pso = T()
            ps_den = k.ps("ps_den", [128, 512], F32, e4); t_psden = T()
            ps_imp = k.ps("ps_imp", [128, 128], F32, e4); t_psimp = T()
            ps_tt = k.ps("ps_tt", [128, 128], BF16, e4); t_pstt = T()
            ps_g = k.ps("ps_g", [64, 512], F32, e4); t_psg = T()
            sc_ = [0]

            def score_tile(lhsT_k, qt, extra, rd_extra):
                i3 = sc_[0] % 3
                sc_[0] += 1
                pss, tps = ps_s[i3], t_pss_[i3]
                n = len(extra)
                k.mm(pss[:], lhsT_k, qt, True, n == 0, rd_extra + [t_QT], [tps])
                for ei, (l_, r_, rds) in enumerate(extra):
                    k.mm(pss[:], l_, r_, False, ei == n - 1, rds, [tps])
                return pss, tps

            def finish_branch(bi):
                k.cp("act", osb[bi][:], ps_o[:], [t_pso], [t_osb[bi]])
                k.mm(ps_den[:], sel65[:], osb[bi][:], True, True, [t_sel65, t_osb[bi]], [t_psden])
                k.ts("dve", rden[bi][:], ps_den[:], 1e-30, None, ALU.max, None, [t_psden], [t_rden[bi]])
                k.recip(rden[bi][:], rden[bi][:], [t_rden[bi]], [t_rden[bi]])

            def run_branch(n, kfn, exfn, rdk, vfn, rdv, ptile, LA=2):
                scored = {}

                def emit_score(j):
                    scored[j] = score_tile(kfn(j), qt_cur[0], exfn(j), rdk)

                for j in range(min(LA, n)):
                    emit_score(j)
                for j in range(n):
                    pss, tps = scored.pop(j)
                    pt, tpt = ptile(j)
                    k.act(pt[:], pss[:], AF.Exp, [tps], [tpt], scale=0.125)
                    if j + LA < n:
                        emit_score(j + LA)
                    k.mm(ps_o[:], vfn(j), pt[:], j == 0, j == n - 1, rdv + [tpt], [t_pso])

            qt_cur = [None]
            for i in (chunks if chunks is not None else range(16)):
                ktd = 48 + i
                k.load(cmpb[:], cmpb_d[i], [t_cmpb])
                k.load(winb[:], winb_d[i], [t_winb])
                k.load(M1[:], M1_d[i], [t_M]); k.load(M2[:], M2_d[i], [t_M])
                for g in range(2):
                    gs = slice(g * 64, (g + 1) * 64)
                    qt = QT[gs, i, :, :].rearrange("p a b -> p (a b)")
                    qt_cur[0] = qt
                    run_branch(4, lambda ct: KcT[gs, ct * 128:(ct + 1) * 128],
                               lambda ct: [(identb[:], cmpb[:, ct, :], [t_identb, t_cmpb])], [t_KcT],
                               lambda ct: V1c[:, ct, g, :], [t_V1c], lambda ct: (Pc[ct], t_Pc[ct]))
                    finish_branch(0)
                    for ct in range(4):
                        k.tt("pool", Pn[ct][:], Pc[ct][:], rden[0][:], ALU.mult, [t_Pc[ct], t_rden[0]], [t_Pn[ct]])
                    for ct in range(4):
                        for r in range(4):
                            k.mm(ps_imp[:], Pn[ct][:, r * 128:(r + 1) * 128], ov[:, ct, :], ct == 0 and r == 0, ct == 3 and r == 3,
                                 [t_Pn[ct], t_ov], [t_psimp])
                    k.tt("dve", imp2[:], ps_imp[:], M1[:], ALU.mult, [t_psimp, t_M], [t_imp2])
                    k.tt("dve", imp2[:], imp2[:], M2[:], ALU.add, [t_imp2, t_M], [t_imp2])
                    P.op("dve", lambda q: q.max(out=mx[:, 0:8], in_=imp2[:]), [t_imp2], [t_mx])
                    P.op("dve", lambda q: q.match_replace(out=imp3[:], in_to_replace=mx[:, 0:8], in_values=imp2[:], imm_value=-2e30),
                         [t_imp2, t_mx], [t_imp3])
                    P.op("dve", lambda q: q.max(out=mx[:, 8:16], in_=imp3[:]), [t_imp3], [t_mx])
                    k.ts("dve", thr[:], mx[:, 15:16], -1e29, None, ALU.max, None, [t_mx], [t_thr])
                    k.ts("dve", imp3[:], imp2[:], thr[:, 0:1], 1.0, ALU.is_ge, ALU.subtract, [t_imp2, t_thr], [t_imp3])
                    k.ts("dve", nsel[:], imp3[:], 30000.0, None, ALU.mult, None, [t_imp3], [t_nsel])
                    k.tr(ps_tt[:], nsel[:], identb[:], [t_nsel, t_identb], [t_pstt])
                    k.cp("dve", nselT4[:], ps_tt[:].unsqueeze(1).to_broadcast([128, 4, 128]), [t_pstt], [t_nselT])
                    nsT = nselT4[:].rearrange("p a b -> p (a b)")
                    def ex_slc_p(kt):
                        extra = [(Em[:, kt, :], nsT, [t_Em, t_nselT])]
                        if kt == ktd:
                            extra.append((identb[:], causal4[:], [t_identb, t_causal]))
                        return extra
                    run_branch(ktd + 1, lambda kt: KslcT[gs, kt * 128:(kt + 1) * 128], ex_slc_p, [t_KslcT],
                               lambda kt: V1s[:, kt, g, :], [t_V1s], lambda kt: (Pt[kt % 3], t_Pt[kt % 3]))
                    finish_branch(1)
                    run_branch(5, lambda w: KwinT[gs, (i + w) * 128:(i + w + 1) * 128],
                               lambda w: [(identb[:], winb[:, w, :], [t_identb, t_winb])], [t_KwinT],
                               lambda w: V1w[:, i + w, g, :], [t_V1w], lambda w: (Pt[w % 3], t_Pt[w % 3]))
                    finish_branch(2)
                    for n in range(3):
                        for r in range(4):
                            k.mm(ps_g[:, r * 128:(r + 1) * 128], selg[:, (g * 4 + r) * 3 + n, :], gates[:, i * 128:(i + 1) * 128],
                                 True, True, [t_selg, t_gates], [t_psg])
                        k.tt("pool", tA[:], osb[n][0:64, :], rden[n][0:64, :], ALU.mult, [t_osb[n], t_rden[n]], [t_tA])
                        if n == 0:
                            k.tt("dve", acc[:], tA[:], ps_g[:], ALU.mult, [t_tA, t_psg], [t_acc])
                        else:
                            k.tt("dve", tB[:], tA[:], ps_g[:], ALU.mult, [t_tA, t_psg], [t_tB])
                            k.tt("pool", acc[:], acc[:], tB[:], ALU.add, [t_acc, t_tB], [t_acc])
                    P.dma("sp", lambda q, o=attn_d[:, g * 4:(g + 1) * 4, i * 128:(i + 1) * 128], i_=acc[:].rearrange("p (a b) -> p a b", b=128):
                          q.dma_start(out=o, in_=i_), [t_acc], [t_attnd])
            if dbg:
                att_sb = k.sb("att_sb", [64, 8, 128], F32, e4); t_attsb = T()
                for i in (chunks if chunks is not None else range(16)):
                    k.load(att_sb[:], attn_d[:, :, i * 128:(i + 1) * 128], [t_attsb], rd=[t_attnd])
                    k.store(attn_out[:, :, i * 128:(i + 1) * 128], att_sb[:], [t_attsb])

    eA.close()
    P.barrier()
    if do_sample:
      with ES() as e6:
        NS = 16896
        big = k.sb("big", [128, 2, NS], BF16, e6); t_big = [T(), T()]
        XK = big[:, 0, :]; XV = big[:, 1, :]; KsT = big[:, 0, :]
        V1ss = big[:, 1, 0:129 * 130].rearrange("p (t g e) -> p t g e", g=2, e=65)
        Em = k.sb("Em2", [128, 64, 128], BF16, e6); t_Em = T()
        k.load(Em[:], Em_d, [t_Em])
        w1 = [k.sb(f"w1s_{i}", [128, 32, 128], BF16, e6) for i in range(2)]; t_w1 = T()
        for i in range(2):
            src = cmp_w1_d[i].rearrange("(r d) h -> d r h", d=64)
            k.load(w1[i][0:64], src, [t_w1], eng="pool")
            k.load(w1[i][64:128], src, [t_w1], eng="pool")
        peT = k.sb("peT2", [128, 32, 2], BF16, e6); t_peT = T()
        k.load(peT[:], cmp_peT_d, [t_peT], eng="pool")
        w2kp = k.sb("w2kp2", [128, 2, 128], BF16, e6); t_w2 = T()
        w2v = k.sb("w2v2", [128, 64], BF16, e6)
        k.load(w2kp[:], cmp_w2kp_d, [t_w2], eng="pool")
        k.load(w2v[:], cmp_w2v_d, [t_w2], eng="pool")
        selg = k.sb("selg2", [24, 24, 64], F32, e6); t_selg = T()
        sel65 = k.sb("sel652", [65, 128], F32, e6); t_sel65 = T()
        k.load(selg[:], selg_d, [t_selg]); k.load(sel65[:], sel65_d, [t_sel65])
        ovs = k.sb("ovs", [128, 8, 257], BF16, e6); t_ovs = T()
        Ms = k.sb("Ms", [4, 2, 257], F32, e6); t_Ms = T()
        sbias = k.sb("sbias", [128, 7, 16], BF16, e6); t_sbias = T()
        k.load(ovs[:], ovs_d, [t_ovs]); k.load(Ms[:], Ms_d, [t_Ms]); k.load(sbias[:], sbias_d, [t_sbias])
        ptab = k.sb("ptab", [128, 4, 128], I32, e6); t_ptab = T()
        piota = k.sb("piota", [128, 1], F32, e6); t_piota = T()
        idxs = k.sb("idxs", [128, 4, 128], I32, e6); t_idxs = T()
        k.load(ptab[:], ptab_d, [t_ptab]); k.load(piota[:], piota_d, [t_piota])
        k.ts("dve", idxs[:], ptab[:], 128.0, piota[:, 0:1], ALU.mult, ALU.add, [t_ptab, t_piota], [t_idxs])
        pg = [k.sb(f"pg{i}", [128, 256], BF16, e6) for i in range(4)]; t_pg = [T() for _ in range(4)]
        wpg = k.sb("wpg", [128, 4, 256], BF16, e6); t_wpg = T()
        KwT = k.sb("KwT", [128, 640], BF16, e6); t_KwT = T()
        V1ws = k.sb("V1ws", [128, 5, 2, 65], BF16, e6); t_V1ws = T()
        KcTs = k.sb("KcTs", [128, 1024], BF16, e6); t_KcTs = T()
        V1cs = k.sb("V1cs", [128, 8, 2, 65], BF16, e6); t_V1cs = T()
        k.memset("pool", V1cs[:], 1.0, [t_V1cs])
        pebias = k.sb("pebias2", [128, 2], F32, e6); t_peb = T()
        xg = k.sb("xg2", [128, 512], F32, e6); t_xg = T()
        tg = k.sb("tg2", [128, 512], F32, e6); t_tg = T()
        sg = k.sb("sg2", [128, 512], F32, e6); t_sg = T()
        gh = [k.sb(f"ghs{i}", [128, 1024], BF16, e6) for i in range(2)]; t_gh = [T(), T()]
        P8 = [k.sb(f"P8_{i}", [128, 8, 16], BF16, e6) for i in range(2)]; t_P8 = [T(), T()]
        Pn8 = k.sb("Pn8", [128, 8, 16], BF16, e6); t_Pn8 = T()
        osb = [k.sb(f"osbs{i}", [65, 16], F32, e6) for i in range(3)]; t_osb = [T() for _ in range(3)]
        rden = [k.sb(f"rdens{i}", [128, 16], F32, e6) for i in range(3)]; t_rden = [T() for _ in range(3)]
        imp2 = k.sb("imp2s", [4, 257], F32, e6); t_imp2 = T()
        imp3 = k.sb("imp3s", [4, 257], F32, e6); t_imp3 = T()
        mx = k.sb("mxs", [4, 16], F32, e6); t_mx = T()
        thr = k.sb("thrs", [4, 1], F32, e6); t_thr = T()
        nsel = k.sb("nsels", [4, 384], BF16, e6); t_nsel = T()
        k.memset("dve", nsel[:], 0.0, [t_nsel])
        nselT4 = k.sb("nselT4s", [128, 3, 4, 4], BF16, e6); t_nselT = T()
        acc = k.sb("accs", [64, 16], F32, e6); t_acc = T()
        tA = k.sb("tAs", [64, 16], F32, e6); t_tA = T()
        tB = k.sb("tBs", [64, 16], F32, e6); t_tB = T()
        ps_t1 = [k.ps(f"ps_t1{i}", [128, 4, 128], BF16, e6) for i in range(2)]; t_pt1 = [T(), T()]
        ps_t2 = [k.ps(f"ps_t2{i}", [128, 4, 128], BF16, e6) for i in range(2)]; t_pt2 = [T(), T()]
        ps_m = k.ps("ps_ms", [128, 512], F32, e6); t_psm = T()
        ps_s8 = [k.ps(f"ps_s8{i}", [128, 8, 16], F32, e6) for i in range(2)]; t_ps8 = [T(), T()]
        ps_o = k.ps("ps_os", [128, 512], F32, e6); t_pso = T()
        for kv in range(2):
            for r in range(32):
                k.mm(ps_m[:, kv:kv + 1], w1[kv][0:64, r, :], peT[0:64, r, kv:kv + 1], r == 0, r == 31, [t_w1, t_peT], [t_psm])
        k.cp("dve", pebias[:], ps_m[:, 0:2], [t_psm], [t_peb])
        k.memset("pool", big[:, 1, :], 1.0, [t_big[1]])
        cc_ = [0]

        def gather_pages(cache_v, bi, kdst, on_v):
            for lp in range(128):
                pgi = pg[lp % 4]; tpg = t_pg[lp % 4]
                P.dma("pool", lambda q, o=pgi[:], src=cache_v, ix=idxs[:, bi, lp:lp + 1]:
                      q.indirect_dma_start(out=o, out_offset=None, in_=src, in_offset=bass.IndirectOffsetOnAxis(ap=ix, axis=0)),
                      [t_idxs], [tpg])
                grp = (lp // 4) % 2
                k.tr(ps_t1[grp][:, lp % 4, :], pgi[:, 0:128], identb[:], [tpg, t_identb], [t_pt1[grp]])
                on_v(lp, pgi, tpg, grp)
                if lp % 4 == 3:
                    l0 = lp - 3
                    k.cp("dve", kdst[:, l0 * 128:(l0 + 4) * 128], ps_t1[grp][:].rearrange("p a b -> p (a b)"), [t_pt1[grp]], [t_big[0]])

        for bi in range(nsb):
            def v_cmp(lp, pgi, tpg, grp):
                k.tr(ps_t2[grp][:, lp % 4, :], pgi[:, 128:256], identb[:], [tpg, t_identb], [t_pt2[grp]])
                if lp % 4 == 3:
                    l0 = lp - 3
                    k.cp("act", XV[:, l0 * 128:(l0 + 4) * 128], ps_t2[grp][:].rearrange("p a b -> p (a b)"), [t_pt2[grp]], [t_big[1]])
            gather_pages(cache_cmp_v, bi, XK, v_cmp)
            for g in range(2):
                k.memset("dve", gh[g][:, 1023:1024], 0.0, [t_gh[g]])
            for kv in range(2):
                X = XK if kv == 0 else XV
                for g in range(2):
                    gs = slice(g * 64, (g + 1) * 64)
                    for half in range(2):
                        n = 512 if half == 0 else 511
                        b0 = 16 * 512 * half
                        for r in range(32):
                            k.mm(ps_m[:, 0:n], w1[kv][gs, r, :], X[gs, b0 + r:b0 + r + 16 * (n - 1) + 1:16], r == 0, r == 31,
                                 [t_w1, t_big[kv]], [t_psm])
                        k.act(xg[:, 0:n], ps_m[:, 0:n], AF.Identity, [t_psm, t_peb], [t_xg], bias=pebias[:, kv:kv + 1])
                        k.tt("dve", tg[:, 0:n], xg[:, 0:n], xg[:, 0:n], ALU.mult, [t_xg], [t_tg])
                        k.ts("dve", tg[:, 0:n], tg[:, 0:n], 0.044715, 1.0, ALU.mult, ALU.add, [t_tg], [t_tg])
                        k.tt("dve", tg[:, 0:n], tg[:, 0:n], xg[:, 0:n], ALU.mult, [t_tg, t_xg], [t_tg])
                        k.act(sg[:, 0:n], tg[:, 0:n], AF.Sigmoid, [t_tg], [t_sg], scale=1.5957691216)
                        k.tt("dve", gh[g][:, half * 512:half * 512 + n], xg[:, 0:n], sg[:, 0:n], ALU.mult, [t_xg, t_sg], [t_gh[g]])
                    if kv == 1:
                        for ct in range(8):
                            k.mm(ps_m[:, 0:64], gh[g][:, ct * 128:(ct + 1) * 128], w2v[:], True, True, [t_gh[g], t_w2], [t_psm])
                            k.cp("dve", V1cs[:, ct, g, 0:64], ps_m[:, 0:64], [t_psm], [t_V1cs])
                if kv == 0:
                    for half in range(2):
                        for g in range(2):
                            k.mm(ps_m[:], w2kp[:, g, :], gh[g][:, half * 512:(half + 1) * 512], g == 0, g == 1, [t_w2, t_gh[g]], [t_psm])
                        k.cp("dve", KcTs[:, half * 512:(half + 1) * 512], ps_m[:], [t_psm], [t_KcTs])
            def v_slc(lp, pgi, tpg, grp):
                k.cp("pool", V1ss[:, lp, :, 0:64], pgi[:, 128:256].rearrange("p (g d) -> p g d", g=2), [tpg], [t_big[1]])
            k.memset("pool", V1ss[:, 0:128, :, 64:65], 1.0, [t_big[1]])
            gather_pages(cache_slc_v, bi, KsT, v_slc)
            bc = slice(bi * 4, bi * 4 + 4)
            k.memset("dve", KsT[:, 16384:16512], 0.0, [t_big[0]])
            k.cp("dve", KsT[:, 16384:16388], kvnew[:, 0, bc], [t_kvnew], [t_big[0]])
            k.memset("pool", V1ss[:, 128, :, :], 0.0, [t_big[1]])
            k.tr(ps_t2[0][0:4, 0, :], kvnew[:, 1, bc], identb[:], [t_kvnew, t_identb], [t_pt2[0]])
            k.cp("act", V1ss[0:4, 128, :, 0:64], ps_t2[0][0:4, 0, :].rearrange("p (g d) -> p g d", g=2), [t_pt2[0]], [t_big[1]])
            k.memset("pool", V1ss[0:4, 128, :, 64:65], 1.0, [t_big[1]])
            k.load(wpg[:], cache_win_d[bi].rearrange("(t p) f -> p t f", p=128), [t_wpg], eng="pool")
            for t4 in range(4):
                k.tr(ps_t1[0][:, t4, :], wpg[:, t4, 0:128], identb[:], [t_wpg, t_identb], [t_pt1[0]])
            k.memset("dve", KwT[:], 0.0, [t_KwT])
            k.cp("dve", KwT[:, 0:512], ps_t1[0][:].rearrange("p a b -> p (a b)"), [t_pt1[0]], [t_KwT])
            k.cp("dve", KwT[:, 512:516], kvnew[:, 2, bc], [t_kvnew], [t_KwT])
            k.memset("pool", V1ws[:, 0:4, :, 64:65], 1.0, [t_V1ws])
            k.cp("pool", V1ws[:, 0:4, :, 0:64], wpg[:, :, 128:256].rearrange("p t (g d) -> p t g d", g=2), [t_wpg], [t_V1ws])
            k.memset("pool", V1ws[:, 4, :, :], 0.0, [t_V1ws])
            k.tr(ps_t2[1][0:4, 0, :], kvnew[:, 3, bc], identb[:], [t_kvnew, t_identb], [t_pt2[1]])
            k.cp("act", V1ws[0:4, 4, :, 0:64], ps_t2[1][0:4, 0, :].rearrange("p (g d) -> p g d", g=2), [t_pt2[1]], [t_V1ws])
            k.memset("pool", V1ws[0:4, 4, :, 64:65], 1.0, [t_V1ws])

            def branch_done(bidx):
                k.cp("act", osb[bidx][:], ps_o[0:65, 0:16], [t_pso], [t_osb[bidx]])
                k.mm(ps_m[:, 0:16], sel65[:], osb[bidx][:], True, True, [t_sel65, t_osb[bidx]], [t_psm])
                k.ts("dve", rden[bidx][:], ps_m[:, 0:16], 1e-30, None, ALU.max, None, [t_psm], [t_rden[bidx]])
                k.recip(rden[bidx][:], rden[bidx][:], [t_rden[bidx]], [t_rden[bidx]])

            def run_tiles(ntiles, kfn, extra_fn, vfn, rdk, rdv):
                done = 0
                while done < ntiles:
                    n8 = min(8, ntiles - done)
                    pi = cc_[0] % 2; cc_[0] += 1
                    pss, tps = ps_s8[pi], t_ps8[pi]
                    for t8 in range(n8):
                        kt = done + t8
                        ex = extra_fn(kt)
                        k.mm(pss[:, t8, :], kfn(kt), qt, True, len(ex) == 0, rdk + [t_QTs], [tps])
                        for ei, (l_, r_, rds) in enumerate(ex):
                            k.mm(pss[:, t8, :], l_, r_, False, ei == len(ex) - 1, rds, [tps])
                    p8, tp8 = P8[pi], t_P8[pi]
                    k.act(p8[:, 0:n8, :], pss[:, 0:n8, :], AF.Exp, [tps], [tp8], scale=0.125)
                    for t8 in range(n8):
                        kt = done + t8
                        k.mm(ps_o[0:65, 0:16], vfn(kt), p8[:, t8, :], kt == 0, kt == ntiles - 1, rdv + [tp8], [t_pso])
                    done += n8
                return p8, tp8

            for g in range(2):
                gs = slice(g * 64, (g + 1) * 64)
                qt = QTs[gs, bi, :, :].rearrange("p a b -> p (a b)")
                p8, tp8 = run_tiles(8, lambda kt: KcTs[gs, kt * 128:(kt + 1) * 128],
                                    lambda kt: ([(identb[:], sbias[:, 0, :], [t_identb, t_sbias])] if kt == 7 else []),
                                    lambda kt: V1cs[:, kt, g, :], [t_KcTs], [t_V1cs])
                branch_done(0)
                k.tt("dve", Pn8[:], p8[:], rden[0][:].unsqueeze(1).to_broadcast([128, 8, 16]), ALU.mult, [tp8, t_rden[0]], [t_Pn8])
                for ct in range(8):
                    for r in range(4):
                        k.mm(ps_m[0:4, 0:257], Pn8[:, ct, r * 4:(r + 1) * 4], ovs[:, ct, :], ct == 0 and r == 0, ct == 7 and r == 3,
                             [t_Pn8, t_ovs], [t_psm])
                k.tt("dve", imp2[:], ps_m[0:4, 0:257], Ms[:, 0, :], ALU.mult, [t_psm, t_Ms], [t_imp2])
                k.tt("dve", imp2[:], imp2[:], Ms[:, 1, :], ALU.add, [t_imp2, t_Ms], [t_imp2])
                P.op("dve", lambda q: q.max(out=mx[:, 0:8], in_=imp2[:]), [t_imp2], [t_mx])
                P.op("dve", lambda q: q.match_replace(out=imp3[:], in_to_replace=mx[:, 0:8], in_values=imp2[:], imm_value=-2e30),
                     [t_imp2, t_mx], [t_imp3])
                P.op("dve", lambda q: q.max(out=mx[:, 8:16], in_=imp3[:]), [t_imp3], [t_mx])
                k.ts("dve", thr[:], mx[:, 15:16], -1e29, None, ALU.max, None, [t_mx], [t_thr])
                k.ts("dve", imp3[:], imp2[:], thr[:, 0:1], 1.0, ALU.is_ge, ALU.subtract, [t_imp2, t_thr], [t_imp3])
                k.ts("dve", nsel[:, 0:257], imp3[:], 30000.0, None, ALU.mult, None, [t_imp3], [t_nsel])
                for jt in range(3):
                    k.tr(ps_t1[1][:, jt, 0:4], nsel[:, jt * 128:(jt + 1) * 128], identb[0:4, 0:4], [t_nsel, t_identb], [t_pt1[1]])
                k.cp("dve", nselT4[:], ps_t1[1][:, 0:3, 0:4].unsqueeze(2).to_broadcast([128, 3, 4, 4]), [t_pt1[1]], [t_nselT])
                def ex_slc(kt):
                    ex = [(Em[:, kt % 64, :], nselT4[:, kt // 64, :, :].rearrange("p a b -> p (a b)"), [t_Em, t_nselT])]
                    if kt == 128:
                        ex.append((identb[:], sbias[:, 1, :], [t_identb, t_sbias]))
                    return ex
                run_tiles(129, lambda kt: KsT[gs, kt * 128:(kt + 1) * 128], ex_slc, lambda kt: V1ss[:, kt, g, :], [t_big[0]], [t_big[1]])
                branch_done(1)
                run_tiles(5, lambda kt: KwT[gs, kt * 128:(kt + 1) * 128],
                          lambda kt: [(identb[:], sbias[:, 2 + kt, :], [t_identb, t_sbias])],
                          lambda kt: V1ws[:, kt, g, :], [t_KwT], [t_V1ws])
                branch_done(2)
                for n in range(3):
                    for r in range(4):
                        k.mm(ps_m[0:64, r * 4:(r + 1) * 4], selg[:, (g * 4 + r) * 3 + n, :], gates_s[:, bc], True, True,
                             [t_selg, t_gates_s], [t_psm])
                    k.tt("pool", tA[:], osb[n][0:64, :], rden[n][0:64, :], ALU.mult, [t_osb[n], t_rden[n]], [t_tA])
                    if n == 0:
                        k.tt("dve", acc[:], tA[:], ps_m[0:64, 0:16], ALU.mult, [t_tA, t_psm], [t_acc])
                    else:
                        k.tt("dve", tB[:], tA[:], ps_m[0:64, 0:16], ALU.mult, [t_tA, t_psm], [t_tB])
                        k.tt("pool", acc[:], acc[:], tB[:], ALU.add, [t_acc, t_tB], [t_acc])
                k.cp("dve", attn_s[:, g * 4:(g + 1) * 4, bc], acc[:].rearrange("p (a b) -> p a b", b=4), [t_acc], [t_attn_s])
    P.barrier()
    with ES() as e5:
        wo_a = k.sb("wo_a", [64, 8, 1024], BF16, e5); wo_s = k.sb("wo_s", [128, 4, 1024], BF16, e5); t_wo = T()
        k.load(wo_a[:], w_out_d[0:512, :].rearrange("(h d) n -> d h n", d=64), [t_wo], eng="pool")
        k.load(wo_s[:], w_out_d[512:1024, :].rearrange("(kt p) n -> p kt n", p=128), [t_wo], eng="pool")
        wglu = k.sb("wglu", [128, 4, 512], BF16, e5); t_wglu = T()
        k.load(wglu[:], w_glu_d.rearrange("(kt p) n -> p kt n", p=128), [t_wglu], eng="pool")
        fcols = k.sb("fcols", [128, 16], F32, e5); t_fcols = T()
        k.load(fcols[:], fcols_d, [t_fcols])
        xt2 = k.sb("xt2", [128, 8, 512], F32, e5); t_xt2 = T()
        x1 = k.sb("x1", [128, 8, 512], F32, e5); t_x1 = T()
        at = k.sb("at", [64, 8, 512], F32, e5); t_at = T()
        anb = k.sb("anb", [64, 8, 512], BF16, e5); t_anb = T()
        sqb = k.sb("sqb", [128, 8, 512], BF16, e5); t_sqb = T()
        h2 = k.sb("h2", [128, 8, 512], BF16, e5); t_h2 = T()
        ys = k.sb("ys", [128, 4, 512], F32, e5); t_ys = T()
        gsf = k.sb("gsf", [128, 4, 512], F32, e5); t_gsf = T()
        gsb = k.sb("gsb", [128, 4, 512], BF16, e5); t_gsb = T()
        snb = k.sb("snb", [128, 4, 512], BF16, e5); t_snb = T()
        hm = k.sb("hm", [128, 22, 512], BF16, e5); t_hm = T()
        wg = [k.sb(f"wg{i}", [128, 8, 128], BF16, e5) for i in range(2)]; t_wg = [T(), T()]
        wu = [k.sb(f"wu{i}", [128, 8, 128], BF16, e5) for i in range(2)]; t_wu = [T(), T()]
        wd = [k.sb(f"wd{i}", [128, 22, 128], BF16, e5) for i in range(2)]; t_wd = [T(), T()]
        tmpa = [k.sb(f"tmpa{i}", [128, 512], F32, e5) for i in range(2)]; t_tmpa = [T(), T()]
        rsd = k.sb("rsd", [128, 512], F32, e5); t_rsd = T()
        yo = [k.sb(f"yo{i}", [128, 512], F32, e5) for i in range(2)]; t_yo = [T(), T()]
        ps_a = [k.ps(f"ps_a{i}", [128, 512], F32, e5) for i in range(2)]; t_psa = [T(), T()]
        ps_u = [k.ps(f"ps_u{i}", [128, 512], F32, e5) for i in range(2)]; t_psu = [T(), T()]
        ps_q = k.ps("ps_q", [128, 512], F32, e5); t_psq = T()
        wgv = w_ffn_gate_d.rearrange("(kt p) n -> p kt n", p=128)
        wuv = w_ffn_up_d.rearrange("(kt p) n -> p kt n", p=128)
        wdv = w_ffn_down_d.rearrange("(f p) n -> p f n", p=128)
        yT_v = yT_out.rearrange("(kt p) n -> p kt n", p=128)
        xT_v2 = xT.rearrange("(kt p) n -> p kt n", p=128)
        tgs = x1[:, 0:4, :]; sgs = x1[:, 4:8, :]

        def rms_rstd(sq_tiles, rows, nfeat):
            n = len(sq_tiles)
            for ii, sq_ap in enumerate(sq_tiles):
                k.mm(ps_q[:], onesb[0:rows, :], sq_ap, ii == 0, ii == n - 1, [t_onesb, t_sqb], [t_psq])
            k.act(rsd[:], ps_q[:], AF.Sqrt, [t_psq], [t_rsd], bias=EPS, scale=1.0 / nfeat)
            k.recip(rsd[:], rsd[:], [t_rsd], [t_rsd])

        cnt = 0
        for tt_ in (ftiles if ftiles is not None else range(4)):
            oc0 = tt_ * 512
            c0 = (NW - OWN) + oc0
            k.load(xt2[:], xT_v2[:, :, c0:c0 + 512], [t_xt2])
            k.load(at[:], attn_d[:, :, oc0:oc0 + 512], [t_at], rd=[t_attnd])
            k.load(ys[:], yssm_d[:, :, oc0:oc0 + 512], [t_ys], rd=[t_yssm])
            k.tt("dve", tgs, ys[:], ys[:], ALU.mult, [t_ys], [t_x1])
            k.ts("dve", tgs, tgs, 0.044715, 1.0, ALU.mult, ALU.add, [t_x1], [t_x1])
            k.tt("dve", tgs, tgs, ys[:], ALU.mult, [t_x1, t_ys], [t_x1])
            k.act(sgs, tgs, AF.Sigmoid, [t_x1], [t_x1], scale=1.5957691216)
            k.tt("pool", gsf[:], ys[:], sgs, ALU.mult, [t_ys, t_x1], [t_gsf])
            k.cp("act", gsb[:], gsf[:], [t_gsf], [t_gsb])
            for oc in range(4):
                pa, tpa = ps_a[oc % 2], t_psa[oc % 2]
                for kt in range(4):
                    k.mm(pa[:], wglu[:, kt, oc * 128:(oc + 1) * 128], gsb[:, kt, :], kt == 0, kt == 3, [t_wglu, t_gsb], [tpa])
                ta, tta = tmpa[oc % 2], t_tmpa[oc % 2]
                k.act(ta[:], pa[:], AF.Sigmoid, [tpa, t_fcols], [tta], bias=fcols[:, oc:oc + 1])
                k.tt("pool", ys[:, oc, :], gsf[:, oc, :], ta[:], ALU.mult, [t_gsf, tta], [t_ys])
            k.act(sqb[:, 0:4, :], ys[:], AF.Square, [t_ys], [t_sqb])
            rms_rstd([sqb[:, kt, :] for kt in range(4)], 128, 512)
            for kt in range(4):
                k.stt(snb[:, kt, :], ys[:, kt, :], fcols[:, 4 + kt:5 + kt], rsd[:], ALU.mult, ALU.mult, [t_ys, t_fcols, t_rsd], [t_snb])
            k.act(sqb[0:64, :, :], at[:], AF.Square, [t_at], [t_sqb])
            rms_rstd([sqb[0:64, h, :] for h in range(8)], 64, 512)
            for h in range(8):
                k.stt(anb[:, h, :], at[:, h, :], fcols[0:64, 8 + h:9 + h], rsd[0:64, :], ALU.mult, ALU.mult, [t_at, t_fcols, t_rsd], [t_anb])
            for oc in range(8):
                pa, tpa = ps_a[oc % 2], t_psa[oc % 2]
                for h in range(8):
                    k.mm(pa[:], wo_a[:, h, oc * 128:(oc + 1) * 128], anb[:, h, :], h == 0, False, [t_wo, t_anb], [tpa])
                for kt in range(4):
                    k.mm(pa[:], wo_s[:, kt, oc * 128:(oc + 1) * 128], snb[:, kt, :], False, kt == 3, [t_wo, t_snb], [tpa])
                k.stt(x1[:, oc, :], pa[:], adaT[:, 2 * 8 + oc, 0:1], xt2[:, oc, :], ALU.mult, ALU.add, [tpa, t_ada, t_xt2], [t_x1])
            k.act(sqb[:], x1[:], AF.Square, [t_x1], [t_sqb])
            rms_rstd([sqb[:, kt, :] for kt in range(8)], 128, D)
            for kt in range(8):
                ta, tta = tmpa[kt % 2], t_tmpa[kt % 2]
                k.stt(ta[:], x1[:, kt, :], A2[:, kt, 0:1], rsd[:], ALU.mult, ALU.mult, [t_x1, t_A2, t_rsd], [tta])
                k.act(h2[:, kt, :], ta[:], AF.Identity, [tta, t_ada], [t_h2], bias=adaT[:, 3 * 8 + kt, 0:1])
            for f in range(22):
                wgb, twg = wg[f % 2], t_wg[f % 2]
                wub, twu = wu[f % 2], t_wu[f % 2]
                k.load(wgb[:], wgv[:, :, f * 128:(f + 1) * 128], [twg], eng="pool")
                k.load(wub[:], wuv[:, :, f * 128:(f + 1) * 128], [twu], eng="pool")
                pa, tpa = ps_a[f % 2], t_psa[f % 2]
                pu, tpu = ps_u[f % 2], t_psu[f % 2]
                for kt in range(8):
                    k.mm(pa[:], wgb[:, kt, :], h2[:, kt, :], kt == 0, kt == 7, [twg, t_h2], [tpa])
                for kt in range(8):
                    k.mm(pu[:], wub[:, kt, :], h2[:, kt, :], kt == 0, kt == 7, [twu, t_h2], [tpu])
                ta, tta = tmpa[f % 2], t_tmpa[f % 2]
                k.act(ta[:], pa[:], AF.Silu, [tpa], [tta])
                k.tt("dve", hm[:, f, :], pu[:], ta[:], ALU.mult, [tpu, tta], [t_hm])
            for oc in range(8):
                wdb, twd = wd[oc % 2], t_wd[oc % 2]
                k.load(wdb[:], wdv[:, :, oc * 128:(oc + 1) * 128], [twd], eng="pool")
                pa, tpa = ps_a[oc % 2], t_psa[oc % 2]
                for f in range(22):
                    k.mm(pa[:], wdb[:, f, :], hm[:, f, :], f == 0, f == 21, [twd, t_hm], [tpa])
                yb, tyb = yo[oc % 2], t_yo[oc % 2]
                k.stt(yb[:], pa[:], adaT[:, 5 * 8 + oc, 0:1], x1[:, oc, :], ALU.mult, ALU.add, [tpa, t_ada, t_x1], [tyb])
                k.store(yT_v[:, oc, oc0:oc0 + 512], yb[:], [tyb])

        if do_sample:
            N = 16
            xs2 = k.sb("xs2", [128, 8, N], F32, e5); t_xs2 = T()
            k.load(xs2[:], xsT_d, [t_xs2])
            f1 = k.sb("f1", [128, 8, N], F32, e5); t_f1 = T()
            f2 = k.sb("f2", [128, 8, N], F32, e5); t_f2 = T()
            gss = k.sb("gss", [128, 4, N], F32, e5); t_gss = T()
            gsbs = k.sb("gsbs", [128, 4, N], BF16, e5); t_gsbs = T()
            s2s = k.sb("s2s", [128, 4, N], F32, e5); t_s2s = T()
            sqs2 = k.sb("sqs2", [128, 8, N], BF16, e5); t_sqs2 = T()
            snbs = k.sb("snbs", [128, 4, N], BF16, e5); t_snbs = T()
            anbs = k.sb("anbs", [64, 8, N], BF16, e5); t_anbs = T()
            x1s = k.sb("x1s", [128, 8, N], F32, e5); t_x1s = T()
            h2s = k.sb("h2s", [128, 8, N], BF16, e5); t_h2s = T()
            hms = k.sb("hms", [128, 22, N], BF16, e5); t_hms = T()
            rs_s = k.sb("rs_s", [128, N], F32, e5); t_rs_s = T()
            ysT = k.sb("ysT", [128, 8, N], F32, e5); t_ysT = T()
            tmps = k.sb("tmps", [128, N], F32, e5); t_tmps = T()
            Y4 = yssm_s[:]
            k.tt("dve", f1[:, 0:4, :], Y4, Y4, ALU.mult, [t_yssm_s], [t_f1])
            k.ts("dve", f1[:, 0:4, :], f1[:, 0:4, :], 0.044715, 1.0, ALU.mult, ALU.add, [t_f1], [t_f1])
            k.tt("dve", f1[:, 0:4, :], f1[:, 0:4, :], Y4, ALU.mult, [t_f1, t_yssm_s], [t_f1])
            k.act(f2[:, 0:4, :], f1[:, 0:4, :], AF.Sigmoid, [t_f1], [t_f2], scale=1.5957691216)
            k.tt("dve", gss[:], Y4, f2[:, 0:4, :], ALU.mult, [t_yssm_s, t_f2], [t_gss])
            k.cp("act", gsbs[:], gss[:], [t_gss], [t_gsbs])
            for oc in range(4):
                for kt in range(4):
                    k.mm(ps_q[:, 0:N], wglu[:, kt, oc * 128:(oc + 1) * 128], gsbs[:, kt, :], kt == 0, kt == 3, [t_wglu, t_gsbs], [t_psq])
                k.act(tmps[:], ps_q[:, 0:N], AF.Sigmoid, [t_psq, t_fcols], [t_tmps], bias=fcols[:, oc:oc + 1])
                k.tt("dve", s2s[:, oc, :], gss[:, oc, :], tmps[:], ALU.mult, [t_gss, t_tmps], [t_s2s])

            def rstd_s(sq_list, rows, nfeat):
                n = len(sq_list)
                for ii, sq_ap in enumerate(sq_list):
                    k.mm(ps_q[:, 0:N], onesb[0:rows, :], sq_ap, ii == 0, ii == n - 1, [t_onesb, t_sqs2], [t_psq])
                k.act(rs_s[:], ps_q[:, 0:N], AF.Sqrt, [t_psq], [t_rs_s], bias=EPS, scale=1.0 / nfeat)
                k.recip(rs_s[:], rs_s[:], [t_rs_s], [t_rs_s])

            k.act(sqs2[:, 0:4, :], s2s[:], AF.Square, [t_s2s], [t_sqs2])
            rstd_s([sqs2[:, kt, :] for kt in range(4)], 128, 512)
            for kt in range(4):
                k.stt(snbs[:, kt, :], s2s[:, kt, :], fcols[:, 4 + kt:5 + kt], rs_s[:], ALU.mult, ALU.mult, [t_s2s, t_fcols, t_rs_s], [t_snbs])
            k.act(sqs2[0:64, :, :], attn_s[:], AF.Square, [t_attn_s], [t_sqs2])
            rstd_s([sqs2[0:64, h, :] for h in range(8)], 64, 512)
            for h in range(8):
                k.stt(anbs[:, h, :], attn_s[:, h, :], fcols[0:64, 8 + h:9 + h], rs_s[0:64, :], ALU.mult, ALU.mult,
                      [t_attn_s, t_fcols, t_rs_s], [t_anbs])
            for oc in range(8):
                for h in range(8):
                    k.mm(ps_q[:, 0:N], wo_a[:, h, oc * 128:(oc + 1) * 128], anbs[:, h, :], h == 0, False, [t_wo, t_anbs], [t_psq])
                for kt in range(4):
                    k.mm(ps_q[:, 0:N], wo_s[:, kt, oc * 128:(oc + 1) * 128], snbs[:, kt, :], False, kt == 3, [t_wo, t_snbs], [t_psq])
                k.tt("dve", tmps[:], ps_q[:, 0:N], mods[:, 2, oc, :], ALU.mult, [t_psq, t_mods], [t_tmps])
                k.tt("dve", x1s[:, oc, :], tmps[:], xs2[:, oc, :], ALU.add, [t_tmps, t_xs2], [t_x1s])
            k.act(sqs2[:], x1s[:], AF.Square, [t_x1s], [t_sqs2])
            rstd_s([sqs2[:, kt, :] for kt in range(8)], 128, D)
            k.tt("dve", f1[:], x1s[:], A2s[:], ALU.mult, [t_x1s, t_mods], [t_f1])
            k.tt("dve", f1[:], f1[:], rs_s[:].unsqueeze(1).to_broadcast([128, 8, N]), ALU.mult, [t_f1, t_rs_s], [t_f1])
            k.tt("dve", h2s[:], f1[:], mods[:, 3, :, :], ALU.add, [t_f1, t_mods], [t_h2s])
            for f in range(22):
                wgb, twg = wg[f % 2], t_wg[f % 2]
                wub, twu = wu[f % 2], t_wu[f % 2]
                k.load(wgb[:], wgv[:, :, f * 128:(f + 1) * 128], [twg], eng="pool")
                k.load(wub[:], wuv[:, :, f * 128:(f + 1) * 128], [twu], eng="pool")
                pa, tpa = ps_a[f % 2], t_psa[f % 2]
                pu, tpu = ps_u[f % 2], t_psu[f % 2]
                for kt in range(8):
                    k.mm(pa[:, 0:N], wgb[:, kt, :], h2s[:, kt, :], kt == 0, kt == 7, [twg, t_h2s], [tpa])
                for kt in range(8):
                    k.mm(pu[:, 0:N], wub[:, kt, :], h2s[:, kt, :], kt == 0, kt == 7, [twu, t_h2s], [tpu])
                k.act(tmps[:], pa[:, 0:N], AF.Silu, [tpa], [t_tmps])
                k.tt("dve", hms[:, f, :], pu[:, 0:N], tmps[:], ALU.mult, [tpu, t_tmps], [t_hms])
            for oc in range(8):
                wdb, twd = wd[oc % 2], t_wd[oc % 2]
                k.load(wdb[:], wdv[:, :, oc * 128:(oc + 1) * 128], [twd], eng="pool")
                pa, tpa = ps_a[oc % 2], t_psa[oc % 2]
                for f in range(22):
                    k.mm(pa[:, 0:N], wdb[:, f, :], hms[:, f, :], f == 0, f == 21, [twd, t_hms], [tpa])
                k.tt("dve", tmps[:], pa[:, 0:N], mods[:, 5, oc, :], ALU.mult, [tpa, t_mods], [t_tmps])
                k.tt("dve", ysT[:, oc, :], tmps[:], x1s[:, oc, :], ALU.add, [t_tmps, t_x1s], [t_ysT])
            k.store(ysT_out, ysT[:], [t_ysT])

    P.finalize()
    k.es.close()
    return nc, P


def _rope_tables(pos):
    half = 8
    inv = 500000.0 ** (-(np.arange(half, dtype=np.float64) * 2.0 / 16))
    ang = pos[None, :] * inv[:, None]
    C = np.ones((128, pos.shape[0]), np.float32)
    S = np.zeros((128, pos.shape[0]), np.float32)
    for base in (0, 64):
        C[base:base + 8] = np.cos(ang); C[base + 8:base + 16] = np.cos(ang)
        S[base:base + 8] = np.sin(ang); S[base + 8:base + 16] = np.sin(ang)
    return C, S


def _col(v):
    return np.ascontiguousarray(v.reshape(8, 128).T)


def prepare_inputs(inp):
    x_prompt = inp["x_prompt"]
    w_in = inp["w_in"][0]
    o1, o2, o3 = 512, 512 + 768, 512 + 768 + 24
    qperm = []
    for r in range(4):
        for g in range(2):
            h = g * 4 + r
            qperm += list(range(h * 64, (h + 1) * 64))
    perm = qperm + list(range(o1, o2)) + list(range(o3, NCOL)) + list(range(o2, o3))
    w_in_p = np.ascontiguousarray(w_in[:, perm])
    w_ada = np.ascontiguousarray(inp["w_ada"][0])
    b_adaT = np.ascontiguousarray(inp["b_ada"][0].reshape(48, 128).T)
    gcols = np.stack([_col(inp["norm_mix_g"][0]), _col(inp["norm_ffn_g"][0])], axis=2)
    gq = inp["q_norm_g"][0]
    gk = inp["k_norm_g"][0]
    gqk = np.stack([np.tile(gq, 2)] + [np.tile(gk[n], 2) for n in range(3)], axis=1).astype(np.float32)
    ident = np.eye(128, dtype=np.float32)
    Rt = np.zeros((128, 128), np.float32)
    for base in (0, 64):
        for d in range(8):
            Rt[base + d + 8, base + d] = -1.0
            Rt[base + d, base + d + 8] = 1.0
    def pj(a):
        return np.ascontiguousarray(a.reshape(16, 2, 64).transpose(1, 2, 0).reshape(128, 16))
    a_re = inp["ssm_a_re"][0]; a_im = inp["ssm_a_im"][0]
    logdt = np.repeat(inp["ssm_log_dt"][0][:, None], 64, axis=1)
    ssm_cols = np.stack([pj(a_re), pj(a_im), pj(logdt)], axis=2).astype(np.float32)
    kk1 = np.tile(np.arange(1, 129, dtype=np.float32)[None, :], (128, 1))
    b_re = inp["ssm_b_re"][0]; b_im = inp["ssm_b_im"][0]
    c_re = inp["ssm_c_re"][0]; c_im = inp["ssm_c_im"][0]
    LBre = np.zeros((128, 16, 128), np.float32); LBim = np.zeros_like(LBre)
    LCre = np.zeros_like(LBre); LCim = np.zeros_like(LBre)
    for j in range(16):
        for gs in range(2):
            g = 2 * j + gs
            gl = g % 8
            LBre[gl * 16:(gl + 1) * 16, j, gs * 64:(gs + 1) * 64] = b_re[g].T
            LBim[gl * 16:(gl + 1) * 16, j, gs * 64:(gs + 1) * 64] = b_im[g].T
            LCre[gs * 64:(gs + 1) * 64, j, gl * 16:(gl + 1) * 16] = c_re[g].T
            LCim[gs * 64:(gs + 1) * 64, j, gl * 16:(gl + 1) * 16] = c_im[g].T
    Dcol = np.ascontiguousarray(inp["ssm_d"][0].reshape(4, 128).T)

    bf = ml_dtypes.bfloat16
    NEGB = -30000.0
    cmp_peT = np.zeros((128, 32, 2), np.float32)
    for kv_, nm in enumerate(("cmp_pe_k", "cmp_pe_v")):
        pe = inp[nm][0]
        cmp_peT[0:64, :, kv_] = pe.T
        cmp_peT[64:128, :, kv_] = pe.T
    w2k = inp["cmp_w2_k"][0]
    cmp_w2kp = np.zeros((128, 2, 128), np.float32)
    cmp_w2kp[:, 0, 0:64] = w2k
    cmp_w2kp[:, 1, 64:128] = w2k
    cmp_w2v = np.ascontiguousarray(inp["cmp_w2_v"][0])
    Em = np.zeros((128, 64, 128), np.float32)
    for kt in range(64):
        for half in range(2):
            Em[2 * kt + half, kt, half * 64:(half + 1) * 64] = 1.0
    cs = np.arange(512) * 16
    ss = np.arange(128) * 64
    ovl = np.minimum(cs[:, None] + 32, ss[None, :] + 64) - np.maximum(cs[:, None], ss[None, :])
    ovl = (np.clip(ovl, 0, None) / 32.0).astype(np.float32)
    ovl[511] = 0.0
    ov = np.ascontiguousarray(ovl.reshape(4, 128, 128).transpose(1, 0, 2))
    selg = np.zeros((24, 24, 64), np.float32)
    for i_ in range(24):
        selg[i_, i_, :] = 1.0
    sel65 = np.zeros((65, 128), np.float32); sel65[64] = 1.0
    kk_ = np.arange(128)
    causal = np.where(kk_[:, None] <= kk_[None, :], 0.0, NEGB).astype(np.float32)
    causal4 = np.tile(causal, (1, 4))
    w_out_h = np.ascontiguousarray(inp["w_out"][0]); w_glu_h = np.ascontiguousarray(inp["w_glu"][0])
    wfg = np.ascontiguousarray(inp["w_ffn_gate"][0]); wfu = np.ascontiguousarray(inp["w_ffn_up"][0]); wfd = np.ascontiguousarray(inp["w_ffn_down"][0])
    fcols = np.zeros((128, 16), np.float32)
    fcols[:, 0:4] = inp["b_glu"][0].reshape(4, 128).T
    fcols[:, 4:8] = inp["ssm_out_g"][0].reshape(4, 128).T
    fcols[0:64, 8:16] = inp["attn_out_g"][0].reshape(8, 64).T
    cache_cmp2 = inp["cache_kv_cmp"][0].reshape(NPHYS * 128, 256)
    cache_slc2 = inp["cache_kv_slc"][0].reshape(NPHYS * 128, 256)
    cs2 = np.arange(1024) * 16
    ss2 = np.arange(257) * 64
    ov2 = np.minimum(cs2[:, None] + 32, ss2[None, :] + 64) - np.maximum(cs2[:, None], ss2[None, :])
    ov2 = (np.clip(ov2, 0, None) / 32.0).astype(np.float32)
    ov2[1023] = 0.0
    ovs = np.ascontiguousarray(ov2.reshape(8, 128, 257).transpose(1, 0, 2))
    Ms = np.zeros((4, 2, 257), np.float32)
    Ms[:, 0, :] = 1.0
    for jf in (0, 255, 256):
        Ms[:, 0, jf] = 0.0
        Ms[:, 1, jf] = 1e4
    sbias = np.zeros((128, 7, 16), np.float32)
    sbias[127, 0, :] = NEGB
    qcol = np.tile(np.arange(4), 4)
    for t_ in range(4):
        sbias[t_, 1, :] = np.where(t_ <= qcol, 0.0, NEGB)
        sbias[t_, 6, :] = np.where(t_ <= qcol, 0.0, NEGB)
        sbias[t_, 2, :] = np.where(t_ >= qcol, 0.0, NEGB)
    piota = np.arange(128, dtype=np.float32)[:, None].copy()
    Cs_, Ss_ = _rope_tables(16384.0 + np.arange(4, dtype=np.float64))
    ropes = np.stack([np.tile(Cs_, (1, 4)), np.tile(Ss_, (1, 4))], axis=1).astype(np.float32)
    maps = []
    for c in range(8):
        b, kq = c // 4, c % 4
        start = kq * OWN
        pad = (NW - OWN) - start
        xw = np.zeros((NW, D), np.float32)
        xw[pad:] = x_prompt[b, :start + OWN]
        xT = np.ascontiguousarray(xw.T)
        pos = np.arange(NW, dtype=np.float64) - pad
        C, S = _rope_tables(np.maximum(pos, 0.0))
        tv = np.zeros((128, NW), np.float32); tv[:, pad:] = 1.0
        cT = np.zeros((128, 8, 5), np.float32)
        cT[:, :, 0] = _col(inp["c_prompt"][b])
        for i in range(4):
            cT[:, :, 1 + i] = _col(inp["c_sample"][4 * c + i])
        qi = np.arange(128)
        cmpb = np.zeros((16, 128, 4, 512), np.float32)
        winb = np.zeros((16, 128, 5, 512), np.float32)
        M1 = np.zeros((16, 128, 128), np.float32); M2 = np.zeros((16, 128, 128), np.float32)
        j0 = pad // 64
        jj_ = np.arange(128)
        for i_ in range(16):
            qp = (NW - OWN) + 128 * i_ + qi
            for ct in range(4):
                cc = ct * 128 + np.arange(128)
                valid = (16 * cc[:, None] + 31 <= qp[None, :]) & (16 * cc[:, None] >= pad) & (cc[:, None] <= 510)
                cmpb[i_, :, ct, :] = np.tile(np.where(valid, 0.0, NEGB), (1, 4))
            for w in range(5):
                kp = (44 + i_ + w) * 128 + np.arange(128)
                valid = (kp[:, None] <= qp[None, :]) & (kp[:, None] >= qp[None, :] - 512) & (kp[:, None] >= pad)
                winb[i_, :, w, :] = np.tile(np.where(valid, 0.0, NEGB), (1, 4))
            cur = qp // 64
            fut = (jj_[None, :] > cur[:, None]) | (jj_[None, :] < j0)
            forced = ((jj_[None, :] == j0) | (jj_[None, :] == cur[:, None]) | (jj_[None, :] == cur[:, None] - 1)) & ~fut
            M1[i_] = np.where(fut | forced, 0.0, 1.0)
            M2[i_] = np.where(fut, -1e30, np.where(forced, 1e4, 0.0))
        xsT = np.ascontiguousarray(inp["x_sample"][4 * c:4 * c + 4].reshape(16, 8, 128).transpose(2, 1, 0))
        hs0 = np.zeros((128, 16, 4, 2), np.float32)
        for ri, nm in enumerate(("state_ssm_re", "state_ssm_im")):
            st_ = inp[nm][0, 4 * c:4 * c + 4]
            hs0[:, :, :, ri] = st_.reshape(4, 16, 2, 64).transpose(2, 3, 1, 0).reshape(128, 16, 4)
        ptab = np.ascontiguousarray(np.broadcast_to(inp["page_table"][4 * c:4 * c + 4].astype(np.int32)[None], (128, 4, 128)))
        cwin = np.ascontiguousarray(inp["cache_kv_win"][0, 4 * c:4 * c + 4].reshape(4, 512, 256))
        maps.append(dict(cache_cmp=cache_cmp2, cache_slc=cache_slc2, cache_win=cwin, ovs=ovs.astype(bf), Ms=Ms, sbias=sbias.astype(bf),
                         ptab=ptab, piota=piota, xsT=xsT, ropes=ropes, hs0=hs0, w_out=w_out_h, w_glu=w_glu_h, fcols=fcols, w_ffn_gate=wfg, w_ffn_up=wfu, w_ffn_down=wfd,
                         cmp_w1_k=np.ascontiguousarray(inp["cmp_w1_k"][0]), cmp_w1_v=np.ascontiguousarray(inp["cmp_w1_v"][0]),
                         cmp_peT=cmp_peT, cmp_w2kp=cmp_w2kp, cmp_w2v=cmp_w2v, Em=Em.astype(bf), ov=ov.astype(bf), selg=selg,
                         sel65=sel65, causal4=causal4.astype(bf), cmpb=cmpb.astype(bf), winb=winb.astype(bf), M1=M1, M2=M2,
                         xT=xT, ropeC=C, ropeS=S, tokvalid=tv, w_in=w_in_p, w_ada=w_ada, b_adaT=b_adaT, cT=cT,
                         gcols=gcols.astype(np.float32), gqk=gqk, ident=ident, Rt=Rt, ssm_cols=ssm_cols, kk1=kk1,
                         LBre=LBre, LBim=LBim, LCre=LCre, LCim=LCim, Dcol=Dcol))
    return maps


_CACHE = {}


def run_device(inp, dbg=False):
    if dbg not in _CACHE:
        _CACHE[dbg] = build_program(dbg)
    nc, P = _CACHE[dbg]
    maps = prepare_inputs(inp)
    res = run_bass_kernel_spmd(nc, maps, core_ids=list(range(8)))
    return res.results


def kernel(**inp):
    inp = {k_: np.asarray(v) for k_, v in inp.items()}
    res = run_device(inp)
    B, L = 2, 8192
    y_prompt = np.zeros((B, L, D), np.float32)
    y_sample = np.zeros((32, 4, D), np.float32)
    kv = np.zeros((3, B, L, 2, 2, 64), np.float32)
    kvs = np.zeros((3, 32, 4, 2, 2, 64), np.float32)
    ssm_p = np.zeros((2, 1, B, 32, 64), np.float32)
    ssm_s = np.zeros((2, 1, 32, 32, 64), np.float32)
    for c in range(8):
        b, kq = c // 4, c % 4
        r = res[c]
        y_prompt[b, kq * OWN:(kq + 1) * OWN] = np.asarray(r["yT_out"]).T
        y_sample[4 * c:4 * c + 4] = np.asarray(r["ysT_out"]).transpose(2, 1, 0).reshape(4, 4, D)
        rows = np.asarray(r["kvT_out"]).T.reshape(OWN, 3, 2, 2, 64)
        for n in range(3):
            kv[n, b, kq * OWN:(kq + 1) * OWN] = rows[:, n]
        rows_s = np.asarray(r["kvs_out"]).T.reshape(4, 4, 3, 2, 2, 64)
        for n in range(3):
            kvs[n, 4 * c:4 * c + 4] = rows_s[:, :, n]
        if kq == 3:
            so = np.asarray(r["ssm_out"])
            st = so.reshape(2, 64, 16, 2).transpose(2, 0, 1, 3).reshape(32, 64, 2)
            ssm_p[0, 0, b] = st[:, :, 0]
            ssm_p[1, 0, b] = st[:, :, 1]
        ss = np.asarray(r["ssm_s_out"])
        st = ss.reshape(2, 64, 16, 4, 2).transpose(3, 2, 0, 1, 4).reshape(4, 32, 64, 2)
        ssm_s[0, 0, 4 * c:4 * c + 4] = st[..., 0]
        ssm_s[1, 0, 4 * c:4 * c + 4] = st[..., 1]
    return (y_prompt, y_sample, kv[0][None], kv[1][None], np.ascontiguousarray(kv[2][None][:, :, L - 512:]),
            ssm_p[0], ssm_p[1], kvs[0][None], kvs[1][None], kvs[2][None], ssm_s[0], ssm_s[1])
```

```python
import contextlib
import math
import numpy as np
import ml_dtypes
import concourse.bass as bass
import concourse.mybir as mybir
from concourse.bass_utils import run_bass_kernel_spmd

F32 = mybir.dt.float32
BF16 = mybir.dt.bfloat16
I32 = mybir.dt.int32
AF = mybir.ActivationFunctionType
ALU = mybir.AluOpType
AX = mybir.AxisListType

NW = 8192
OWN = 2048
NT = NW // 512
OWN_T0 = (NW - OWN) // 512
EPS = 1e-6
D = 1024
NCOL = 1816
NPHYS = 5120
MAGIC = 12582912.0
TWO_PI = 2.0 * math.pi


class Tok:
    __slots__ = ("name", "writer", "readers")

    def __init__(self, name=""):
        self.name = name
        self.writer = None
        self.readers = []


class Ins:
    __slots__ = ("eng", "fn", "deps", "signal", "semval", "sem", "is_dma")

    def __init__(self, eng, fn, is_dma):
        self.eng = eng
        self.fn = fn
        self.deps = []
        self.signal = False
        self.semval = None
        self.sem = None
        self.is_dma = is_dma


class Prog:
    ENGS = ("pe", "act", "dve", "pool", "sp")

    def __init__(self, nc, n_dma_sems=32):
        self.nc = nc
        self.streams = {e: [] for e in self.ENGS}
        self.n_dma_sems = n_dma_sems
        self.all = []
        self.pending = {e: [] for e in self.ENGS}
        self.dmas_since = []

    def tok(self, name=""):
        return Tok(name)

    def toks(self, n, name=""):
        return [Tok(f"{name}{i}") for i in range(n)]

    def _add(self, eng, fn, rd, wr, is_dma):
        ins = Ins(eng, fn, is_dma)
        deps = set()
        for t in rd:
            if t.writer is not None:
                deps.add(t.writer)
        for t in wr:
            if t.writer is not None:
                deps.add(t.writer)
            for r in t.readers:
                deps.add(r)
        if self.pending[eng]:
            deps.update(self.pending[eng])
            self.pending[eng] = []
        ins.deps = list(deps)
        if is_dma:
            self.dmas_since.append(ins)
        for t in wr:
            t.writer = ins
            t.readers = []
        for t in rd:
            t.readers.append(ins)
        self.all.append(ins)
        self.streams[eng].append(ins)
        return ins

    def barrier(self):
        lasts = []
        for e in self.ENGS:
            for ins in reversed(self.streams[e]):
                if not ins.is_dma:
                    lasts.append(ins)
                    break
        lasts += self.dmas_since
        self.dmas_since = []
        for e in self.ENGS:
            self.pending[e] = list(self.pending[e]) + lasts

    def op(self, eng, fn, rd=(), wr=()):
        return self._add(eng, fn, rd, wr, False)

    def dma(self, eng, fn, rd=(), wr=()):
        return self._add(eng, fn, rd, wr, True)

    def finalize(self, final_wait_eng="sp"):
        nc = self.nc
        engobj = {"pe": nc.tensor, "act": nc.scalar, "dve": nc.vector, "pool": nc.gpsimd, "sp": nc.sync}
        for ins in self.all:
            for d in ins.deps:
                if d.eng == "pe" and ins.eng == "pe" and not d.is_dma and not ins.is_dma:
                    continue
                d.signal = True
            if ins.is_dma:
                ins.signal = True
        with contextlib.ExitStack() as es:
            csem = {e: es.enter_context(nc.semaphore(f"s_{e}")) for e in self.ENGS}
            qengs = sorted({ins.eng for ins in self.all if ins.is_dma})
            nper = {q: (self.n_dma_sems if q == "sp" else 8) for q in qengs}
            dsems = {q: [es.enter_context(nc.semaphore(f"d_{q}_{i}")) for i in range(nper[q])] for q in qengs}
            ccount = {e: 0 for e in self.ENGS}
            dcount = {q: [0] * nper[q] for q in qengs}
            dlast = {q: [None] * nper[q] for q in qengs}
            dnext = {q: 0 for q in qengs}
            prev_on_sem = {}
            for ins in self.all:
                if ins.is_dma:
                    q = ins.eng
                    k = dnext[q]
                    dnext[q] = (k + 1) % nper[q]
                    dcount[q][k] += 16
                    ins.sem = dsems[q][k]
                    ins.semval = dcount[q][k]
                    if dlast[q][k] is not None:
                        prev_on_sem[ins] = dlast[q][k]
                    dlast[q][k] = ins
                elif ins.signal:
                    ccount[ins.eng] += 1
                    ins.sem = csem[ins.eng]
                    ins.semval = ccount[ins.eng]
            for e in self.ENGS:
                eo = engobj[e]
                waited = {}
                for ins in self.streams[e]:
                    deps = list(ins.deps)
                    if ins in prev_on_sem:
                        deps.append(prev_on_sem[ins])
                    need = {}
                    for d in deps:
                        if d.eng == "pe" and e == "pe" and not d.is_dma and not ins.is_dma:
                            continue
                        key = id(d.sem)
                        if waited.get(key, 0) >= d.semval:
                            continue
                        if key not in need or need[key][1] < d.semval:
                            need[key] = (d.sem, d.semval)
                    for key, (sem, val) in need.items():
                        eo.wait_ge(sem, val)
                        waited[key] = val
                    bi = ins.fn(eo)
                    if ins.signal:
                        bi.then_inc(ins.sem, 16 if ins.is_dma else 1)
            eo = engobj[final_wait_eng]
            for q in qengs:
                for k in range(nper[q]):
                    if dcount[q][k] > 0:
                        eo.wait_ge(dsems[q][k], dcount[q][k])
            self.stats = dict(n_ins=len(self.all), counts=dict(ccount))


class K:
    def __init__(self, nc):
        self.nc = nc
        self.P = Prog(nc)
        self.es = contextlib.ExitStack()

    def dram_in(self, name, shape, dt=F32):
        return self.nc.dram_tensor(name, list(shape), dt, kind="ExternalInput").ap()

    def dram_out(self, name, shape, dt=F32):
        return self.nc.dram_tensor(name, list(shape), dt, kind="ExternalOutput").ap()

    def sb(self, name, shape, dt, es=None):
        return (es or self.es).enter_context(self.nc.sbuf_tensor("s_" + name, list(shape), dt))

    def ps(self, name, shape, dt=F32, es=None):
        return (es or self.es).enter_context(self.nc.psum_tensor("p_" + name, list(shape), dt))

    def scratch(self, name, shape, dt):
        return self.nc.dram_tensor(name, list(shape), dt).ap()

    def mm(self, out, lhsT, rhs, start, stop, rd, wr):
        self.P.op("pe", lambda q: q.matmul(out, lhsT=lhsT, rhs=rhs, start=start, stop=stop), rd, wr)

    def tr(self, out, in_, ident, rd, wr):
        self.P.op("pe", lambda q: q.transpose(out, in_, ident), rd, wr)

    def act(self, out, in_, func, rd, wr, bias=None, scale=None):
        kw = {}
        if bias is not None:
            kw["bias"] = bias
        if scale is not None:
            kw["scale"] = scale
        self.P.op("act", lambda q: q.activation(out=out, in_=in_, func=func, **kw), rd, wr)

    def tt(self, eng, out, in0, in1, op, rd, wr):
        self.P.op(eng, lambda q: q.tensor_tensor(out=out, in0=in0, in1=in1, op=op), rd, wr)

    def ts(self, eng, out, in0, s1, s2, op0, op1, rd, wr):
        if s2 is None:
            self.P.op(eng, lambda q: q.tensor_scalar(out=out, in0=in0, scalar1=s1, scalar2=None, op0=op0), rd, wr)
        else:
            self.P.op(eng, lambda q: q.tensor_scalar(out=out, in0=in0, scalar1=s1, scalar2=s2, op0=op0, op1=op1), rd, wr)

    def stt(self, out, in0, scalar, in1, op0, op1, rd, wr):
        self.P.op("dve", lambda q: q.scalar_tensor_tensor(out=out, in0=in0, scalar=scalar, in1=in1, op0=op0, op1=op1), rd, wr)

    def cp(self, eng, out, in_, rd, wr):
        if eng == "act":
            self.P.op("act", lambda q: q.activation(out=out, in_=in_, func=AF.Identity), rd, wr)
        else:
            self.P.op(eng, lambda q: q.tensor_copy(out=out, in_=in_), rd, wr)

    def recip(self, out, in_, rd, wr):
        self.P.op("dve", lambda q: q.reciprocal(out=out, in_=in_), rd, wr)

    def memset(self, eng, ap, val, wr):
        self.P.op(eng, lambda q: q.memset(ap, val), (), wr)

    def load(self, out, in_, wr, eng="sp", rd=()):
        self.P.dma(eng, lambda q: q.dma_start(out=out, in_=in_), rd, wr)

    def store(self, out, in_, rd, eng="sp"):
        self.P.dma(eng, lambda q: q.dma_start(out=out, in_=in_), rd, ())


def build_program(dbg=False, tlist=None, stop=9, tiles_override=None, noscr=False, chunks=None, ftiles=None, do_sample=True, skip23=False, nsb=4, no_prompt=False):
    tlist = list(range(NT)) if tlist is None else tlist
    nc = bass.Bass("TRN2", target_bir_lowering=False)
    k = K(nc)
    P = k.P
    T = P.tok
    ES = contextlib.ExitStack

    xT = k.dram_in("xT", [D, NW])
    ropeC = k.dram_in("ropeC", [128, NW])
    ropeS = k.dram_in("ropeS", [128, NW])
    tokvalid = k.dram_in("tokvalid", [128, NW])
    w_in_d = k.dram_in("w_in", [D, NCOL])
    w_ada_d = k.dram_in("w_ada", [D, 6 * D])
    b_adaT_d = k.dram_in("b_adaT", [128, 48])
    cT_d = k.dram_in("cT", [128, 8, 5])
    gcols_d = k.dram_in("gcols", [128, 8, 2])
    gqk_d = k.dram_in("gqk", [128, 4])
    ident_d = k.dram_in("ident", [128, 128])
    Rt_d = k.dram_in("Rt", [128, 128])
    ssmc_d = k.dram_in("ssm_cols", [128, 16, 3])
    kk1_d = k.dram_in("kk1", [128, 128])
    LB_d = [k.dram_in(n, [128, 16, 128]) for n in ("LBre", "LBim", "LCre", "LCim")]
    Dcol_d = k.dram_in("Dcol", [128, 4])
    cmp_w1_d = [k.dram_in(n, [2048, 128]) for n in ("cmp_w1_k", "cmp_w1_v")]
    cmp_peT_d = k.dram_in("cmp_peT", [128, 32, 2])
    cmp_w2kp_d = k.dram_in("cmp_w2kp", [128, 2, 128])
    cmp_w2v_d = k.dram_in("cmp_w2v", [128, 64])
    Em_d = k.dram_in("Em", [128, 64, 128], BF16)
    ov_d = k.dram_in("ov", [128, 4, 128], BF16)
    selg_d = k.dram_in("selg", [24, 24, 64])
    sel65_d = k.dram_in("sel65", [65, 128])
    causal4_d = k.dram_in("causal4", [128, 512], BF16)
    cmpb_d = k.dram_in("cmpb", [16, 128, 4, 512], BF16)
    winb_d = k.dram_in("winb", [16, 128, 5, 512], BF16)
    M1_d = k.dram_in("M1", [16, 128, 128])
    M2_d = k.dram_in("M2", [16, 128, 128])
    attn_out = k.dram_out("attn_out", [64, 8, OWN]) if dbg else None
    w_out_d = k.dram_in("w_out", [D, D])
    w_glu_d = k.dram_in("w_glu", [512, 512])
    fcols_d = k.dram_in("fcols", [128, 16])
    w_ffn_gate_d = k.dram_in("w_ffn_gate", [D, 2816])
    w_ffn_up_d = k.dram_in("w_ffn_up", [D, 2816])
    w_ffn_down_d = k.dram_in("w_ffn_down", [2816, D])
    yT_out = k.dram_out("yT_out", [D, OWN])
    xsT_d = k.dram_in("xsT", [128, 8, 16])
    cache_cmp_v = k.dram_in("cache_cmp", [NPHYS * 128, 256]) if do_sample else None
    cache_slc_v = k.dram_in("cache_slc", [NPHYS * 128, 256]) if do_sample else None
    cache_win_d = k.dram_in("cache_win", [4, 512, 256])
    ovs_d = k.dram_in("ovs", [128, 8, 257], BF16)
    Ms_d = k.dram_in("Ms", [4, 2, 257])
    sbias_d = k.dram_in("sbias", [128, 7, 16], BF16)
    ptab_d = k.dram_in("ptab", [128, 4, 128], I32)
    piota_d = k.dram_in("piota", [128, 1])
    ropes_d = k.dram_in("ropes", [128, 2, 16])
    hs0_d = k.dram_in("hs0", [128, 16, 4, 2])
    kvs_out = k.dram_out("kvs_out", [768, 16])
    ysT_out = k.dram_out("ysT_out", [128, 8, 16])
    ssm_s_out = k.dram_out("ssm_s_out", [128, 16, 4, 2])

    kvT_out = k.dram_out("kvT_out", [768, OWN])
    ssm_out = k.dram_out("ssm_out", [128, 16, 2])

    uT_bf_d = k.scratch("uT_bf_d", [128, 4, NW], BF16); t_uTd = T()
    uT_f_d = k.scratch("uT_f_d", [128, 4, OWN], F32); t_uTfd = T()
    KcmpT_d = k.scratch("KcmpT_d", [128, NW], BF16); t_Kcd = T()
    VcmpT_d = k.scratch("VcmpT_d", [128, NW], BF16); t_Vcd = T()

    ident = k.sb("ident", [128, 128], F32); t_ident = T()
    identb = k.sb("identb", [128, 128], BF16); t_identb = T()
    Rt = k.sb("Rt", [128, 128], F32); t_Rt = T()
    onesb = k.sb("onesb", [128, 128], BF16); t_onesb = T()
    blkones = k.sb("blkones", [128, 128], BF16); t_blk = T()
    k.load(ident[:], ident_d, [t_ident])
    k.load(Rt[:], Rt_d, [t_Rt])
    k.cp("dve", identb[:], ident[:], [t_ident], [t_identb])
    k.memset("pool", onesb[:], 1.0, [t_onesb])
    k.memset("pool", blkones[:], 0.0, [t_blk])
    k.memset("pool", blkones[0:64, 0:64], 1.0, [t_blk])
    k.memset("pool", blkones[64:128, 64:128], 1.0, [t_blk])

    adaT = k.sb("adaT", [128, 48, 5], F32); t_ada = T()
    gcols = k.sb("gcols", [128, 8, 2], F32); t_gcols = T()
    gqk = k.sb("gqk", [128, 4], F32); t_gqk = T()
    A1 = k.sb("A1", [128, 8, 5], F32); t_A1 = T()
    A2 = k.sb("A2", [128, 8, 5], F32); t_A2 = T()
    k.load(gcols[:], gcols_d, [t_gcols])
    k.load(gqk[:], gqk_d, [t_gqk])

    mods = k.sb("mods", [128, 6, 8, 16], F32); A1s = k.sb("A1s", [128, 8, 16], F32); A2s = k.sb("A2s", [128, 8, 16], F32); t_mods = T()
    gates_s = k.sb("gates_s", [24, 16], F32); t_gates_s = T()
    us_f = k.sb("us_f", [128, 4, 16], F32); us_b = k.sb("us_b", [128, 4, 16], BF16); t_us = T()
    QTs = k.sb("QTs", [128, 4, 4, 4], BF16); t_QTs = T()
    kvnew = k.sb("kvnew", [128, 4, 16], BF16); t_kvnew = T()
    yssm_s = k.sb("yssm_s", [128, 4, 16], F32); t_yssm_s = T()
    attn_s = k.sb("attn_s", [64, 8, 16], F32); t_attn_s = T()
    k.memset("pool", attn_s[:], 0.0, [t_attn_s])
    eA = ES()
    KslcT = k.sb("KslcT", [128, NW], BF16, eA); t_KslcT = T()
    KwinT = k.sb("KwinT", [128, 2560], BF16, eA); t_KwinT = T()
    QT = k.sb("QT", [128, 16, 4, 128], BF16, eA); t_QT = T()
    gates = k.sb("gates", [24, OWN], F32, eA); t_gates = T()
    yssm_d = k.scratch("yssm_d", [128, 4, OWN], F32); t_yssm = T()
    attn_d = k.scratch("attn_d", [64, 8, OWN], F32); t_attnd = T()
    V1s = k.sb("V1s", [128, 64, 2, 65], BF16, eA); t_V1s = T()
    V1w = k.sb("V1w", [128, 20, 2, 65], BF16, eA); t_V1w = T()
    KcT = k.sb("KcT", [128, 512], BF16, eA); t_KcT = T()
    V1c = k.sb("V1c", [128, 4, 2, 65], BF16, eA); t_V1c = T()
    k.memset("pool", V1s[:], 1.0, [t_V1s])
    k.memset("pool", V1w[:], 1.0, [t_V1w])
    k.memset("pool", V1c[:], 1.0, [t_V1c])
    hst = k.sb("hst", [128, 16, 4], F32, eA); t_hst = T()
    k.memset("dve", hst[:], 0.0, [t_hst])

    with ES() as e0:
        cT = k.sb("cT", [128, 8, 5], F32, e0); t_cT = T()
        scb = k.sb("scb", [128, 8, 5], BF16, e0); t_scb = T()
        b_adaT = k.sb("b_adaT", [128, 48], F32, e0); t_bada = T()
        k.load(cT[:], cT_d, [t_cT])
        k.load(b_adaT[:], b_adaT_d, [t_bada])
        k.act(scb[:], cT[:], AF.Silu, [t_cT], [t_scb])
        ps_ada = k.ps("ps_ada", [128, 48, 5], F32, e0); t_psada = T()
        wada = [k.sb(f"wada{i}", [128, 8, 1024], BF16, e0) for i in range(2)]
        t_wada = [T(), T()]
        w_ada_v = w_ada_d.rearrange("(kt p) n -> p kt n", p=128)
        for i in range(6):
            wb = wada[i % 2]; tw = t_wada[i % 2]
            k.load(wb[:], w_ada_v[:, :, i * 1024:(i + 1) * 1024], [tw], eng="pool")
            for ko in range(8):
                j = i * 8 + ko
                for kt in range(8):
                    k.mm(ps_ada[:, j, :], wb[:, kt, ko * 128:(ko + 1) * 128], scb[:, kt, :], kt == 0, kt == 7,
                         [tw, t_scb], [t_psada])
        k.tt("dve", adaT[:], ps_ada[:], b_adaT[:].unsqueeze(2).to_broadcast([128, 48, 5]), ALU.add,
             [t_psada, t_bada], [t_ada])
        for (A, tA, si, gi) in ((A1, t_A1, 1, 0), (A2, t_A2, 4, 1)):
            k.ts("dve", A[:], adaT[:, si * 8:(si + 1) * 8, :], 1.0, None, ALU.add, None, [t_ada], [tA])
            k.tt("dve", A[:], A[:], gcols[:, :, gi:gi + 1].to_broadcast([128, 8, 5]), ALU.mult, [tA, t_gcols], [tA])

    for i6 in range(6):
        k.cp("dve", mods[:, i6, :, :].rearrange("p k (b q) -> p k b q", q=4),
             adaT[:, i6 * 8:(i6 + 1) * 8, 1:5].unsqueeze(3).to_broadcast([128, 8, 4, 4]), [t_ada], [t_mods])
    k.cp("dve", A1s[:].rearrange("p k (b q) -> p k b q", q=4), A1[:, :, 1:5].unsqueeze(3).to_broadcast([128, 8, 4, 4]), [t_A1], [t_mods])
    k.cp("dve", A2s[:].rearrange("p k (b q) -> p k b q", q=4), A2[:, :, 1:5].unsqueeze(3).to_broadcast([128, 8, 4, 4]), [t_A2], [t_mods])
    P.barrier()
    with ES() as e1:
        w_in = k.sb("w_in", [128, 8, NCOL], BF16, e1); t_win = T()
        for kt_ in range(8):
            k.load(w_in[:, kt_, :], w_in_d[kt_ * 128:(kt_ + 1) * 128, :], [t_win], eng="pool")
        xt = k.sb("xt", [128, 8, 512], F32, e1); t_xt = T()
        hb = k.sb("hb", [128, 8, 512], BF16, e1); t_hb = T()
        tmpf = [k.sb(f"tmpf{i}", [128, 512], F32, e1) for i in range(2)]; t_tmpf = [T(), T()]
        rstd = k.sb("rstd", [128, 512], F32, e1); t_rstd = T()
        cst = k.sb("cst", [128, 512], F32, e1); t_cst = T()
        sst = k.sb("sst", [128, 512], F32, e1); t_sst = T()
        tvl = k.sb("tvl", [128, 512], F32, e1); t_tvl = T()
        sqz = k.sb("sqz", [128, 512], BF16, e1); t_sqz = T()
        rs2 = k.sb("rs2", [128, 512], F32, e1); t_rs2 = T()
        zn = k.sb("zn", [128, 512], F32, e1); t_zn = T()
        r1 = k.sb("r1", [128, 512], F32, e1); t_r1 = T()
        r2 = k.sb("r2", [128, 512], F32, e1); t_r2 = T()
        zraw = k.sb("zraw", [128, 512], F32, e1); t_zraw = T()
        zo = [k.sb(f"zo{i}", [128, 512], F32, e1) for i in range(2)]; t_zo = [T(), T()]
        zb = [k.sb(f"zb{i}", [128, 512], BF16, e1) for i in range(2)]; t_zb = [T(), T()]
        ps_ss = k.ps("ps_ss", [128, 512], F32, e1); t_pss = T()
        ps_z = [k.ps(f"ps_z{i}", [128, 512], F32, e1) for i in range(2)]; t_psz = [T(), T()]
        ps_n = k.ps("ps_n", [128, 512], F32, e1); t_psn = T()
        ps_r = k.ps("ps_r", [128, 512], F32, e1); t_psr = T()
        ps_tb = k.ps("ps_tb", [128, 4, 128], BF16, e1); t_pstb = T()
        xT_v = xT.rearrange("(kt p) n -> p kt n", p=128)
        zc = 0
        for t in (tlist if stop >= 1 else []):
            own = t >= OWN_T0
            c0 = t * 512
            oc0 = (t - OWN_T0) * 512
            k.load(xt[:], xT_v[:, :, c0:c0 + 512], [t_xt])
            k.load(cst[:], ropeC[:, c0:c0 + 512], [t_cst])
            k.load(sst[:], ropeS[:, c0:c0 + 512], [t_sst])
            if not own:
                k.load(tvl[:], tokvalid[:, c0:c0 + 512], [t_tvl])
            k.act(hb[:], xt[:], AF.Square, [t_xt], [t_hb])
            for kt in range(8):
                k.mm(ps_ss[:], onesb[:], hb[:, kt, :], kt == 0, kt == 7, [t_onesb, t_hb], [t_pss])
            k.act(rstd[:], ps_ss[:], AF.Sqrt, [t_pss], [t_rstd], bias=EPS, scale=1.0 / D)
            k.recip(rstd[:], rstd[:], [t_rstd], [t_rstd])
            for kt in range(8):
                tf = tmpf[kt % 2]; ttf = t_tmpf[kt % 2]
                k.stt(tf[:], xt[:, kt, :], A1[:, kt, 0:1], rstd[:], ALU.mult, ALU.mult, [t_xt, t_A1, t_rstd], [ttf])
                k.act(hb[:, kt, :], tf[:], AF.Identity, [ttf, t_ada], [t_hb], bias=adaT[:, kt, 0:1])
            tiles = list(range(4, 14)) + ([0, 1, 2, 3, 14] if own else [])
            if tiles_override is not None:
                tiles = tiles_override
            for m in tiles:
                pz = ps_z[zc % 2]; tpz = t_psz[zc % 2]
                zz = zo[zc % 2]; tzz = t_zo[zc % 2]
                zzb = zb[zc % 2]; tzzb = t_zb[zc % 2]
                zc += 1
                ncols = 128 if m < 14 else 24
                col0 = m * 128
                for kt in range(8):
                    k.mm(pz[0:ncols, :], w_in[:, kt, col0:col0 + ncols], hb[:, kt, :], kt == 0, kt == 7,
                         [t_win, t_hb], [tpz])
                if m == 14:
                    k.act(gates[:, oc0:oc0 + 512], pz[0:24, :], AF.Sigmoid, [tpz], [t_gates])
                    continue
                if 10 <= m < 14:
                    u_i = m - 10
                    if own:
                        k.cp("act", zz[:], pz[:], [tpz], [tzz])
                        if not noscr:
                            k.store(uT_f_d[:, u_i, oc0:oc0 + 512], zz[:], [tzz])
                        k.cp("pool", zzb[:], zz[:], [tzz], [tzzb])
                    else:
                        k.tt("dve", zzb[:], pz[:], tvl[:], ALU.mult, [tpz, t_tvl], [tzzb])
                    if not noscr:
                        P.dma("sp", lambda q, o=uT_bf_d[:, u_i, c0:c0 + 512], i_=zzb[:]: q.dma_start(out=o, in_=i_), [tzzb], [t_uTd])
                    continue
                is_k = (m < 4) or (m in (4, 6, 8))
                if is_k:
                    gi = 0 if m < 4 else 1 + (m - 4) // 2
                    k.cp("act", zraw[:], pz[:], [tpz], [t_zraw])
                    k.act(sqz[:], zraw[:], AF.Square, [t_zraw], [t_sqz])
                    k.mm(ps_n[:], blkones[:], sqz[:], True, True, [t_blk, t_sqz], [t_psn])
                    k.act(rs2[:], ps_n[:], AF.Sqrt, [t_psn], [t_rs2], bias=EPS, scale=1.0 / 64)
                    k.recip(rs2[:], rs2[:], [t_rs2], [t_rs2])
                    k.stt(zn[:], zraw[:], gqk[:, gi:gi + 1], rs2[:], ALU.mult, ALU.mult, [t_zraw, t_gqk, t_rs2], [t_zn])
                    k.mm(ps_r[:], Rt[:], zn[:], True, True, [t_Rt, t_zn], [t_psr])
                    k.tt("pool", r1[:], zn[:], cst[:], ALU.mult, [t_zn, t_cst], [t_r1])
                    k.tt("dve", r2[:], ps_r[:], sst[:], ALU.mult, [t_psr, t_sst], [t_r2])
                    k.tt("pool", zz[:], r1[:], r2[:], ALU.add, [t_r1, t_r2], [tzz])
                else:
                    k.cp("act", zz[:], pz[:], [tpz], [tzz])
                if m < 4:
                    k.cp("act", QT[:, oc0 // 128:oc0 // 128 + 4, m, :], zz[:].rearrange("p (a b) -> p a b", b=128), [tzz], [t_QT])
                    continue
                if own:
                    k.store(kvT_out[(m - 4) * 128:(m - 3) * 128, oc0:oc0 + 512], zz[:], [tzz])
                if m == 6:
                    k.cp("act", KslcT[:, c0:c0 + 512], zz[:], [tzz], [t_KslcT])
                elif m == 8 and c0 + 512 > NW - 2560:
                    w0 = c0 - (NW - 2560)
                    k.cp("act", KwinT[:, w0:w0 + 512], zz[:], [tzz], [t_KwinT])
                elif m in (4, 5):
                    k.cp("act", zzb[:], zz[:], [tzz], [tzzb])
                    dd, td = (KcmpT_d, t_Kcd) if m == 4 else (VcmpT_d, t_Vcd)
                    P.dma("sp", lambda q, o=dd[:, c0:c0 + 512], i_=zzb[:]: q.dma_start(out=o, in_=i_), [tzzb], [td])
                if m == 7 or (m == 9 and c0 + 512 > NW - 2560):
                    k.cp("act", zzb[:], zz[:], [tzz], [tzzb])
                    for s4 in range(4):
                        k.tr(ps_tb[:, s4, :], zzb[:, s4 * 128:(s4 + 1) * 128], identb[:], [tzzb, t_identb], [t_pstb])
                    if m == 7:
                        dstv, tdv, kt0 = V1s, t_V1s, c0 // 128
                    else:
                        dstv, tdv, kt0 = V1w, t_V1w, (c0 - (NW - 2560)) // 128
                    k.cp("dve", dstv[:, kt0:kt0 + 4, :, 0:64], ps_tb[:].rearrange("p a (g d) -> p a g d", g=2), [t_pstb], [tdv])

        if do_sample:
            xs = k.sb("xs", [128, 8, 16], F32, e1); t_xs = T()
            sqs = k.sb("sqs", [128, 8, 16], BF16, e1); t_sqs = T()
            hsb = k.sb("hsb", [128, 8, 16], BF16, e1); t_hsb = T()
            hsf = k.sb("hsf", [128, 8, 16], F32, e1); t_hsf = T()
            rss = k.sb("rss", [128, 16], F32, e1); t_rss = T()
            rcs = k.sb("rcs", [128, 2, 16], F32, e1); t_rcs = T()
            k.load(xs[:], xsT_d, [t_xs])
            k.load(rcs[:], ropes_d, [t_rcs])
            k.act(sqs[:], xs[:], AF.Square, [t_xs], [t_sqs])
            for kt in range(8):
                k.mm(ps_ss[:, 0:16], onesb[:], sqs[:, kt, :], kt == 0, kt == 7, [t_onesb, t_sqs], [t_pss])
            k.act(rss[:], ps_ss[:, 0:16], AF.Sqrt, [t_pss], [t_rss], bias=EPS, scale=1.0 / D)
            k.recip(rss[:], rss[:], [t_rss], [t_rss])
            k.tt("dve", hsf[:], xs[:], A1s[:], ALU.mult, [t_xs, t_mods], [t_hsf])
            k.tt("dve", hsf[:], hsf[:], rss[:].unsqueeze(1).to_broadcast([128, 8, 16]), ALU.mult, [t_hsf, t_rss], [t_hsf])
            k.tt("dve", hsb[:], hsf[:], mods[:, 0, :, :], ALU.add, [t_hsf, t_mods], [t_hsb])
            for m in range(15):
                pz = ps_z[zc % 2]; tpz = t_psz[zc % 2]
                zz = zo[zc % 2]; tzz = t_zo[zc % 2]
                zc += 1
                ncols = 128 if m < 14 else 24
                col0 = m * 128
                for kt in range(8):
                    k.mm(pz[0:ncols, 0:16], w_in[:, kt, col0:col0 + ncols], hsb[:, kt, :], kt == 0, kt == 7, [t_win, t_hsb], [tpz])
                if m == 14:
                    k.act(gates_s[:], pz[0:24, 0:16], AF.Sigmoid, [tpz], [t_gates_s])
                    continue
                if 10 <= m < 14:
                    k.cp("act", us_f[:, m - 10, :], pz[:, 0:16], [tpz], [t_us])
                    k.cp("act", us_b[:, m - 10, :], us_f[:, m - 10, :], [t_us], [t_us])
                    continue
                is_k = (m < 4) or (m in (4, 6, 8))
                Z = zz[:, 0:16]
                if is_k:
                    gi = 0 if m < 4 else 1 + (m - 4) // 2
                    k.cp("act", zraw[:, 0:16], pz[:, 0:16], [tpz], [t_zraw])
                    k.act(sqz[:, 0:16], zraw[:, 0:16], AF.Square, [t_zraw], [t_sqz])
                    k.mm(ps_n[:, 0:16], blkones[:], sqz[:, 0:16], True, True, [t_blk, t_sqz], [t_psn])
                    k.act(rs2[:, 0:16], ps_n[:, 0:16], AF.Sqrt, [t_psn], [t_rs2], bias=EPS, scale=1.0 / 64)
                    k.recip(rs2[:, 0:16], rs2[:, 0:16], [t_rs2], [t_rs2])
                    k.stt(zn[:, 0:16], zraw[:, 0:16], gqk[:, gi:gi + 1], rs2[:, 0:16], ALU.mult, ALU.mult, [t_zraw, t_gqk, t_rs2], [t_zn])
                    k.mm(ps_r[:, 0:16], Rt[:], zn[:, 0:16], True, True, [t_Rt, t_zn], [t_psr])
                    k.tt("pool", r1[:, 0:16], zn[:, 0:16], rcs[:, 0, :], ALU.mult, [t_zn, t_rcs], [t_r1])
                    k.tt("dve", r2[:, 0:16], ps_r[:, 0:16], rcs[:, 1, :], ALU.mult, [t_psr, t_rcs], [t_r2])
                    k.tt("pool", Z, r1[:, 0:16], r2[:, 0:16], ALU.add, [t_r1, t_r2], [tzz])
                else:
                    k.cp("act", Z, pz[:, 0:16], [tpz], [tzz])
                if m < 4:
                    k.cp("act", QTs[:, :, m, :], Z.rearrange("p (b q) -> p b q", q=4), [tzz], [t_QTs])
                    continue
                k.store(kvs_out[(m - 4) * 128:(m - 3) * 128, :], Z, [tzz])
                if m in (6, 7, 8, 9):
                    k.cp("act", kvnew[:, m - 6, :], Z, [tzz], [t_kvnew])
    P.barrier()
    with ES() as e2:
        ssmc = k.sb("ssmc", [128, 16, 3], F32, e2); t_ssmc = T()
        kk1 = k.sb("kk1", [128, 128], F32, e2); t_kk1 = T()
        k.load(ssmc[:], ssmc_d, [t_ssmc])
        k.load(kk1[:], kk1_d, [t_kk1])
        LB = []
        t_LB = T()
        for i, dd in enumerate(LB_d):
            tl = k.sb(f"LB{i}", [128, 16, 128], BF16, e2)
            k.load(tl[:], dd, [t_LB], eng="pool")
            LB.append(tl)
        LBre, LBim, LCre, LCim = LB
        Dcol = k.sb("Dcol", [128, 4], F32, e2); t_Dcol = T()
        k.load(Dcol[:], Dcol_d, [t_Dcol])
        sm = k.sb("sm", [128, 16, 12], F32, e2); t_sm = T()
        DT, ARD, R_, TH, LBR, LBI, DEN, NRE, FRE, FIM, TMP1, TMP2 = [sm[:, :, i] for i in range(12)]
        a_re = ssmc[:, :, 0]; a_im = ssmc[:, :, 1]; logdt = ssmc[:, :, 2]
        Er = k.sb("Er", [128, 16, 128], F32, e2); Ei = k.sb("Ei", [128, 16, 128], F32, e2)
        Fr = k.sb("Fr", [128, 16, 128], F32, e2); Fi = k.sb("Fi", [128, 16, 128], F32, e2)
        Rm = k.sb("Rm", [128, 16, 128], F32, e2)
        t_tab = T()
        ang = k.sb("ang", [128, 16, 128], F32, e2); t_ang = T()
        ang2 = k.sb("ang2", [128, 16, 128], F32, e2); t_ang2 = T()
        k.act(DT, logdt, AF.Exp, [t_ssmc], [t_sm])
        k.tt("dve", ARD, a_re, DT, ALU.mult, [t_ssmc, t_sm], [t_sm])
        k.act(R_, ARD, AF.Exp, [t_sm], [t_sm])
        k.tt("dve", TH, a_im, DT, ALU.mult, [t_ssmc, t_sm], [t_sm])
        for j in range(16):
            k.ts("dve", ang[:, j, :], kk1[:], sm[:, j, 3:4], None, ALU.mult, None, [t_kk1, t_sm], [t_ang])

        def sin_table(dst, shift):
            src = ang
            ts_ = t_ang
            if shift != 0.0:
                k.ts("dve", ang2[:], ang[:], shift, None, ALU.add, None, [t_ang], [t_ang2])
                src = ang2
                ts_ = t_ang2
            k.ts("dve", dst[:], src[:], 1.0 / TWO_PI, MAGIC, ALU.mult, ALU.add, [ts_], [t_tab])
            k.ts("dve", dst[:], dst[:], MAGIC, None, ALU.subtract, None, [t_tab], [t_tab])
            k.stt(dst[:], dst[:], -TWO_PI, src[:], ALU.mult, ALU.add, [t_tab, ts_], [t_tab])
            k.ts("dve", dst[:], dst[:], 3.1415925, -3.1415925, ALU.min, ALU.max, [t_tab], [t_tab])
            k.act(dst[:], dst[:], AF.Sin, [t_tab], [t_tab])

        sin_table(Ei, 0.0)
        sin_table(Er, math.pi / 2)
        k.tt("dve", LBR, R_, Er[:, :, 0], ALU.mult, [t_sm, t_tab], [t_sm])
        k.tt("dve", LBI, R_, Ei[:, :, 0], ALU.mult, [t_sm, t_tab], [t_sm])
        k.tt("dve", DEN, a_re, a_re, ALU.mult, [t_ssmc], [t_sm])
        k.tt("dve", TMP1, a_im, a_im, ALU.mult, [t_ssmc], [t_sm])
        k.tt("dve", DEN, DEN, TMP1, ALU.add, [t_sm], [t_sm])
        k.recip(DEN, DEN, [t_sm], [t_sm])
        k.ts("dve", NRE, LBR, -1.0, None, ALU.add, None, [t_sm], [t_sm])
        k.tt("dve", TMP1, NRE, a_re, ALU.mult, [t_sm, t_ssmc], [t_sm])
        k.tt("dve", TMP2, LBI, a_im, ALU.mult, [t_sm, t_ssmc], [t_sm])
        k.tt("dve", FRE, TMP1, TMP2, ALU.add, [t_sm], [t_sm])
        k.tt("dve", FRE, FRE, DEN, ALU.mult, [t_sm], [t_sm])
        k.tt("dve", TMP1, LBI, a_re, ALU.mult, [t_sm, t_ssmc], [t_sm])
        k.tt("dve", TMP2, NRE, a_im, ALU.mult, [t_sm, t_ssmc], [t_sm])
        k.tt("dve", FIM, TMP1, TMP2, ALU.subtract, [t_sm], [t_sm])
        k.tt("dve", FIM, FIM, DEN, ALU.mult, [t_sm], [t_sm])
        fre_b = sm[:, :, 8:9].to_broadcast([128, 16, 128])
        fim_b = sm[:, :, 9:10].to_broadcast([128, 16, 128])
        k.tt("dve", Fr[:], Er[:], fre_b, ALU.mult, [t_tab, t_sm], [t_tab])
        k.tt("dve", ang[:], Ei[:], fim_b, ALU.mult, [t_tab, t_sm], [t_ang])
        k.tt("dve", Fr[:], Fr[:], ang[:], ALU.add, [t_tab, t_ang], [t_tab])
        k.tt("dve", Fi[:], Er[:], fim_b, ALU.mult, [t_tab, t_sm], [t_tab])
        k.tt("dve", ang[:], Ei[:], fre_b, ALU.mult, [t_tab, t_sm], [t_ang])
        k.tt("dve", Fi[:], Fi[:], ang[:], ALU.subtract, [t_tab, t_ang], [t_tab])
        k.cp("dve", Rm[:], sm[:, :, 2:3].to_broadcast([128, 16, 128]), [t_sm], [t_tab])
        k.memset("dve", Rm[:, :, 0:1], 0.0, [t_tab])

        ub = k.sb("ub", [128, 4, 512], BF16, e2); t_ub = T()
        uf = k.sb("uf", [128, 4, 512], F32, e2); t_uf = T()
        ga = k.sb("ga", [128, 4, 128], F32, e2); t_ga = T()
        gb = k.sb("gb", [128, 4, 128], F32, e2); t_gb = T()
        gri = k.sb("gri", [128, 4, 128], F32, e2); t_gri = T()
        gii = k.sb("gii", [128, 4, 128], F32, e2); t_gii = T()
        gre = k.sb("gre", [128, 4, 128], F32, e2); t_gre = T()
        gim = k.sb("gim", [128, 4, 128], F32, e2); t_gim = T()
        hre = k.sb("hre", [128, 4, 128], F32, e2); t_hre = T()
        him = k.sb("him", [128, 4, 128], F32, e2); t_him = T()
        hbre = k.sb("hbre", [128, 4, 128], BF16, e2); t_hbre = T()
        hbim = k.sb("hbim", [128, 4, 128], BF16, e2); t_hbim = T()
        ps_bre = k.ps("ps_bre", [128, 4, 128], F32, e2); t_pbre = T()
        ps_bim = k.ps("ps_bim", [128, 4, 128], F32, e2); t_pbim = T()
        ps_y = k.ps("ps_y", [128, 128], F32, e2); t_psy = T()
        yt = k.sb("yt", [128, 4, 512], F32, e2); t_yt = T()

        def fl(ap):
            return ap.rearrange("p a b -> p (a b)")

        for t in (tlist if stop >= 2 else []):
            own = t >= OWN_T0
            c0 = t * 512
            oc0 = (t - OWN_T0) * 512
            k.load(ub[:], uT_bf_d[:, :, c0:c0 + 512], [t_ub], rd=[t_uTd])
            if own:
                k.load(uf[:], uT_f_d[:, :, oc0:oc0 + 512], [t_uf], rd=[t_uTfd])
            for s in range(4):
                sc0 = s * 128
                for hf in range(4):
                    js = list(range(hf * 4, hf * 4 + 4))
                    hs = slice(hf * 4, hf * 4 + 4)
                    for jj, j in enumerate(js):
                        k.mm(ps_bre[:, jj, :], LBre[:, j, :], ub[:, hf, sc0:sc0 + 128], True, True, [t_LB, t_ub], [t_pbre])
                        k.mm(ps_bim[:, jj, :], LBim[:, j, :], ub[:, hf, sc0:sc0 + 128], True, True, [t_LB, t_ub], [t_pbim])
                    Frh = Fr[:, hs, :]; Fih = Fi[:, hs, :]
                    Erh = Er[:, hs, :]; Eih = Ei[:, hs, :]
                    k.tt("dve", ga[:], ps_bre[:], Frh, ALU.mult, [t_pbre, t_tab], [t_ga])
                    k.tt("dve", gb[:], ps_bim[:], Fih, ALU.mult, [t_pbim, t_tab], [t_gb])
                    k.tt("pool", gri[:], ga[:], gb[:], ALU.subtract, [t_ga, t_gb], [t_gri])
                    k.tt("dve", ga[:], ps_bre[:], Fih, ALU.mult, [t_pbre, t_tab], [t_ga])
                    k.tt("dve", gb[:], ps_bim[:], Frh, ALU.mult, [t_pbim, t_tab], [t_gb])
                    k.tt("pool", gii[:], ga[:], gb[:], ALU.add, [t_ga, t_gb], [t_gii])
                    k.tt("dve", gri[:, :, 0], gri[:, :, 0], hst[:, hs, 2], ALU.add, [t_gri, t_hst], [t_gri])
                    k.tt("dve", gii[:, :, 0], gii[:, :, 0], hst[:, hs, 3], ALU.add, [t_gii, t_hst], [t_gii])
                    Rmh = fl(Rm[:, hs, :])
                    P.op("dve", lambda q, o=fl(gre[:]), d0=Rmh, d1=fl(gri[:]):
                         q.tensor_tensor_scan(out=o, data0=d0, data1=d1, initial=0.0, op0=ALU.mult, op1=ALU.add),
                         [t_tab, t_gri], [t_gre])
                    P.op("dve", lambda q, o=fl(gim[:]), d0=Rmh, d1=fl(gii[:]):
                         q.tensor_tensor_scan(out=o, data0=d0, data1=d1, initial=0.0, op0=ALU.mult, op1=ALU.add),
                         [t_tab, t_gii], [t_gim])
                    cs = slice(0, 128) if own else slice(127, 128)
                    k.tt("pool", ga[:, :, cs], gre[:, :, cs], Erh[:, :, cs], ALU.mult, [t_gre, t_tab], [t_ga])
                    k.tt("pool", gb[:, :, cs], gim[:, :, cs], Eih[:, :, cs], ALU.mult, [t_gim, t_tab], [t_gb])
                    k.tt("dve", hre[:, :, cs], ga[:, :, cs], gb[:, :, cs], ALU.subtract, [t_ga, t_gb], [t_hre])
                    k.tt("pool", ga[:, :, cs], gre[:, :, cs], Eih[:, :, cs], ALU.mult, [t_gre, t_tab], [t_ga])
                    k.tt("pool", gb[:, :, cs], gim[:, :, cs], Erh[:, :, cs], ALU.mult, [t_gim, t_tab], [t_gb])
                    k.tt("dve", him[:, :, cs], ga[:, :, cs], gb[:, :, cs], ALU.add, [t_ga, t_gb], [t_him])
                    k.cp("dve", hst[:, hs, 0], hre[:, :, 127], [t_hre], [t_hst])
                    k.cp("dve", hst[:, hs, 1], him[:, :, 127], [t_him], [t_hst])
                    k.tt("dve", hst[:, hs, 2], hre[:, :, 127], sm[:, hs, 2], ALU.mult, [t_hre, t_sm], [t_hst])
                    k.tt("dve", hst[:, hs, 3], him[:, :, 127], sm[:, hs, 2], ALU.mult, [t_him, t_sm], [t_hst])
                    if own:
                        k.cp("act", hbre[:], hre[:], [t_hre], [t_hbre])
                        k.act(hbim[:], him[:], AF.Copy, [t_him], [t_hbim], scale=-1.0)
                        for jj, j in enumerate(js):
                            k.mm(ps_y[:], LCre[:, j, :], hbre[:, jj, :], jj == 0, False, [t_LB, t_hbre], [t_psy])
                            k.mm(ps_y[:], LCim[:, j, :], hbim[:, jj, :], False, jj == 3, [t_LB, t_hbim], [t_psy])
                        k.stt(yt[:, hf, sc0:sc0 + 128], uf[:, hf, sc0:sc0 + 128], Dcol[:, hf:hf + 1], ps_y[:],
                              ALU.mult, ALU.add, [t_uf, t_Dcol, t_psy], [t_yt])
            if own:
                P.dma("sp", lambda q, o=yssm_d[:, :, oc0:oc0 + 512], i_=yt[:]: q.dma_start(out=o, in_=i_), [t_yt], [t_yssm])
        k.store(ssm_out[:, :, :], hst[:, :, 0:2], [t_hst])

        if do_sample:
            hs0 = k.sb("hs0", [128, 16, 4, 2], F32, e2); t_hs0 = T()
            k.load(hs0[:], hs0_d, [t_hs0])
            rh0 = k.sb("rh0", [128, 16, 4, 2], F32, e2); t_rh0 = T()
            k.tt("dve", rh0[:], hs0[:], sm[:, :, 2:3].unsqueeze(3).to_broadcast([128, 16, 4, 2]), ALU.mult, [t_hs0, t_sm], [t_rh0])
            Rms = k.sb("Rms", [128, 16, 4, 4], F32, e2); t_Rms = T()
            k.cp("dve", Rms[:], sm[:, :, 2:3].unsqueeze(3).to_broadcast([128, 16, 4, 4]), [t_sm], [t_Rms])
            k.memset("dve", Rms[:, :, :, 0:1], 0.0, [t_Rms])
            hso = k.sb("hso", [128, 16, 4, 2], F32, e2); t_hso = T()
            sw = [k.sb(f"sw{i}", [128, 4, 4, 4], F32, e2) for i in range(8)]; t_sw = [T() for _ in range(8)]
            shb = [k.sb(f"shb{i}", [128, 4, 16], BF16, e2) for i in range(2)]; t_shb = [T(), T()]
            sga, sgb, sgri, sgii, sgre, sgim, shre, shim = sw
            tga, tgb, tgri, tgii, tgre, tgim, thre, thim = t_sw

            def fl2(ap):
                return ap.rearrange("p a b c -> p (a b c)")

            def b4(ap):
                return ap.unsqueeze(2).to_broadcast([128, 4, 4, 4])

            for hf in range(4):
                hs = slice(hf * 4, hf * 4 + 4)
                for jj in range(4):
                    j = hf * 4 + jj
                    k.mm(ps_bre[:, jj, 0:16], LBre[:, j, :], us_b[:, hf, :], True, True, [t_LB, t_us], [t_pbre])
                    k.mm(ps_bim[:, jj, 0:16], LBim[:, j, :], us_b[:, hf, :], True, True, [t_LB, t_us], [t_pbim])
                bre4 = ps_bre[:, :, 0:16].rearrange("p j (b q) -> p j b q", q=4)
                bim4 = ps_bim[:, :, 0:16].rearrange("p j (b q) -> p j b q", q=4)
                Fr4 = b4(Fr[:, hs, 0:4]); Fi4 = b4(Fi[:, hs, 0:4]); Er4 = b4(Er[:, hs, 0:4]); Ei4 = b4(Ei[:, hs, 0:4])
                k.tt("dve", sga[:], bre4, Fr4, ALU.mult, [t_pbre, t_tab], [tga])
                k.tt("dve", sgb[:], bim4, Fi4, ALU.mult, [t_pbim, t_tab], [tgb])
                k.tt("dve", sgri[:], sga[:], sgb[:], ALU.subtract, [tga, tgb], [tgri])
                k.tt("dve", sga[:], bre4, Fi4, ALU.mult, [t_pbre, t_tab], [tga])
                k.tt("dve", sgb[:], bim4, Fr4, ALU.mult, [t_pbim, t_tab], [tgb])
                k.tt("dve", sgii[:], sga[:], sgb[:], ALU.add, [tga, tgb], [tgii])
                k.tt("dve", sgri[:, :, :, 0], sgri[:, :, :, 0], rh0[:, hs, :, 0], ALU.add, [tgri, t_rh0], [tgri])
                k.tt("dve", sgii[:, :, :, 0], sgii[:, :, :, 0], rh0[:, hs, :, 1], ALU.add, [tgii, t_rh0], [tgii])
                Rmsh = fl2(Rms[:, hs, :, :])
                P.op("dve", lambda q, o=fl2(sgre[:]), d0=Rmsh, d1=fl2(sgri[:]):
                     q.tensor_tensor_scan(out=o, data0=d0, data1=d1, initial=0.0, op0=ALU.mult, op1=ALU.add), [t_Rms, tgri], [tgre])
                P.op("dve", lambda q, o=fl2(sgim[:]), d0=Rmsh, d1=fl2(sgii[:]):
                     q.tensor_tensor_scan(out=o, data0=d0, data1=d1, initial=0.0, op0=ALU.mult, op1=ALU.add), [t_Rms, tgii], [tgim])
                k.tt("dve", sga[:], sgre[:], Er4, ALU.mult, [tgre, t_tab], [tga])
                k.tt("dve", sgb[:], sgim[:], Ei4, ALU.mult, [tgim, t_tab], [tgb])
                k.tt("dve", shre[:], sga[:], sgb[:], ALU.subtract, [tga, tgb], [thre])
                k.tt("dve", sga[:], sgre[:], Ei4, ALU.mult, [tgre, t_tab], [tga])
                k.tt("dve", sgb[:], sgim[:], Er4, ALU.mult, [tgim, t_tab], [tgb])
                k.tt("dve", shim[:], sga[:], sgb[:], ALU.add, [tga, tgb], [thim])
                k.cp("dve", hso[:, hs, :, 0], shre[:, :, :, 3], [thre], [t_hso])
                k.cp("dve", hso[:, hs, :, 1], shim[:, :, :, 3], [thim], [t_hso])
                k.cp("act", shb[0][:], shre[:].rearrange("p j b q -> p j (b q)"), [thre], [t_shb[0]])
                k.act(shb[1][:], shim[:].rearrange("p j b q -> p j (b q)"), AF.Copy, [thim], [t_shb[1]], scale=-1.0)
                for jj in range(4):
                    j = hf * 4 + jj
                    k.mm(ps_y[:, 0:16], LCre[:, j, :], shb[0][:, jj, :], jj == 0, False, [t_LB, t_shb[0]], [t_psy])
                    k.mm(ps_y[:, 0:16], LCim[:, j, :], shb[1][:, jj, :], False, jj == 3, [t_LB, t_shb[1]], [t_psy])
                k.stt(yssm_s[:, hf, :], us_f[:, hf, :], Dcol[:, hf:hf + 1], ps_y[:, 0:16], ALU.mult, ALU.add, [t_us, t_Dcol, t_psy], [t_yssm_s])
            k.store(ssm_s_out, hso[:], [t_hso])
    if skip23:
        P.finalize()
        eA.close()
        k.es.close()
        return nc, P
    if not no_prompt:
        P.barrier()
        with ES() as e3:
            Xc = [k.sb(f"Xc{i}", [128, NW], BF16, e3) for i in range(2)]; t_Xc = [T(), T()]
            k.load(Xc[0][:], KcmpT_d, [t_Xc[0]], rd=[t_Kcd])
            k.load(Xc[1][:], VcmpT_d, [t_Xc[1]], rd=[t_Vcd])
            w1 = [k.sb(f"w1_{i}", [128, 32, 128], BF16, e3) for i in range(2)]; t_w1 = T()
            for i in range(2):
                src = cmp_w1_d[i].rearrange("(r d) h -> d r h", d=64)
                k.load(w1[i][0:64], src, [t_w1], eng="pool")
                k.load(w1[i][64:128], src, [t_w1], eng="pool")
            peT = k.sb("peT", [128, 32, 2], BF16, e3); t_peT = T()
            k.load(peT[:], cmp_peT_d, [t_peT], eng="pool")
            w2kp = k.sb("w2kp", [128, 2, 128], BF16, e3); t_w2 = T()
            w2v = k.sb("w2v", [128, 64], BF16, e3)
            k.load(w2kp[:], cmp_w2kp_d, [t_w2], eng="pool")
            k.load(w2v[:], cmp_w2v_d, [t_w2], eng="pool")
            pebias = k.sb("pebias", [128, 2], F32, e3); t_peb = T()
            ps_pb = k.ps("ps_pb", [128, 2], F32, e3); t_pspb = T()
            ps_pre = k.ps("ps_pre", [128, 512], F32, e3); t_pspre = T()
            ps_kc = k.ps("ps_kc", [128, 512], F32, e3); t_pskc = T()
            ps_vc = k.ps("ps_vc", [128, 64], F32, e3); t_psvc = T()
            for kv in range(2):
                for r in range(32):
                    k.mm(ps_pb[:, kv:kv + 1], w1[kv][0:64, r, :], peT[0:64, r, kv:kv + 1], r == 0, r == 31, [t_w1, t_peT], [t_pspb])
            k.cp("dve", pebias[:], ps_pb[:], [t_pspb], [t_peb])
            xg = k.sb("xg", [128, 512], F32, e3); t_xg = T()
            tg = k.sb("tg", [128, 512], F32, e3); t_tg = T()
            sg = k.sb("sg", [128, 512], F32, e3); t_sg = T()
            gh = [k.sb(f"gh{i}", [128, 512], BF16, e3) for i in range(2)]; t_gh = [T(), T()]
            for g in range(2):
                k.memset("dve", gh[g][:], 0.0, [t_gh[g]])
            for kv in range(2):
                for g in range(2):
                    gs = slice(g * 64, (g + 1) * 64)
                    for r in range(32):
                        k.mm(ps_pre[:, 0:511], w1[kv][gs, r, :], Xc[kv][gs, r:r + 16 * 510 + 1:16], r == 0, r == 31,
                             [t_w1, t_Xc[kv]], [t_pspre])
                    k.act(xg[:, 0:511], ps_pre[:, 0:511], AF.Identity, [t_pspre, t_peb], [t_xg], bias=pebias[:, kv:kv + 1])
                    k.tt("dve", tg[:, 0:511], xg[:, 0:511], xg[:, 0:511], ALU.mult, [t_xg], [t_tg])
                    k.ts("dve", tg[:, 0:511], tg[:, 0:511], 0.044715, 1.0, ALU.mult, ALU.add, [t_tg], [t_tg])
                    k.tt("dve", tg[:, 0:511], tg[:, 0:511], xg[:, 0:511], ALU.mult, [t_tg, t_xg], [t_tg])
                    k.act(sg[:, 0:511], tg[:, 0:511], AF.Sigmoid, [t_tg], [t_sg], scale=1.5957691216)
                    k.tt("dve", gh[g][:, 0:511], xg[:, 0:511], sg[:, 0:511], ALU.mult, [t_xg, t_sg], [t_gh[g]])
                    if kv == 1:
                        for ct in range(4):
                            k.mm(ps_vc[:], gh[g][:, ct * 128:(ct + 1) * 128], w2v[:], True, True, [t_gh[g], t_w2], [t_psvc])
                            k.cp("dve", V1c[:, ct, g, 0:64], ps_vc[:], [t_psvc], [t_V1c])
                if kv == 0:
                    for g in range(2):
                        k.mm(ps_kc[:], w2kp[:, g, :], gh[g][:], g == 0, g == 1, [t_w2, t_gh[g]], [t_pskc])
                    k.cp("dve", KcT[:], ps_kc[:], [t_pskc], [t_KcT])

        P.barrier()
        with ES() as e4:
            Em = k.sb("Em", [128, 64, 128], BF16, e4); t_Em = T()
            ov = k.sb("ov", [128, 4, 128], BF16, e4); t_ov = T()
            selg = k.sb("selg", [24, 24, 64], F32, e4); t_selg = T()
            sel65 = k.sb("sel65", [65, 128], F32, e4); t_sel65 = T()
            causal4 = k.sb("causal4", [128, 512], BF16, e4); t_causal = T()
            k.load(Em[:], Em_d, [t_Em]); k.load(ov[:], ov_d, [t_ov]); k.load(selg[:], selg_d, [t_selg])
            k.load(sel65[:], sel65_d, [t_sel65]); k.load(causal4[:], causal4_d, [t_causal])
            cmpb = k.sb("cmpb", [128, 4, 512], BF16, e4); t_cmpb = T()
            winb = k.sb("winb", [128, 5, 512], BF16, e4); t_winb = T()
            M1 = k.sb("M1", [128, 128], F32, e4); M2 = k.sb("M2", [128, 128], F32, e4); t_M = T()
            Pc = [k.sb(f"Pc{i}", [128, 512], BF16, e4) for i in range(4)]; t_Pc = [T() for _ in range(4)]
            Pn = [k.sb(f"Pn{i}", [128, 512], BF16, e4) for i in range(4)]; t_Pn = [T() for _ in range(4)]
            Pt = [k.sb(f"Pt{i}", [128, 512], BF16, e4) for i in range(3)]; t_Pt = [T() for _ in range(3)]
            osb = [k.sb(f"osb{i}", [65, 512], F32, e4) for i in range(3)]; t_osb = [T() for _ in range(3)]
            rden = [k.sb(f"rden{i}", [128, 512], F32, e4) for i in range(3)]; t_rden = [T() for _ in range(3)]
            imp2 = k.sb("imp2", [128, 128], F32, e4); t_imp2 = T()
            imp3 = k.sb("imp3", [128, 128], F32, e4); t_imp3 = T()
            mx = k.sb("mx", [128, 16], F32, e4); t_mx = T()
            thr = k.sb("thr", [128, 1], F32, e4); t_thr = T()
            nsel = k.sb("nsel", [128, 128], BF16, e4); t_nsel = T()
            nselT4 = k.sb("nselT4", [128, 4, 128], BF16, e4); t_nselT = T()
            acc = k.sb("acc", [64, 512], F32, e4); t_acc = T()
            tA = k.sb("tA", [64, 512], F32, e4); t_tA = T()
            tB = k.sb("tB", [64, 512], F32, e4); t_tB = T()
            ps_s = [k.ps(f"ps_s{i}", [128, 512], F32, e4) for i in range(3)]; t_pss_ = [T() for _ in range(3)]
            ps_o = k.ps("ps_o", [65, 512], F32, e4); t_pso = T()
            ps_den = k.ps("ps_den", [128, 512], F32, e4); t_psden = T()
            ps_imp = k.ps("ps_imp", [128, 128], F32, e4); t_psimp = T()
            ps_tt = k.ps("ps_tt", [128, 128], BF16, e4); t_pstt = T()
            ps_g = k.ps("ps_g", [64, 512], F32, e4); t_psg = T()
            sc_ = [0]

            def score_tile(lhsT_k, qt, extra, rd_extra):
                i3 = sc_[0] % 3
                sc_[0] += 1
                pss, tps = ps_s[i3], t_pss_[i3]
                n = len(extra)
                k.mm(pss[:], lhsT_k, qt, True, n == 0, rd_extra + [t_QT], [tps])
                for ei, (l_, r_, rds) in enumerate(extra):
                    k.mm(pss[:], l_, r_, False, ei == n - 1, rds, [tps])
                return pss, tps

            def finish_branch(bi):
                k.cp("act", osb[bi][:], ps_o[:], [t_pso], [t_osb[bi]])
                k.mm(ps_den[:], sel65[:], osb[bi][:], True, True, [t_sel65, t_osb[bi]], [t_psden])
                k.ts("dve", rden[bi][:], ps_den[:], 1e-30, None, ALU.max, None, [t_psden], [t_rden[bi]])
                k.recip(rden[bi][:], rden[bi][:], [t_rden[bi]], [t_rden[bi]])

            def run_branch(n, kfn, exfn, rdk, vfn, rdv, ptile, LA=2):
                scored = {}

                def emit_score(j):
                    scored[j] = score_tile(kfn(j), qt_cur[0], exfn(j), rdk)

                for j in range(min(LA, n)):
                    emit_score(j)
                for j in range(n):
                    pss, tps = scored.pop(j)
                    pt, tpt = ptile(j)
                    k.act(pt[:], pss[:], AF.Exp, [tps], [tpt], scale=0.125)
                    if j + LA < n:
                        emit_score(j + LA)
                    k.mm(ps_o[:], vfn(j), pt[:], j == 0, j == n - 1, rdv + [tpt], [t_pso])

            qt_cur = [None]
            for i in (chunks if chunks is not None else range(16)):
                ktd = 48 + i
                k.load(cmpb[:], cmpb_d[i], [t_cmpb])
                k.load(winb[:], winb_d[i], [t_winb])
                k.load(M1[:], M1_d[i], [t_M]); k.load(M2[:], M2_d[i], [t_M])
                for g in range(2):
                    gs = slice(g * 64, (g + 1) * 64)
                    qt = QT[gs, i, :, :].rearrange("p a b -> p (a b)")
                    qt_cur[0] = qt
                    run_branch(4, lambda ct: KcT[gs, ct * 128:(ct + 1) * 128],
                               lambda ct: [(identb[:], cmpb[:, ct, :], [t_identb, t_cmpb])], [t_KcT],
                               lambda ct: V1c[:, ct, g, :], [t_V1c], lambda ct: (Pc[ct], t_Pc[ct]))
                    finish_branch(0)
                    for ct in range(4):
                        k.tt("pool", Pn[ct][:], Pc[ct][:], rden[0][:], ALU.mult, [t_Pc[ct], t_rden[0]], [t_Pn[ct]])
                    for ct in range(4):
                        for r in range(4):
                            k.mm(ps_imp[:], Pn[ct][:, r * 128:(r + 1) * 128], ov[:, ct, :], ct == 0 and r == 0, ct == 3 and r == 3,
                                 [t_Pn[ct], t_ov], [t_psimp])
                    k.tt("dve", imp2[:], ps_imp[:], M1[:], ALU.mult, [t_psimp, t_M], [t_imp2])
                    k.tt("dve", imp2[:], imp2[:], M2[:], ALU.add, [t_imp2, t_M], [t_imp2])
                    P.op("dve", lambda q, o_=mx[:, 0:8], i_=imp2[:]: q.max(out=o_, in_=i_), [t_imp2], [t_mx])
                    P.op("dve", lambda q, o_=imp3[:], m_=mx[:, 0:8], i_=imp2[:]: q.match_replace(out=o_, in_to_replace=m_, in_values=i_, imm_value=-2e30),
                         [t_imp2, t_mx], [t_imp3])
                    P.op("dve", lambda q, o_=mx[:, 8:16], i_=imp3[:]: q.max(out=o_, in_=i_), [t_imp3], [t_mx])
                    k.ts("dve", thr[:], mx[:, 15:16], -1e29, None, ALU.max, None, [t_mx], [t_thr])
                    k.ts("dve", imp3[:], imp2[:], thr[:, 0:1], 1.0, ALU.is_ge, ALU.subtract, [t_imp2, t_thr], [t_imp3])
                    k.ts("dve", nsel[:], imp3[:], 30000.0, None, ALU.mult, None, [t_imp3], [t_nsel])
                    k.tr(ps_tt[:], nsel[:], identb[:], [t_nsel, t_identb], [t_pstt])
                    k.cp("dve", nselT4[:], ps_tt[:].unsqueeze(1).to_broadcast([128, 4, 128]), [t_pstt], [t_nselT])
                    nsT = nselT4[:].rearrange("p a b -> p (a b)")
                    def ex_slc_p(kt):
                        extra = [(Em[:, kt, :], nsT, [t_Em, t_nselT])]
                        if kt == ktd:
                            extra.append((identb[:], causal4[:], [t_identb, t_causal]))
                        return extra
                    run_branch(ktd + 1, lambda kt: KslcT[gs, kt * 128:(kt + 1) * 128], ex_slc_p, [t_KslcT],
                               lambda kt: V1s[:, kt, g, :], [t_V1s], lambda kt: (Pt[kt % 3], t_Pt[kt % 3]))
                    finish_branch(1)
                    run_branch(5, lambda w: KwinT[gs, (i + w) * 128:(i + w + 1) * 128],
                               lambda w: [(identb[:], winb[:, w, :], [t_identb, t_winb])], [t_KwinT],
                               lambda w: V1w[:, i + w, g, :], [t_V1w], lambda w: (Pt[w % 3], t_Pt[w % 3]))
                    finish_branch(2)
                    for n in range(3):
                        for r in range(4):
                            k.mm(ps_g[:, r * 128:(r + 1) * 128], selg[:, (g * 4 + r) * 3 + n, :], gates[:, i * 128:(i + 1) * 128],
                                 True, True, [t_selg, t_gates], [t_psg])
                        k.tt("pool", tA[:], osb[n][0:64, :], rden[n][0:64, :], ALU.mult, [t_osb[n], t_rden[n]], [t_tA])
                        if n == 0:
                            k.tt("dve", acc[:], tA[:], ps_g[:], ALU.mult, [t_tA, t_psg], [t_acc])
                        else:
                            k.tt("dve", tB[:], tA[:], ps_g[:], ALU.mult, [t_tA, t_psg], [t_tB])
                            k.tt("pool", acc[:], acc[:], tB[:], ALU.add, [t_acc, t_tB], [t_acc])
                    P.dma("sp", lambda q, o=attn_d[:, g * 4:(g + 1) * 4, i * 128:(i + 1) * 128], i_=acc[:].rearrange("p (a b) -> p a b", b=128):
                          q.dma_start(out=o, in_=i_), [t_acc], [t_attnd])
            if dbg:
                att_sb = k.sb("att_sb", [64, 8, 128], F32, e4); t_attsb = T()
                for i in (chunks if chunks is not None else range(16)):
                    k.load(att_sb[:], attn_d[:, :, i * 128:(i + 1) * 128], [t_attsb], rd=[t_attnd])
                    k.store(attn_out[:, :, i * 128:(i + 1) * 128], att_sb[:], [t_attsb])

    eA.close()
    P.barrier()
    if do_sample:
      with ES() as e6:
        NS = 16896
        big = k.sb("big", [128, 2, NS], BF16, e6); t_big = [T(), T()]
        XK = big[:, 0, :]; XV = big[:, 1, :]; KsT = big[:, 0, :]
        V1ss = big[:, 1, 0:129 * 130].rearrange("p (t g e) -> p t g e", g=2, e=65)
        Em = k.sb("Em2", [128, 64, 128], BF16, e6); t_Em = T()
        k.load(Em[:], Em_d, [t_Em])
        w1 = [k.sb(f"w1s_{i}", [128, 32, 128], BF16, e6) for i in range(2)]; t_w1 = T()
        for i in range(2):
            src = cmp_w1_d[i].rearrange("(r d) h -> d r h", d=64)
            k.load(w1[i][0:64], src, [t_w1], eng="pool")
            k.load(w1[i][64:128], src, [t_w1], eng="pool")
        peT = k.sb("peT2", [128, 32, 2], BF16, e6); t_peT = T()
        k.load(peT[:], cmp_peT_d, [t_peT], eng="pool")
        w2kp = k.sb("w2kp2", [128, 2, 128], BF16, e6); t_w2 = T()
        w2v = k.sb("w2v2", [128, 64], BF16, e6)
        k.load(w2kp[:], cmp_w2kp_d, [t_w2], eng="pool")
        k.load(w2v[:], cmp_w2v_d, [t_w2], eng="pool")
        selg = k.sb("selg2", [24, 24, 64], F32, e6); t_selg = T()
        sel65 = k.sb("sel652", [65, 128], F32, e6); t_sel65 = T()
        k.load(selg[:], selg_d, [t_selg]); k.load(sel65[:], sel65_d, [t_sel65])
        ovs = k.sb("ovs", [128, 8, 257], BF16, e6); t_ovs = T()
        Ms = k.sb("Ms", [4, 2, 257], F32, e6); t_Ms = T()
        sbias = k.sb("sbias", [128, 7, 16], BF16, e6); t_sbias = T()
        k.load(ovs[:], ovs_d, [t_ovs]); k.load(Ms[:], Ms_d, [t_Ms]); k.load(sbias[:], sbias_d, [t_sbias])
        ptab = k.sb("ptab", [128, 4, 128], I32, e6); t_ptab = T()
        piota = k.sb("piota", [128, 1], F32, e6); t_piota = T()
        idxs = k.sb("idxs", [128, 4, 128], I32, e6); t_idxs = T()
        k.load(ptab[:], ptab_d, [t_ptab]); k.load(piota[:], piota_d, [t_piota])
        k.ts("dve", idxs[:], ptab[:], 128.0, piota[:, 0:1], ALU.mult, ALU.add, [t_ptab, t_piota], [t_idxs])
        pg = [k.sb(f"pg{i}", [128, 256], BF16, e6) for i in range(4)]; t_pg = [T() for _ in range(4)]
        wpg = k.sb("wpg", [128, 4, 256], BF16, e6); t_wpg = T()
        KwT = k.sb("KwT", [128, 640], BF16, e6); t_KwT = T()
        V1ws = k.sb("V1ws", [128, 5, 2, 65], BF16, e6); t_V1ws = T()
        KcTs = k.sb("KcTs", [128, 1024], BF16, e6); t_KcTs = T()
        V1cs = k.sb("V1cs", [128, 8, 2, 65], BF16, e6); t_V1cs = T()
        k.memset("pool", V1cs[:], 1.0, [t_V1cs])
        pebias = k.sb("pebias2", [128, 2], F32, e6); t_peb = T()
        xg = k.sb("xg2", [128, 512], F32, e6); t_xg = T()
        tg = k.sb("tg2", [128, 512], F32, e6); t_tg = T()
        sg = k.sb("sg2", [128, 512], F32, e6); t_sg = T()
        gh = [k.sb(f"ghs{i}", [128, 1024], BF16, e6) for i in range(2)]; t_gh = [T(), T()]
        P8 = [k.sb(f"P8_{i}", [128, 8, 16], BF16, e6) for i in range(2)]; t_P8 = [T(), T()]
        Pn8 = k.sb("Pn8", [128, 8, 16], BF16, e6); t_Pn8 = T()
        osb = [k.sb(f"osbs{i}", [65, 16], F32, e6) for i in range(3)]; t_osb = [T() for _ in range(3)]
        rden = [k.sb(f"rdens{i}", [128, 16], F32, e6) for i in range(3)]; t_rden = [T() for _ in range(3)]
        imp2 = k.sb("imp2s", [4, 257], F32, e6); t_imp2 = T()
        imp3 = k.sb("imp3s", [4, 257], F32, e6); t_imp3 = T()
        mx = k.sb("mxs", [4, 16], F32, e6); t_mx = T()
        thr = k.sb("thrs", [4, 1], F32, e6); t_thr = T()
        nsel = k.sb("nsels", [4, 384], BF16, e6); t_nsel = T()
        k.memset("dve", nsel[:], 0.0, [t_nsel])
        nselT4 = k.sb("nselT4s", [128, 3, 4, 4], BF16, e6); t_nselT = T()
        acc = k.sb("accs", [64, 16], F32, e6); t_acc = T()
        tA = k.sb("tAs", [64, 16], F32, e6); t_tA = T()
        tB = k.sb("tBs", [64, 16], F32, e6); t_tB = T()
        ps_t1 = [k.ps(f"ps_t1{i}", [128, 4, 128], BF16, e6) for i in range(2)]; t_pt1 = [T(), T()]
        ps_t2 = [k.ps(f"ps_t2{i}", [128, 4, 128], BF16, e6) for i in range(2)]; t_pt2 = [T(), T()]
        ps_m = k.ps("ps_ms", [128, 512], F32, e6); t_psm = T()
        ps_s8 = [k.ps(f"ps_s8{i}", [128, 8, 16], F32, e6) for i in range(2)]; t_ps8 = [T(), T()]
        ps_o = k.ps("ps_os", [128, 512], F32, e6); t_pso = T()
        for kv in range(2):
            for r in range(32):
                k.mm(ps_m[:, kv:kv + 1], w1[kv][0:64, r, :], peT[0:64, r, kv:kv + 1], r == 0, r == 31, [t_w1, t_peT], [t_psm])
        k.cp("dve", pebias[:], ps_m[:, 0:2], [t_psm], [t_peb])
        k.memset("pool", big[:, 1, :], 1.0, [t_big[1]])
        cc_ = [0]

        def gather_pages(cache_v, bi, kdst, on_v):
            for lp in range(128):
                pgi = pg[lp % 4]; tpg = t_pg[lp % 4]
                P.dma("pool", lambda q, o=pgi[:], src=cache_v, ix=idxs[:, bi, lp:lp + 1]:
                      q.indirect_dma_start(out=o, out_offset=None, in_=src, in_offset=bass.IndirectOffsetOnAxis(ap=ix, axis=0)),
                      [t_idxs], [tpg])
                grp = (lp // 4) % 2
                k.tr(ps_t1[grp][:, lp % 4, :], pgi[:, 0:128], identb[:], [tpg, t_identb], [t_pt1[grp]])
                on_v(lp, pgi, tpg, grp)
                if lp % 4 == 3:
                    l0 = lp - 3
                    k.cp("dve", kdst[:, l0 * 128:(l0 + 4) * 128], ps_t1[grp][:].rearrange("p a b -> p (a b)"), [t_pt1[grp]], [t_big[0]])

        for bi in range(nsb):
            def v_cmp(lp, pgi, tpg, grp):
                k.tr(ps_t2[grp][:, lp % 4, :], pgi[:, 128:256], identb[:], [tpg, t_identb], [t_pt2[grp]])
                if lp % 4 == 3:
                    l0 = lp - 3
                    k.cp("act", XV[:, l0 * 128:(l0 + 4) * 128], ps_t2[grp][:].rearrange("p a b -> p (a b)"), [t_pt2[grp]], [t_big[1]])
            gather_pages(cache_cmp_v, bi, XK, v_cmp)
            for g in range(2):
                k.memset("dve", gh[g][:, 1023:1024], 0.0, [t_gh[g]])
            for kv in range(2):
                X = XK if kv == 0 else XV
                for g in range(2):
                    gs = slice(g * 64, (g + 1) * 64)
                    for half in range(2):
                        n = 512 if half == 0 else 511
                        b0 = 16 * 512 * half
                        for r in range(32):
                            k.mm(ps_m[:, 0:n], w1[kv][gs, r, :], X[gs, b0 + r:b0 + r + 16 * (n - 1) + 1:16], r == 0, r == 31,
                                 [t_w1, t_big[kv]], [t_psm])
                        k.act(xg[:, 0:n], ps_m[:, 0:n], AF.Identity, [t_psm, t_peb], [t_xg], bias=pebias[:, kv:kv + 1])
                        k.tt("dve", tg[:, 0:n], xg[:, 0:n], xg[:, 0:n], ALU.mult, [t_xg], [t_tg])
                        k.ts("dve", tg[:, 0:n], tg[:, 0:n], 0.044715, 1.0, ALU.mult, ALU.add, [t_tg], [t_tg])
                        k.tt("dve", tg[:, 0:n], tg[:, 0:n], xg[:, 0:n], ALU.mult, [t_tg, t_xg], [t_tg])
                        k.act(sg[:, 0:n], tg[:, 0:n], AF.Sigmoid, [t_tg], [t_sg], scale=1.5957691216)
                        k.tt("dve", gh[g][:, half * 512:half * 512 + n], xg[:, 0:n], sg[:, 0:n], ALU.mult, [t_xg, t_sg], [t_gh[g]])
                    if kv == 1:
                        for ct in range(8):
                            k.mm(ps_m[:, 0:64], gh[g][:, ct * 128:(ct + 1) * 128], w2v[:], True, True, [t_gh[g], t_w2], [t_psm])
                            k.cp("dve", V1cs[:, ct, g, 0:64], ps_m[:, 0:64], [t_psm], [t_V1cs])
                if kv == 0:
                    for half in range(2):
                        for g in range(2):
                            k.mm(ps_m[:], w2kp[:, g, :], gh[g][:, half * 512:(half + 1) * 512], g == 0, g == 1, [t_w2, t_gh[g]], [t_psm])
                        k.cp("dve", KcTs[:, half * 512:(half + 1) * 512], ps_m[:], [t_psm], [t_KcTs])
            def v_slc(lp, pgi, tpg, grp):
                k.cp("pool", V1ss[:, lp, :, 0:64], pgi[:, 128:256].rearrange("p (g d) -> p g d", g=2), [tpg], [t_big[1]])
            k.memset("pool", V1ss[:, 0:128, :, 64:65], 1.0, [t_big[1]])
            gather_pages(cache_slc_v, bi, KsT, v_slc)
            bc = slice(bi * 4, bi * 4 + 4)
            k.memset("dve", KsT[:, 16384:16512], 0.0, [t_big[0]])
            k.cp("dve", KsT[:, 16384:16388], kvnew[:, 0, bc], [t_kvnew], [t_big[0]])
            k.memset("pool", V1ss[:, 128, :, :], 0.0, [t_big[1]])
            k.tr(ps_t2[0][0:4, 0, :], kvnew[:, 1, bc], identb[:], [t_kvnew, t_identb], [t_pt2[0]])
            k.cp("act", V1ss[0:4, 128, :, 0:64], ps_t2[0][0:4, 0, :].rearrange("p (g d) -> p g d", g=2), [t_pt2[0]], [t_big[1]])
            k.memset("pool", V1ss[0:4, 128, :, 64:65], 1.0, [t_big[1]])
            k.load(wpg[:], cache_win_d[bi].rearrange("(t p) f -> p t f", p=128), [t_wpg], eng="pool")
            for t4 in range(4):
                k.tr(ps_t1[0][:, t4, :], wpg[:, t4, 0:128], identb[:], [t_wpg, t_identb], [t_pt1[0]])
            k.memset("dve", KwT[:], 0.0, [t_KwT])
            k.cp("dve", KwT[:, 0:512], ps_t1[0][:].rearrange("p a b -> p (a b)"), [t_pt1[0]], [t_KwT])
            k.cp("dve", KwT[:, 512:516], kvnew[:, 2, bc], [t_kvnew], [t_KwT])
            k.memset("pool", V1ws[:, 0:4, :, 64:65], 1.0, [t_V1ws])
            k.cp("pool", V1ws[:, 0:4, :, 0:64], wpg[:, :, 128:256].rearrange("p t (g d) -> p t g d", g=2), [t_wpg], [t_V1ws])
            k.memset("pool", V1ws[:, 4, :, :], 0.0, [t_V1ws])
            k.tr(ps_t2[1][0:4, 0, :], kvnew[:, 3, bc], identb[:], [t_kvnew, t_identb], [t_pt2[1]])
            k.cp("act", V1ws[0:4, 4, :, 0:64], ps_t2[1][0:4, 0, :].rearrange("p (g d) -> p g d", g=2), [t_pt2[1]], [t_V1ws])
            k.memset("pool", V1ws[0:4, 4, :, 64:65], 1.0, [t_V1ws])

            def branch_done(bidx):
                k.cp("act", osb[bidx][:], ps_o[0:65, 0:16], [t_pso], [t_osb[bidx]])
                k.mm(ps_m[:, 0:16], sel65[:], osb[bidx][:], True, True, [t_sel65, t_osb[bidx]], [t_psm])
                k.ts("dve", rden[bidx][:], ps_m[:, 0:16], 1e-30, None, ALU.max, None, [t_psm], [t_rden[bidx]])
                k.recip(rden[bidx][:], rden[bidx][:], [t_rden[bidx]], [t_rden[bidx]])

            def run_tiles(ntiles, kfn, extra_fn, vfn, rdk, rdv):
                done = 0
                while done < ntiles:
                    n8 = min(8, ntiles - done)
                    pi = cc_[0] % 2; cc_[0] += 1
                    pss, tps = ps_s8[pi], t_ps8[pi]
                    for t8 in range(n8):
                        kt = done + t8
                        ex = extra_fn(kt)
                        k.mm(pss[:, t8, :], kfn(kt), qt, True, len(ex) == 0, rdk + [t_QTs], [tps])
                        for ei, (l_, r_, rds) in enumerate(ex):
                            k.mm(pss[:, t8, :], l_, r_, False, ei == len(ex) - 1, rds, [tps])
                    p8, tp8 = P8[pi], t_P8[pi]
                    k.act(p8[:, 0:n8, :], pss[:, 0:n8, :], AF.Exp, [tps], [tp8], scale=0.125)
                    for t8 in range(n8):
                        kt = done + t8
                        k.mm(ps_o[0:65, 0:16], vfn(kt), p8[:, t8, :], kt == 0, kt == ntiles - 1, rdv + [tp8], [t_pso])
                    done += n8
                return p8, tp8

            for g in range(2):
                gs = slice(g * 64, (g + 1) * 64)
                qt = QTs[gs, bi, :, :].rearrange("p a b -> p (a b)")
                p8, tp8 = run_tiles(8, lambda kt: KcTs[gs, kt * 128:(kt + 1) * 128],
                                    lambda kt: ([(identb[:], sbias[:, 0, :], [t_identb, t_sbias])] if kt == 7 else []),
                                    lambda kt: V1cs[:, kt, g, :], [t_KcTs], [t_V1cs])
                branch_done(0)
                k.tt("dve", Pn8[:], p8[:], rden[0][:].unsqueeze(1).to_broadcast([128, 8, 16]), ALU.mult, [tp8, t_rden[0]], [t_Pn8])
                for ct in range(8):
                    for r in range(4):
                        k.mm(ps_m[0:4, 0:257], Pn8[:, ct, r * 4:(r + 1) * 4], ovs[:, ct, :], ct == 0 and r == 0, ct == 7 and r == 3,
                             [t_Pn8, t_ovs], [t_psm])
                k.tt("dve", imp2[:], ps_m[0:4, 0:257], Ms[:, 0, :], ALU.mult, [t_psm, t_Ms], [t_imp2])
                k.tt("dve", imp2[:], imp2[:], Ms[:, 1, :], ALU.add, [t_imp2, t_Ms], [t_imp2])
                P.op("dve", lambda q, o_=mx[:, 0:8], i_=imp2[:]: q.max(out=o_, in_=i_), [t_imp2], [t_mx])
                P.op("dve", lambda q, o_=imp3[:], m_=mx[:, 0:8], i_=imp2[:]: q.match_replace(out=o_, in_to_replace=m_, in_values=i_, imm_value=-2e30),
                     [t_imp2, t_mx], [t_imp3])
                P.op("dve", lambda q, o_=mx[:, 8:16], i_=imp3[:]: q.max(out=o_, in_=i_), [t_imp3], [t_mx])
                k.ts("dve", thr[:], mx[:, 15:16], -1e29, None, ALU.max, None, [t_mx], [t_thr])
                k.ts("dve", imp3[:], imp2[:], thr[:, 0:1], 1.0, ALU.is_ge, ALU.subtract, [t_imp2, t_thr], [t_imp3])
                k.ts("dve", nsel[:, 0:257], imp3[:], 30000.0, None, ALU.mult, None, [t_imp3], [t_nsel])
                for jt in range(3):
                    k.tr(ps_t1[1][:, jt, 0:4], nsel[:, jt * 128:(jt + 1) * 128], identb[0:4, 0:4], [t_nsel, t_identb], [t_pt1[1]])
                k.cp("dve", nselT4[:], ps_t1[1][:, 0:3, 0:4].unsqueeze(2).to_broadcast([128, 3, 4, 4]), [t_pt1[1]], [t_nselT])
                def ex_slc(kt):
                    ex = [(Em[:, kt % 64, :], nselT4[:, kt // 64, :, :].rearrange("p a b -> p (a b)"), [t_Em, t_nselT])]
                    if kt == 128:
                        ex.append((identb[:], sbias[:, 1, :], [t_identb, t_sbias]))
                    return ex
                run_tiles(129, lambda kt: KsT[gs, kt * 128:(kt + 1) * 128], ex_slc, lambda kt: V1ss[:, kt, g, :], [t_big[0]], [t_big[1]])
                branch_done(1)
                run_tiles(5, lambda kt: KwT[gs, kt * 128:(kt + 1) * 128],
                          lambda kt: [(identb[:], sbias[:, 2 + kt, :], [t_identb, t_sbias])],
                          lambda kt: V1ws[:, kt, g, :], [t_KwT], [t_V1ws])
                branch_done(2)
                for n in range(3):
                    for r in range(4):
                        k.mm(ps_m[0:64, r * 4:(r + 1) * 4], selg[:, (g * 4 + r) * 3 + n, :], gates_s[:, bc], True, True,
                             [t_selg, t_gates_s], [t_psm])
                    k.tt("pool", tA[:], osb[n][0:64, :], rden[n][0:64, :], ALU.mult, [t_osb[n], t_rden[n]], [t_tA])
                    if n == 0:
                        k.tt("dve", acc[:], tA[:], ps_m[0:64, 0:16], ALU.mult, [t_tA, t_psm], [t_acc])
                    else:
                        k.tt("dve", tB[:], tA[:], ps_m[0:64, 0:16], ALU.mult, [t_tA, t_psm], [t_tB])
                        k.tt("pool", acc[:], acc[:], tB[:], ALU.add, [t_acc, t_tB], [t_acc])
                k.cp("dve", attn_s[:, g * 4:(g + 1) * 4, bc], acc[:].rearrange("p (a b) -> p a b", b=4), [t_acc], [t_attn_s])
    P.barrier()
    with ES() as e5:
        wo_a = k.sb("wo_a", [64, 8, 1024], BF16, e5); wo_s = k.sb("wo_s", [128, 4, 1024], BF16, e5); t_wo = T()
        k.load(wo_a[:], w_out_d[0:512, :].rearrange("(h d) n -> d h n", d=64), [t_wo], eng="pool")
        k.load(wo_s[:], w_out_d[512:1024, :].rearrange("(kt p) n -> p kt n", p=128), [t_wo], eng="pool")
        wglu = k.sb("wglu", [128, 4, 512], BF16, e5); t_wglu = T()
        k.load(wglu[:], w_glu_d.rearrange("(kt p) n -> p kt n", p=128), [t_wglu], eng="pool")
        fcols = k.sb("fcols", [128, 16], F32, e5); t_fcols = T()
        k.load(fcols[:], fcols_d, [t_fcols])
        xt2 = k.sb("xt2", [128, 8, 512], F32, e5); t_xt2 = T()
        x1 = k.sb("x1", [128, 8, 512], F32, e5); t_x1 = T()
        at = k.sb("at", [64, 8, 512], F32, e5); t_at = T()
        anb = k.sb("anb", [64, 8, 512], BF16, e5); t_anb = T()
        sqb = k.sb("sqb", [128, 8, 512], BF16, e5); t_sqb = T()
        h2 = k.sb("h2", [128, 8, 512], BF16, e5); t_h2 = T()
        ys = k.sb("ys", [128, 4, 512], F32, e5); t_ys = T()
        gsf = k.sb("gsf", [128, 4, 512], F32, e5); t_gsf = T()
        gsb = k.sb("gsb", [128, 4, 512], BF16, e5); t_gsb = T()
        snb = k.sb("snb", [128, 4, 512], BF16, e5); t_snb = T()
        hm = k.sb("hm", [128, 22, 512], BF16, e5); t_hm = T()
        wg = [k.sb(f"wg{i}", [128, 8, 128], BF16, e5) for i in range(2)]; t_wg = [T(), T()]
        wu = [k.sb(f"wu{i}", [128, 8, 128], BF16, e5) for i in range(2)]; t_wu = [T(), T()]
        wd = [k.sb(f"wd{i}", [128, 22, 128], BF16, e5) for i in range(2)]; t_wd = [T(), T()]
        tmpa = [k.sb(f"tmpa{i}", [128, 512], F32, e5) for i in range(2)]; t_tmpa = [T(), T()]
        rsd = k.sb("rsd", [128, 512], F32, e5); t_rsd = T()
        yo = [k.sb(f"yo{i}", [128, 512], F32, e5) for i in range(2)]; t_yo = [T(), T()]
        ps_a = [k.ps(f"ps_a{i}", [128, 512], F32, e5) for i in range(2)]; t_psa = [T(), T()]
        ps_u = [k.ps(f"ps_u{i}", [128, 512], F32, e5) for i in range(2)]; t_psu = [T(), T()]
        ps_q = k.ps("ps_q", [128, 512], F32, e5); t_psq = T()
        wgv = w_ffn_gate_d.rearrange("(kt p) n -> p kt n", p=128)
        wuv = w_ffn_up_d.rearrange("(kt p) n -> p kt n", p=128)
        wdv = w_ffn_down_d.rearrange("(f p) n -> p f n", p=128)
        yT_v = yT_out.rearrange("(kt p) n -> p kt n", p=128)
        xT_v2 = xT.rearrange("(kt p) n -> p kt n", p=128)
        tgs = x1[:, 0:4, :]; sgs = x1[:, 4:8, :]

        def rms_rstd(sq_tiles, rows, nfeat):
            n = len(sq_tiles)
            for ii, sq_ap in enumerate(sq_tiles):
                k.mm(ps_q[:], onesb[0:rows, :], sq_ap, ii == 0, ii == n - 1, [t_onesb, t_sqb], [t_psq])
            k.act(rsd[:], ps_q[:], AF.Sqrt, [t_psq], [t_rsd], bias=EPS, scale=1.0 / nfeat)
            k.recip(rsd[:], rsd[:], [t_rsd], [t_rsd])

        cnt = 0
        for tt_ in (ftiles if ftiles is not None else range(4)):
            oc0 = tt_ * 512
            c0 = (NW - OWN) + oc0
            k.load(xt2[:], xT_v2[:, :, c0:c0 + 512], [t_xt2])
            k.load(at[:], attn_d[:, :, oc0:oc0 + 512], [t_at], rd=[t_attnd])
            k.load(ys[:], yssm_d[:, :, oc0:oc0 + 512], [t_ys], rd=[t_yssm])
            k.tt("dve", tgs, ys[:], ys[:], ALU.mult, [t_ys], [t_x1])
            k.ts("dve", tgs, tgs, 0.044715, 1.0, ALU.mult, ALU.add, [t_x1], [t_x1])
            k.tt("dve", tgs, tgs, ys[:], ALU.mult, [t_x1, t_ys], [t_x1])
            k.act(sgs, tgs, AF.Sigmoid, [t_x1], [t_x1], scale=1.5957691216)
            k.tt("pool", gsf[:], ys[:], sgs, ALU.mult, [t_ys, t_x1], [t_gsf])
            k.cp("act", gsb[:], gsf[:], [t_gsf], [t_gsb])
            for oc in range(4):
                pa, tpa = ps_a[oc % 2], t_psa[oc % 2]
                for kt in range(4):
                    k.mm(pa[:], wglu[:, kt, oc * 128:(oc + 1) * 128], gsb[:, kt, :], kt == 0, kt == 3, [t_wglu, t_gsb], [tpa])
                ta, tta = tmpa[oc % 2], t_tmpa[oc % 2]
                k.act(ta[:], pa[:], AF.Sigmoid, [tpa, t_fcols], [tta], bias=fcols[:, oc:oc + 1])
                k.tt("pool", ys[:, oc, :], gsf[:, oc, :], ta[:], ALU.mult, [t_gsf, tta], [t_ys])
            k.act(sqb[:, 0:4, :], ys[:], AF.Square, [t_ys], [t_sqb])
            rms_rstd([sqb[:, kt, :] for kt in range(4)], 128, 512)
            for kt in range(4):
                k.stt(snb[:, kt, :], ys[:, kt, :], fcols[:, 4 + kt:5 + kt], rsd[:], ALU.mult, ALU.mult, [t_ys, t_fcols, t_rsd], [t_snb])
            k.act(sqb[0:64, :, :], at[:], AF.Square, [t_at], [t_sqb])
            rms_rstd([sqb[0:64, h, :] for h in range(8)], 64, 512)
            for h in range(8):
                k.stt(anb[:, h, :], at[:, h, :], fcols[0:64, 8 + h:9 + h], rsd[0:64, :], ALU.mult, ALU.mult, [t_at, t_fcols, t_rsd], [t_anb])
            for oc in range(8):
                pa, tpa = ps_a[oc % 2], t_psa[oc % 2]
                for h in range(8):
                    k.mm(pa[:], wo_a[:, h, oc * 128:(oc + 1) * 128], anb[:, h, :], h == 0, False, [t_wo, t_anb], [tpa])
                for kt in range(4):
                    k.mm(pa[:], wo_s[:, kt, oc * 128:(oc + 1) * 128], snb[:, kt, :], False, kt == 3, [t_wo, t_snb], [tpa])
                k.stt(x1[:, oc, :], pa[:], adaT[:, 2 * 8 + oc, 0:1], xt2[:, oc, :], ALU.mult, ALU.add, [tpa, t_ada, t_xt2], [t_x1])
            k.act(sqb[:], x1[:], AF.Square, [t_x1], [t_sqb])
            rms_rstd([sqb[:, kt, :] for kt in range(8)], 128, D)
            for kt in range(8):
                ta, tta = tmpa[kt % 2], t_tmpa[kt % 2]
                k.stt(ta[:], x1[:, kt, :], A2[:, kt, 0:1], rsd[:], ALU.mult, ALU.mult, [t_x1, t_A2, t_rsd], [tta])
                k.act(h2[:, kt, :], ta[:], AF.Identity, [tta, t_ada], [t_h2], bias=adaT[:, 3 * 8 + kt, 0:1])
            for f in range(22):
                wgb, twg = wg[f % 2], t_wg[f % 2]
                wub, twu = wu[f % 2], t_wu[f % 2]
                k.load(wgb[:], wgv[:, :, f * 128:(f + 1) * 128], [twg], eng="pool")
                k.load(wub[:], wuv[:, :, f * 128:(f + 1) * 128], [twu], eng="pool")
                pa, tpa = ps_a[f % 2], t_psa[f % 2]
                pu, tpu = ps_u[f % 2], t_psu[f % 2]
                for kt in range(8):
                    k.mm(pa[:], wgb[:, kt, :], h2[:, kt, :], kt == 0, kt == 7, [twg, t_h2], [tpa])
                for kt in range(8):
                    k.mm(pu[:], wub[:, kt, :], h2[:, kt, :], kt == 0, kt == 7, [twu, t_h2], [tpu])
                ta, tta = tmpa[f % 2], t_tmpa[f % 2]
                k.act(ta[:], pa[:], AF.Silu, [tpa], [tta])
                k.tt("dve", hm[:, f, :], pu[:], ta[:], ALU.mult, [tpu, tta], [t_hm])
            for oc in range(8):
                wdb, twd = wd[oc % 2], t_wd[oc % 2]
                k.load(wdb[:], wdv[:, :, oc * 128:(oc + 1) * 128], [twd], eng="pool")
                pa, tpa = ps_a[oc % 2], t_psa[oc % 2]
                for f in range(22):
                    k.mm(pa[:], wdb[:, f, :], hm[:, f, :], f == 0, f == 21, [twd, t_hm], [tpa])
                yb, tyb = yo[oc % 2], t_yo[oc % 2]
                k.stt(yb[:], pa[:], adaT[:, 5 * 8 + oc, 0:1], x1[:, oc, :], ALU.mult, ALU.add, [tpa, t_ada, t_x1], [tyb])
                k.store(yT_v[:, oc, oc0:oc0 + 512], yb[:], [tyb])

        if do_sample:
            N = 16
            xs2 = k.sb("xs2", [128, 8, N], F32, e5); t_xs2 = T()
            k.load(xs2[:], xsT_d, [t_xs2])
            f1 = k.sb("f1", [128, 8, N], F32, e5); t_f1 = T()
            f2 = k.sb("f2", [128, 8, N], F32, e5); t_f2 = T()
            gss = k.sb("gss", [128, 4, N], F32, e5); t_gss = T()
            gsbs = k.sb("gsbs", [128, 4, N], BF16, e5); t_gsbs = T()
            s2s = k.sb("s2s", [128, 4, N], F32, e5); t_s2s = T()
            sqs2 = k.sb("sqs2", [128, 8, N], BF16, e5); t_sqs2 = T()
            snbs = k.sb("snbs", [128, 4, N], BF16, e5); t_snbs = T()
            anbs = k.sb("anbs", [64, 8, N], BF16, e5); t_anbs = T()
            x1s = k.sb("x1s", [128, 8, N], F32, e5); t_x1s = T()
            h2s = k.sb("h2s", [128, 8, N], BF16, e5); t_h2s = T()
            hms = k.sb("hms", [128, 22, N], BF16, e5); t_hms = T()
            rs_s = k.sb("rs_s", [128, N], F32, e5); t_rs_s = T()
            ysT = k.sb("ysT", [128, 8, N], F32, e5); t_ysT = T()
            tmps = k.sb("tmps", [128, N], F32, e5); t_tmps = T()
            Y4 = yssm_s[:]
            k.tt("dve", f1[:, 0:4, :], Y4, Y4, ALU.mult, [t_yssm_s], [t_f1])
            k.ts("dve", f1[:, 0:4, :], f1[:, 0:4, :], 0.044715, 1.0, ALU.mult, ALU.add, [t_f1], [t_f1])
            k.tt("dve", f1[:, 0:4, :], f1[:, 0:4, :], Y4, ALU.mult, [t_f1, t_yssm_s], [t_f1])
            k.act(f2[:, 0:4, :], f1[:, 0:4, :], AF.Sigmoid, [t_f1], [t_f2], scale=1.5957691216)
            k.tt("dve", gss[:], Y4, f2[:, 0:4, :], ALU.mult, [t_yssm_s, t_f2], [t_gss])
            k.cp("act", gsbs[:], gss[:], [t_gss], [t_gsbs])
            for oc in range(4):
                for kt in range(4):
                    k.mm(ps_q[:, 0:N], wglu[:, kt, oc * 128:(oc + 1) * 128], gsbs[:, kt, :], kt == 0, kt == 3, [t_wglu, t_gsbs], [t_psq])
                k.act(tmps[:], ps_q[:, 0:N], AF.Sigmoid, [t_psq, t_fcols], [t_tmps], bias=fcols[:, oc:oc + 1])
                k.tt("dve", s2s[:, oc, :], gss[:, oc, :], tmps[:], ALU.mult, [t_gss, t_tmps], [t_s2s])

            def rstd_s(sq_list, rows, nfeat):
                n = len(sq_list)
                for ii, sq_ap in enumerate(sq_list):
                    k.mm(ps_q[:, 0:N], onesb[0:rows, :], sq_ap, ii == 0, ii == n - 1, [t_onesb, t_sqs2], [t_psq])
                k.act(rs_s[:], ps_q[:, 0:N], AF.Sqrt, [t_psq], [t_rs_s], bias=EPS, scale=1.0 / nfeat)
                k.recip(rs_s[:], rs_s[:], [t_rs_s], [t_rs_s])

            k.act(sqs2[:, 0:4, :], s2s[:], AF.Square, [t_s2s], [t_sqs2])
            rstd_s([sqs2[:, kt, :] for kt in range(4)], 128, 512)
            for kt in range(4):
                k.stt(snbs[:, kt, :], s2s[:, kt, :], fcols[:, 4 + kt:5 + kt], rs_s[:], ALU.mult, ALU.mult, [t_s2s, t_fcols, t_rs_s], [t_snbs])
            k.act(sqs2[0:64, :, :], attn_s[:], AF.Square, [t_attn_s], [t_sqs2])
            rstd_s([sqs2[0:64, h, :] for h in range(8)], 64, 512)
            for h in range(8):
                k.stt(anbs[:, h, :], attn_s[:, h, :], fcols[0:64, 8 + h:9 + h], rs_s[0:64, :], ALU.mult, ALU.mult,
                      [t_attn_s, t_fcols, t_rs_s], [t_anbs])
            for oc in range(8):
                for h in range(8):
                    k.mm(ps_q[:, 0:N], wo_a[:, h, oc * 128:(oc + 1) * 128], anbs[:, h, :], h == 0, False, [t_wo, t_anbs], [t_psq])
                for kt in range(4):
                    k.mm(ps_q[:, 0:N], wo_s[:, kt, oc * 128:(oc + 1) * 128], snbs[:, kt, :], False, kt == 3, [t_wo, t_snbs], [t_psq])
                k.tt("dve", tmps[:], ps_q[:, 0:N], mods[:, 2, oc, :], ALU.mult, [t_psq, t_mods], [t_tmps])
                k.tt("dve", x1s[:, oc, :], tmps[:], xs2[:, oc, :], ALU.add, [t_tmps, t_xs2], [t_x1s])
            k.act(sqs2[:], x1s[:], AF.Square, [t_x1s], [t_sqs2])
            rstd_s([sqs2[:, kt, :] for kt in range(8)], 128, D)
            k.tt("dve", f1[:], x1s[:], A2s[:], ALU.mult, [t_x1s, t_mods], [t_f1])
            k.tt("dve", f1[:], f1[:], rs_s[:].unsqueeze(1).to_broadcast([128, 8, N]), ALU.mult, [t_f1, t_rs_s], [t_f1])
            k.tt("dve", h2s[:], f1[:], mods[:, 3, :, :], ALU.add, [t_f1, t_mods], [t_h2s])
            for f in range(22):
                wgb, twg = wg[f % 2], t_wg[f % 2]
                wub, twu = wu[f % 2], t_wu[f % 2]
                k.load(wgb[:], wgv[:, :, f * 128:(f + 1) * 128], [twg], eng="pool")
                k.load(wub[:], wuv[:, :, f * 128:(f + 1) * 128], [twu], eng="pool")
                pa, tpa = ps_a[f % 2], t_psa[f % 2]
                pu, tpu = ps_u[f % 2], t_psu[f % 2]
                for kt in range(8):
                    k.mm(pa[:, 0:N], wgb[:, kt, :], h2s[:, kt, :], kt == 0, kt == 7, [twg, t_h2s], [tpa])
                for kt in range(8):
                    k.mm(pu[:, 0:N], wub[:, kt, :], h2s[:, kt, :], kt == 0, kt == 7, [twu, t_h2s], [tpu])
                k.act(tmps[:], pa[:, 0:N], AF.Silu, [tpa], [t_tmps])
                k.tt("dve", hms[:, f, :], pu[:, 0:N], tmps[:], ALU.mult, [tpu, t_tmps], [t_hms])
            for oc in range(8):
                wdb, twd = wd[oc % 2], t_wd[oc % 2]
                k.load(wdb[:], wdv[:, :, oc * 128:(oc + 1) * 128], [twd], eng="pool")
                pa, tpa = ps_a[oc % 2], t_psa[oc % 2]
                for f in range(22):
                    k.mm(pa[:, 0:N], wdb[:, f, :], hms[:, f, :], f == 0, f == 21, [twd, t_hms], [tpa])
                k.tt("dve", tmps[:], pa[:, 0:N], mods[:, 5, oc, :], ALU.mult, [tpa, t_mods], [t_tmps])
                k.tt("dve", ysT[:, oc, :], tmps[:], x1s[:, oc, :], ALU.add, [t_tmps, t_x1s], [t_ysT])
            k.store(ysT_out, ysT[:], [t_ysT])

    P.finalize()
    k.es.close()
    return nc, P


def _rope_tables(pos):
    half = 8
    inv = 500000.0 ** (-(np.arange(half, dtype=np.float64) * 2.0 / 16))
    ang = pos[None, :] * inv[:, None]
    C = np.ones((128, pos.shape[0]), np.float32)
    S = np.zeros((128, pos.shape[0]), np.float32)
    for base in (0, 64):
        C[base:base + 8] = np.cos(ang); C[base + 8:base + 16] = np.cos(ang)
        S[base:base + 8] = np.sin(ang); S[base + 8:base + 16] = np.sin(ang)
    return C, S


def _col(v):
    return np.ascontiguousarray(v.reshape(8, 128).T)


def prepare_inputs(inp):
    x_prompt = inp["x_prompt"]
    w_in = inp["w_in"][0]
    o1, o2, o3 = 512, 512 + 768, 512 + 768 + 24
    qperm = []
    for r in range(4):
        for g in range(2):
            h = g * 4 + r
            qperm += list(range(h * 64, (h + 1) * 64))
    perm = qperm + list(range(o1, o2)) + list(range(o3, NCOL)) + list(range(o2, o3))
    w_in_p = np.ascontiguousarray(w_in[:, perm])
    w_ada = np.ascontiguousarray(inp["w_ada"][0])
    b_adaT = np.ascontiguousarray(inp["b_ada"][0].reshape(48, 128).T)
    gcols = np.stack([_col(inp["norm_mix_g"][0]), _col(inp["norm_ffn_g"][0])], axis=2)
    gq = inp["q_norm_g"][0]
    gk = inp["k_norm_g"][0]
    gqk = np.stack([np.tile(gq, 2)] + [np.tile(gk[n], 2) for n in range(3)], axis=1).astype(np.float32)
    ident = np.eye(128, dtype=np.float32)
    Rt = np.zeros((128, 128), np.float32)
    for base in (0, 64):
        for d in range(8):
            Rt[base + d + 8, base + d] = -1.0
            Rt[base + d, base + d + 8] = 1.0
    def pj(a):
        return np.ascontiguousarray(a.reshape(16, 2, 64).transpose(1, 2, 0).reshape(128, 16))
    a_re = inp["ssm_a_re"][0]; a_im = inp["ssm_a_im"][0]
    logdt = np.repeat(inp["ssm_log_dt"][0][:, None], 64, axis=1)
    ssm_cols = np.stack([pj(a_re), pj(a_im), pj(logdt)], axis=2).astype(np.float32)
    kk1 = np.tile(np.arange(1, 129, dtype=np.float32)[None, :], (128, 1))
    b_re = inp["ssm_b_re"][0]; b_im = inp["ssm_b_im"][0]
    c_re = inp["ssm_c_re"][0]; c_im = inp["ssm_c_im"][0]
    LBre = np.zeros((128, 16, 128), np.float32); LBim = np.zeros_like(LBre)
    LCre = np.zeros_like(LBre); LCim = np.zeros_like(LBre)
    for j in range(16):
        for gs in range(2):
            g = 2 * j + gs
            gl = g % 8
            LBre[gl * 16:(gl + 1) * 16, j, gs * 64:(gs + 1) * 64] = b_re[g].T
            LBim[gl * 16:(gl + 1) * 16, j, gs * 64:(gs + 1) * 64] = b_im[g].T
            LCre[gs * 64:(gs + 1) * 64, j, gl * 16:(gl + 1) * 16] = c_re[g].T
            LCim[gs * 64:(gs + 1) * 64, j, gl * 16:(gl + 1) * 16] = c_im[g].T
    Dcol = np.ascontiguousarray(inp["ssm_d"][0].reshape(4, 128).T)

    bf = ml_dtypes.bfloat16
    NEGB = -30000.0
    cmp_peT = np.zeros((128, 32, 2), np.float32)
    for kv_, nm in enumerate(("cmp_pe_k", "cmp_pe_v")):
        pe = inp[nm][0]
        cmp_peT[0:64, :, kv_] = pe.T
        cmp_peT[64:128, :, kv_] = pe.T
    w2k = inp["cmp_w2_k"][0]
    cmp_w2kp = np.zeros((128, 2, 128), np.float32)
    cmp_w2kp[:, 0, 0:64] = w2k
    cmp_w2kp[:, 1, 64:128] = w2k
    cmp_w2v = np.ascontiguousarray(inp["cmp_w2_v"][0])
    Em = np.zeros((128, 64, 128), np.float32)
    for kt in range(64):
        for half in range(2):
            Em[2 * kt + half, kt, half * 64:(half + 1) * 64] = 1.0
    cs = np.arange(512) * 16
    ss = np.arange(128) * 64
    ovl = np.minimum(cs[:, None] + 32, ss[None, :] + 64) - np.maximum(cs[:, None], ss[None, :])
    ovl = (np.clip(ovl, 0, None) / 32.0).astype(np.float32)
    ovl[511] = 0.0
    ov = np.ascontiguousarray(ovl.reshape(4, 128, 128).transpose(1, 0, 2))
    selg = np.zeros((24, 24, 64), np.float32)
    for i_ in range(24):
        selg[i_, i_, :] = 1.0
    sel65 = np.zeros((65, 128), np.float32); sel65[64] = 1.0
    kk_ = np.arange(128)
    causal = np.where(kk_[:, None] <= kk_[None, :], 0.0, NEGB).astype(np.float32)
    causal4 = np.tile(causal, (1, 4))
    w_out_h = np.ascontiguousarray(inp["w_out"][0]); w_glu_h = np.ascontiguousarray(inp["w_glu"][0])
    wfg = np.ascontiguousarray(inp["w_ffn_gate"][0]); wfu = np.ascontiguousarray(inp["w_ffn_up"][0]); wfd = np.ascontiguousarray(inp["w_ffn_down"][0])
    fcols = np.zeros((128, 16), np.float32)
    fcols[:, 0:4] = inp["b_glu"][0].reshape(4, 128).T
    fcols[:, 4:8] = inp["ssm_out_g"][0].reshape(4, 128).T
    fcols[0:64, 8:16] = inp["attn_out_g"][0].reshape(8, 64).T
    cache_cmp2 = inp["cache_kv_cmp"][0].reshape(NPHYS * 128, 256)
    cache_slc2 = inp["cache_kv_slc"][0].reshape(NPHYS * 128, 256)
    cs2 = np.arange(1024) * 16
    ss2 = np.arange(257) * 64
    ov2 = np.minimum(cs2[:, None] + 32, ss2[None, :] + 64) - np.maximum(cs2[:, None], ss2[None, :])
    ov2 = (np.clip(ov2, 0, None) / 32.0).astype(np.float32)
    ov2[1023] = 0.0
    ovs = np.ascontiguousarray(ov2.reshape(8, 128, 257).transpose(1, 0, 2))
    Ms = np.zeros((4, 2, 257), np.float32)
    Ms[:, 0, :] = 1.0
    for jf in (0, 255, 256):
        Ms[:, 0, jf] = 0.0
        Ms[:, 1, jf] = 1e4
    sbias = np.zeros((128, 7, 16), np.float32)
    sbias[127, 0, :] = NEGB
    qcol = np.tile(np.arange(4), 4)
    for t_ in range(4):
        sbias[t_, 1, :] = np.where(t_ <= qcol, 0.0, NEGB)
        sbias[t_, 6, :] = np.where(t_ <= qcol, 0.0, NEGB)
        sbias[t_, 2, :] = np.where(t_ >= qcol, 0.0, NEGB)
    piota = np.arange(128, dtype=np.float32)[:, None].copy()
    Cs_, Ss_ = _rope_tables(16384.0 + np.arange(4, dtype=np.float64))
    ropes = np.stack([np.tile(Cs_, (1, 4)), np.tile(Ss_, (1, 4))], axis=1).astype(np.float32)
    maps = []
    for c in range(8):
        b, kq = c // 4, c % 4
        start = kq * OWN
        pad = (NW - OWN) - start
        xw = np.zeros((NW, D), np.float32)
        xw[pad:] = x_prompt[b, :start + OWN]
        xT = np.ascontiguousarray(xw.T)
        pos = np.arange(NW, dtype=np.float64) - pad
        C, S = _rope_tables(np.maximum(pos, 0.0))
        tv = np.zeros((128, NW), np.float32); tv[:, pad:] = 1.0
        cT = np.zeros((128, 8, 5), np.float32)
        cT[:, :, 0] = _col(inp["c_prompt"][b])
        for i in range(4):
            cT[:, :, 1 + i] = _col(inp["c_sample"][4 * c + i])
        qi = np.arange(128)
        cmpb = np.zeros((16, 128, 4, 512), np.float32)
        winb = np.zeros((16, 128, 5, 512), np.float32)
        M1 = np.zeros((16, 128, 128), np.float32); M2 = np.zeros((16, 128, 128), np.float32)
        j0 = pad // 64
        jj_ = np.arange(128)
        for i_ in range(16):
            qp = (NW - OWN) + 128 * i_ + qi
            for ct in range(4):
                cc = ct * 128 + np.arange(128)
                valid = (16 * cc[:, None] + 31 <= qp[None, :]) & (16 * cc[:, None] >= pad) & (cc[:, None] <= 510)
                cmpb[i_, :, ct, :] = np.tile(np.where(valid, 0.0, NEGB), (1, 4))
            for w in range(5):
                kp = (44 + i_ + w) * 128 + np.arange(128)
                valid = (kp[:, None] <= qp[None, :]) & (kp[:, None] >= qp[None, :] - 512) & (kp[:, None] >= pad)
                winb[i_, :, w, :] = np.tile(np.where(valid, 0.0, NEGB), (1, 4))
            cur = qp // 64
            fut = (jj_[None, :] > cur[:, None]) | (jj_[None, :] < j0)
            forced = ((jj_[None, :] == j0) | (jj_[None, :] == cur[:, None]) | (jj_[None, :] == cur[:, None] - 1)) & ~fut
            M1[i_] = np.where(fut | forced, 0.0, 1.0)
            M2[i_] = np.where(fut, -1e30, np.where(forced, 1e4, 0.0))
        xsT = np.ascontiguousarray(inp["x_sample"][4 * c:4 * c + 4].reshape(16, 8, 128).transpose(2, 1, 0))
        hs0 = np.zeros((128, 16, 4, 2), np.float32)
        for ri, nm in enumerate(("state_ssm_re", "state_ssm_im")):
            st_ = inp[nm][0, 4 * c:4 * c + 4]
            hs0[:, :, :, ri] = st_.reshape(4, 16, 2, 64).transpose(2, 3, 1, 0).reshape(128, 16, 4)
        ptab = np.ascontiguousarray(np.broadcast_to(inp["page_table"][4 * c:4 * c + 4].astype(np.int32)[None], (128, 4, 128)))
        cwin = np.ascontiguousarray(inp["cache_kv_win"][0, 4 * c:4 * c + 4].reshape(4, 512, 256))
        maps.append(dict(cache_cmp=cache_cmp2, cache_slc=cache_slc2, cache_win=cwin, ovs=ovs.astype(bf), Ms=Ms, sbias=sbias.astype(bf),
                         ptab=ptab, piota=piota, xsT=xsT, ropes=ropes, hs0=hs0, w_out=w_out_h, w_glu=w_glu_h, fcols=fcols, w_ffn_gate=wfg, w_ffn_up=wfu, w_ffn_down=wfd,
                         cmp_w1_k=np.ascontiguousarray(inp["cmp_w1_k"][0]), cmp_w1_v=np.ascontiguousarray(inp["cmp_w1_v"][0]),
                         cmp_peT=cmp_peT, cmp_w2kp=cmp_w2kp, cmp_w2v=cmp_w2v, Em=Em.astype(bf), ov=ov.astype(bf), selg=selg,
                         sel65=sel65, causal4=causal4.astype(bf), cmpb=cmpb.astype(bf), winb=winb.astype(bf), M1=M1, M2=M2,
                         xT=xT, ropeC=C, ropeS=S, tokvalid=tv, w_in=w_in_p, w_ada=w_ada, b_adaT=b_adaT, cT=cT,
                         gcols=gcols.astype(np.float32), gqk=gqk, ident=ident, Rt=Rt, ssm_cols=ssm_cols, kk1=kk1,
                         LBre=LBre, LBim=LBim, LCre=LCre, LCim=LCim, Dcol=Dcol))
    return maps


_CACHE = {}


def run_device(inp, dbg=False):
    if dbg not in _CACHE:
        _CACHE[dbg] = build_program(dbg)
    nc, P = _CACHE[dbg]
    maps = prepare_inputs(inp)
    res = run_bass_kernel_spmd(nc, maps, core_ids=list(range(8)))
    return res.results


def kernel(**inp):
    inp = {k_: np.asarray(v) for k_, v in inp.items()}
    res = run_device(inp)
    B, L = 2, 8192
    y_prompt = np.zeros((B, L, D), np.float32)
    y_sample = np.zeros((32, 4, D), np.float32)
    kv = np.zeros((3, B, L, 2, 2, 64), np.float32)
    kvs = np.zeros((3, 32, 4, 2, 2, 64), np.float32)
    ssm_p = np.zeros((2, 1, B, 32, 64), np.float32)
    ssm_s = np.zeros((2, 1, 32, 32, 64), np.float32)
    for c in range(8):
        b, kq = c // 4, c % 4
        r = res[c]
        y_prompt[b, kq * OWN:(kq + 1) * OWN] = np.asarray(r["yT_out"]).T
        y_sample[4 * c:4 * c + 4] = np.asarray(r["ysT_out"]).transpose(2, 1, 0).reshape(4, 4, D)
        rows = np.asarray(r["kvT_out"]).T.reshape(OWN, 3, 2, 2, 64)
        for n in range(3):
            kv[n, b, kq * OWN:(kq + 1) * OWN] = rows[:, n]
        rows_s = np.asarray(r["kvs_out"]).T.reshape(4, 4, 3, 2, 2, 64)
        for n in range(3):
            kvs[n, 4 * c:4 * c + 4] = rows_s[:, :, n]
        if kq == 3:
            so = np.asarray(r["ssm_out"])
            st = so.reshape(2, 64, 16, 2).transpose(2, 0, 1, 3).reshape(32, 64, 2)
            ssm_p[0, 0, b] = st[:, :, 0]
            ssm_p[1, 0, b] = st[:, :, 1]
        ss = np.asarray(r["ssm_s_out"])
        st = ss.reshape(2, 64, 16, 4, 2).transpose(3, 2, 0, 1, 4).reshape(4, 32, 64, 2)
        ssm_s[0, 0, 4 * c:4 * c + 4] = st[..., 0]
        ssm_s[1, 0, 4 * c:4 * c + 4] = st[..., 1]
    return (y_prompt, y_sample, kv[0][None], kv[1][None], np.ascontiguousarray(kv[2][None][:, :, L - 512:]),
            ssm_p[0], ssm_p[1], kvs[0][None], kvs[1][None], kvs[2][None], ssm_s[0], ssm_s[1])
```
